# Optimizing a Trainium2 kernel written in Bass

```python
import itertools
import jax, jax.numpy as jnp
from jax import lax
import numpy as np

D_MODEL = 2048
BATCH = 2
SEQ = 16384
DEPTH = 2
DEC_BATCH = 8
DEC_SEQ = 4096
PAST_LEN = 128

A_HEADS = 8
Q_LORA = 512
KV_LORA = 512
QK_NOPE = 128
QK_ROPE = 64
V_HEAD = 128
ROPE_THETA = 10000.0
Q_BLOCK = 128
A_WIDTH = A_HEADS * V_HEAD
M_HEADS = 8
M_QK = 128
M_V = 128
CHUNK = 64
M_QK_WIDTH = M_HEADS * M_QK
M_WIDTH = M_HEADS * M_V
N_GATES = 4 * M_HEADS
EPS = 1e-6
SPLIT_SIZES = (Q_LORA, KV_LORA, QK_ROPE, A_WIDTH,
               M_QK_WIDTH, M_QK_WIDTH, M_WIDTH, M_WIDTH, M_WIDTH, N_GATES,
               D_MODEL, D_MODEL)
IN_COLS = Q_LORA + KV_LORA + QK_ROPE + A_WIDTH + 2 * M_QK_WIDTH + 3 * M_WIDTH + N_GATES + 2 * D_MODEL

kernel_name = "hybrid_mla_mlstm_bidir_encoder"


def rms_norm(x, g):
    xf = x.astype(jnp.float32)
    y = xf * lax.rsqrt(jnp.mean(xf * xf, axis=-1, keepdims=True) + EPS)
    return (y * g.astype(jnp.float32)).astype(x.dtype)


def split_cols(t):
    offs = list(itertools.accumulate(SPLIT_SIZES))[:-1]
    return jnp.split(t, offs, axis=-1)


def rope_tables(seq_len):
    inv = ROPE_THETA ** (-jnp.arange(0, QK_ROPE, 2, dtype=jnp.float32) / QK_ROPE)
    ang = jnp.arange(seq_len, dtype=jnp.float32)[:, None] * inv[None, :]
    return jnp.cos(ang), jnp.sin(ang)


def apply_rope(x, cos, sin):
    extra = x.ndim - 3
    c = cos.reshape(cos.shape[0], *([1] * extra), cos.shape[1])
    s = sin.reshape(sin.shape[0], *([1] * extra), sin.shape[1])
    xf = x.astype(jnp.float32)
    x1, x2 = xf[..., :QK_ROPE // 2], xf[..., QK_ROPE // 2:]
    out = jnp.concatenate([x1 * c - x2 * s, x2 * c + x1 * s], axis=-1)
    return out.astype(x.dtype)


def mla_branch(c_q, c_kv, k_rope, q_a_norm, w_uq, kv_a_norm, w_ukv, cos, sin):
    B, S, _ = c_q.shape
    q = (rms_norm(c_q, q_a_norm) @ w_uq).reshape(B, S, A_HEADS, QK_NOPE + QK_ROPE)
    q_nope = q[..., :QK_NOPE]
    q_rope = apply_rope(q[..., QK_NOPE:], cos, sin)
    kv = (rms_norm(c_kv, kv_a_norm) @ w_ukv).reshape(B, S, A_HEADS, QK_NOPE + V_HEAD)
    k_nope, v = kv[..., :QK_NOPE], kv[..., QK_NOPE:]
    k_r = apply_rope(k_rope, cos, sin)
    scale = (QK_NOPE + QK_ROPE) ** -0.5
    nb = S // Q_BLOCK

    def to_blocks(t):
        return t.reshape(B, nb, Q_BLOCK, *t.shape[2:]).swapaxes(0, 1)

    def attend(blk):
        qn, qr = blk
        s = (jnp.einsum('bqhd,bkhd->bhqk', qn, k_nope, preferred_element_type=jnp.float32)
             + jnp.einsum('bqhd,bkd->bhqk', qr, k_r, preferred_element_type=jnp.float32))
        p = jax.nn.softmax(s * scale, axis=-1)
        return jnp.einsum('bhqk,bkhd->bqhd', p.astype(v.dtype), v)

    o = lax.map(attend, (to_blocks(q_nope), to_blocks(q_rope)))
    return o.swapaxes(0, 1).reshape(B, S, A_WIDTH)


def mlstm_chunkwise(q, k, v, i_pre, f_pre):
    B, S, H, DK = q.shape
    DV = v.shape[-1]
    L = CHUNK
    NC = S // L
    f32 = jnp.float32
    q = q.astype(f32).reshape(B, NC, L, H, DK) * (DK ** -0.5)
    k = k.astype(f32).reshape(B, NC, L, H, DK)
    v = v.astype(f32).reshape(B, NC, L, H, DV)
    log_f = jax.nn.log_sigmoid(f_pre.astype(f32)).reshape(B, NC, L, H).transpose(0, 1, 3, 2)
    log_i = i_pre.astype(f32).reshape(B, NC, L, H).transpose(0, 1, 3, 2)
    b = jnp.cumsum(log_f, axis=-1)
    g = b[..., -1]
    a = g[..., None] - b + log_i
    m_loc = a.max(axis=-1)
    w = jnp.exp(a - m_loc[..., None])
    kw = k * w.transpose(0, 1, 3, 2)[..., None]
    C_loc = jnp.einsum('bclhk,bclhv->bchkv', kw, v)
    n_loc = kw.sum(axis=2)

    def step(carry, xs):
        C, n, m = carry
        Cl, nl, ml, gl = xs
        m_new = jnp.maximum(gl + m, ml)
        a_old = jnp.exp(gl + m - m_new)
        a_new = jnp.exp(ml - m_new)
        C_new = a_old[..., None, None] * C + a_new[..., None, None] * Cl
        n_new = a_old[..., None] * n + a_new[..., None] * nl
        return (C_new, n_new, m_new), (C, n, m)

    init = (jnp.zeros((B, H, DK, DV), f32), jnp.zeros((B, H, DK), f32), jnp.zeros((B, H), f32))
    xs = tuple(jnp.moveaxis(t, 1, 0) for t in (C_loc, n_loc, m_loc, g))
    _, (C_prev, n_prev, m_prev) = lax.scan(step, init, xs)
    C_prev = jnp.moveaxis(C_prev, 0, 1)
    n_prev = jnp.moveaxis(n_prev, 0, 1)
    m_prev = jnp.moveaxis(m_prev, 0, 1)

    lower = jnp.tril(jnp.ones((L, L), dtype=bool))
    Dm = jnp.where(lower, b[..., :, None] - b[..., None, :] + log_i[..., None, :], -jnp.inf)
    e = b + m_prev[..., None]
    m_t = jnp.maximum(e, Dm.max(axis=-1))
    P = jnp.exp(Dm - m_t[..., None]) * jnp.einsum('bcjhd,bclhd->bchjl', q, k)
    w_inter = jnp.exp(e - m_t)
    num = (jnp.einsum('bchjl,bclhv->bchjv', P, v)
           + w_inter[..., None] * jnp.einsum('bcjhd,bchdv->bchjv', q, C_prev))
    den = P.sum(axis=-1) + w_inter * jnp.einsum('bcjhd,bchd->bchj', q, n_prev)
    h = num / jnp.maximum(jnp.abs(den), jnp.exp(-m_t))[..., None]
    return h.transpose(0, 1, 3, 2, 4).reshape(B, S, H, DV)


def mlstm_bidir(q, k, v, gates):
    i_fw, i_bw, f_fw, f_bw = jnp.split(gates, 4, axis=-1)
    h_fw = mlstm_chunkwise(q, k, v, i_fw, f_fw)
    flip = lambda t: jnp.flip(t, axis=1)
    h_bw = flip(mlstm_chunkwise(flip(q), flip(k), flip(v), flip(i_bw), flip(f_bw)))
    return h_fw + h_bw


def encoder_layer(x, cos, sin, norm_in, w_in, b_gates, q_a_norm, w_uq, kv_a_norm, w_ukv,
                  w_oa, m_head_norm, w_ob, w_out):
    B, S, _ = x.shape
    h = rms_norm(x, norm_in)
    (c_q, c_kv, k_rope, z_a, q_m, k_m, v_m, o_m, z_m, gates, g_a, g_b) = split_cols(h @ w_in)
    y_a = mla_branch(c_q, c_kv, k_rope, q_a_norm, w_uq, kv_a_norm, w_ukv, cos, sin) * jax.nn.silu(z_a)
    hm = mlstm_bidir(q_m.reshape(B, S, M_HEADS, M_QK), k_m.reshape(B, S, M_HEADS, M_QK),
                     v_m.reshape(B, S, M_HEADS, M_V), gates.astype(jnp.float32) + b_gates.astype(jnp.float32))
    hm = hm * lax.rsqrt(jnp.mean(hm * hm, axis=-1, keepdims=True) + EPS)
    hm = (hm * m_head_norm.astype(jnp.float32).reshape(M_HEADS, M_V)).reshape(B, S, M_WIDTH).astype(x.dtype)
    y_b = hm * jax.nn.sigmoid(o_m) * jax.nn.silu(z_m)
    merged = jax.nn.sigmoid(g_a) * (y_a @ w_oa) + jax.nn.sigmoid(g_b) * (y_b @ w_ob)
    return x + merged @ w_out


def trunk(x, norm_in, w_in, b_gates, q_a_norm, w_uq, kv_a_norm, w_ukv, w_oa, m_head_norm,
          w_ob, w_out, norm_f):
    cos, sin = rope_tables(x.shape[1])
    for l in range(DEPTH):
        x = encoder_layer(x, cos, sin, norm_in[l], w_in[l], b_gates[l], q_a_norm[l], w_uq[l],
                          kv_a_norm[l], w_ukv[l], w_oa[l], m_head_norm[l], w_ob[l], w_out[l])
    return rms_norm(x, norm_f)


def setup_inputs(seed: int = 0) -> dict:
    key = jax.random.key(seed)
    ks = jax.random.split(key, 16)
    f32 = jnp.float32
    nrm = lambda k, shape: jax.random.normal(k, shape, f32)
    gain = lambda k, shape: 1.0 + 0.02 * nrm(k, shape)
    b_i = 0.1 * nrm(ks[2], (DEPTH, 2 * M_HEADS))
    b_f = 3.0 + 0.5 * nrm(ks[3], (DEPTH, 2 * M_HEADS))
    return {
        "x_prompt": nrm(ks[0], (BATCH, SEQ, D_MODEL)),
        "x_sample": nrm(ks[1], (DEC_BATCH, DEC_SEQ, D_MODEL)),
        "norm_in": gain(ks[4], (DEPTH, D_MODEL)),
        "w_in": nrm(ks[5], (DEPTH, D_MODEL, IN_COLS)) * D_MODEL ** -0.5,
        "b_gates": jnp.concatenate([b_i, b_f], axis=-1),
        "q_a_norm": gain(ks[6], (DEPTH, Q_LORA)),
        "w_uq": nrm(ks[7], (DEPTH, Q_LORA, A_HEADS * (QK_NOPE + QK_ROPE))) * Q_LORA ** -0.5,
        "kv_a_norm": gain(ks[8], (DEPTH, KV_LORA)),
        "w_ukv": nrm(ks[9], (DEPTH, KV_LORA, A_HEADS * (QK_NOPE + V_HEAD))) * KV_LORA ** -0.5,
        "w_oa": nrm(ks[10], (DEPTH, A_WIDTH, D_MODEL)) * A_WIDTH ** -0.5,
        "m_head_norm": gain(ks[11], (DEPTH, M_WIDTH)),
        "w_ob": nrm(ks[12], (DEPTH, M_WIDTH, D_MODEL)) * M_WIDTH ** -0.5,
        "w_out": nrm(ks[13], (DEPTH, D_MODEL, D_MODEL)) * D_MODEL ** -0.5,
        "norm_f": gain(ks[14], (D_MODEL,)),
    }


def reference(x_prompt, x_sample, norm_in, w_in, b_gates, q_a_norm, w_uq, kv_a_norm, w_ukv,
              w_oa, m_head_norm, w_ob, w_out, norm_f):
    y_prompt = trunk(x_prompt, norm_in, w_in, b_gates, q_a_norm, w_uq, kv_a_norm, w_ukv,
                     w_oa, m_head_norm, w_ob, w_out, norm_f)
    y_sample = trunk(x_sample, norm_in, w_in, b_gates, q_a_norm, w_uq, kv_a_norm, w_ukv,
                     w_oa, m_head_norm, w_ob, w_out, norm_f)
    return (y_prompt, y_sample)
```

```python
import numpy as np
from contextlib import ExitStack
import ml_dtypes

import concourse.bass as bass
import concourse.mybir as mybir
from concourse.bass_utils import run_bass_kernel_spmd

F32 = mybir.dt.float32
BF16 = mybir.dt.bfloat16
AF = mybir.ActivationFunctionType
ALU = mybir.AluOpType
AX = mybir.AxisListType

D_MODEL = 2048
DEPTH = 2
A_HEADS = 8
Q_LORA = 512
KV_LORA = 512
QK_NOPE = 128
QK_ROPE = 64
V_HEAD = 128
ROPE_THETA = 10000.0
M_HEADS = 8
CHUNK = 64
EPS = 1e-6
IN_COLS = 11360
IN_EXT = IN_COLS + 64
O_CQ, O_CKV, O_KR, O_ZA = 0, 512, 1024, 1088
O_QM, O_KM, O_VM, O_OM, O_ZM, O_GT, O_GA, O_GB = 2112, 3136, 4160, 5184, 6208, 7232, 7264, 9312
O_KRS = IN_COLS
ATT_SCALE = float((QK_NOPE + QK_ROPE) ** -0.5)
MQ_SCALE = float(128 ** -0.5)

COMPUTE = ("pe", "act", "dve", "pool")
ENGS = ("pe", "act", "dve", "pool", "sp")


class Tile:
    __slots__ = ("name", "w", "r", "lsem", "ssem")

    def __init__(self, name):
        self.name = name
        self.w = {}
        self.r = {}
        self.lsem = None
        self.ssem = None


class Op:
    __slots__ = ("eng", "fn", "waits", "sig", "idx", "val", "dma")

    def __init__(self, eng, fn):
        self.eng = eng
        self.fn = fn
        self.waits = []
        self.sig = False
        self.idx = -1
        self.val = -1
        self.dma = None


class Prog:
    def __init__(self, nc):
        self.nc = nc
        self.q = {e: [] for e in ENGS}
        self.esem = {e: nc.alloc_semaphore(f"cnt_{e}") for e in COMPUTE}
        self.bar = nc.alloc_semaphore("bar")
        self.nbar = 0
        self.count = {e: 0 for e in COMPUTE}
        self.nops = {e: 0 for e in ENGS}
        self.waited = {e: {} for e in ENGS}
        self.dma_out = {e: {} for e in ENGS}
        self.tiles = []
        self.free_sems = {}
        self.semq = {}
        self.semcnt = {}
        self.semobj = {}
        self.n_inst = 0

    def tile(self, name):
        t = Tile(name)
        self.tiles.append(t)
        return t

    def _get_sem(self, queue):
        fl = self.free_sems.setdefault(queue, [])
        if fl:
            return fl.pop()
        s = self.nc.alloc_semaphore(f"dsem{len(self.semobj)}")
        self.semobj[id(s)] = s
        self.semcnt[id(s)] = 0
        self.semq[id(s)] = queue
        return s

    @staticmethod
    def _newer(a, b):
        if isinstance(a, Op):
            return a.idx > b.idx
        return a[1] > b[1]

    def _collect(self, reads, writes):
        need = {}
        for t in reads:
            for k, v in t.w.items():
                c = need.get(k)
                if c is None or self._newer(v, c):
                    need[k] = v
        for t in writes:
            for d in (t.w, t.r):
                for k, v in d.items():
                    c = need.get(k)
                    if c is None or self._newer(v, c):
                        need[k] = v
        return need

    def _filter(self, eng, need):
        waits = []
        wd = self.waited[eng]
        for k, v in need.items():
            if isinstance(v, Op):
                if k == eng and eng == "pe":
                    continue
                if wd.get(k, -1) >= v.idx:
                    continue
                wd[k] = v.idx
                v.sig = True
                waits.append(v)
            else:
                if wd.get(k, 0) >= v[1]:
                    continue
                wd[k] = v[1]
                waits.append(v)
        return waits

    def op(self, eng, fn, reads=(), writes=()):
        o = Op(eng, fn)
        o.idx = self.nops[eng]
        self.nops[eng] += 1
        o.waits = self._filter(eng, self._collect(reads, writes))
        for t in reads:
            t.r[eng] = o
        for t in writes:
            t.w = {eng: o}
            t.r = {}
        self.q[eng].append(o)
        return o

    def dma(self, queue, out, in_, reads=(), writes=(), load=True):
        o = Op(queue, lambda e: e.dma_start(out=out, in_=in_))
        o.idx = self.nops[queue]
        self.nops[queue] += 1
        o.waits = self._filter(queue, self._collect(reads, writes))
        if load:
            t = writes[0]
            if t.lsem is None:
                t.lsem = self._get_sem(queue)
            assert self.semq[id(t.lsem)] == queue
            sem = t.lsem
        else:
            t = reads[0]
            if t.ssem is None:
                t.ssem = self._get_sem(queue)
            assert self.semq[id(t.ssem)] == queue
            sem = t.ssem
        self.semcnt[id(sem)] += 16
        dep = (sem, self.semcnt[id(sem)])
        o.dma = dep
        key = ("d", id(sem))
        for t in reads:
            t.r[key] = dep
        for t in writes:
            t.w = {key: dep}
            t.r = {}
        self.dma_out[queue][id(sem)] = dep
        self.q[queue].append(o)
        return o

    def dma_raw(self, queue, out, in_, sem):
        o = Op(queue, lambda e: e.dma_start(out=out, in_=in_))
        o.idx = self.nops[queue]
        self.nops[queue] += 1
        self.semcnt[id(sem)] += 16
        o.dma = (sem, self.semcnt[id(sem)])
        self.dma_out[queue][id(sem)] = o.dma
        self.q[queue].append(o)

    def end_phase(self):
        nc = self.nc
        self.nbar += 1
        for e in COMPUTE:
            ql = self.q[e]
            for o in reversed(ql):
                if o.dma is None:
                    o.sig = True
                    break
        for e in COMPUTE:
            c = self.count[e]
            for o in self.q[e]:
                if o.dma is None and o.sig:
                    c += 1
                    o.val = c
            self.count[e] = c
        names = {"sp": "sync", "act": "scalar", "dve": "vector", "pe": "tensor", "pool": "gpsimd"}
        with nc.Block() as block:
            for e in ENGS:
                deco = getattr(block, names[e])

                def body(eng, e=e):
                    self._emit_engine(e, eng)

                deco(body)
        for e in ENGS:
            self.n_inst += len(self.q[e])
            self.q[e] = []
            self.dma_out[e] = {}
        for t in self.tiles:
            if t.lsem is not None:
                self.free_sems[self.semq[id(t.lsem)]].append(t.lsem)
            if t.ssem is not None:
                self.free_sems[self.semq[id(t.ssem)]].append(t.ssem)
        self.tiles = []

    def _emit_engine(self, e, eng):
        esem = self.esem
        for o in self.q[e]:
            for w in o.waits:
                if isinstance(w, Op):
                    eng.wait_ge(esem[w.eng], w.val)
                else:
                    eng.wait_ge(w[0], w[1])
            inst = o.fn(eng)
            if o.dma is not None:
                inst.then_inc(o.dma[0], 16)
            elif o.sig:
                inst.then_inc(esem[e], 1)
        if e in COMPUTE and self.count[e] > 0:
            eng.wait_ge(esem[e], self.count[e])
        for sem, val in self.dma_out[e].values():
            eng.wait_ge(sem, val)
        eng.sem_inc(self.bar, 1)
        eng.wait_ge(self.bar, len(ENGS) * self.nbar)

    def mm(self, out, lhsT, rhs, start, stop, reads, writes):
        return self.op("pe", lambda e: e.matmul(out, lhsT=lhsT, rhs=rhs, start=start, stop=stop), reads, writes)

    def tr(self, out, in_, ident, reads, writes):
        return self.op("pe", lambda e: e.transpose(out, in_, ident), reads, writes)

    def act(self, out, in_, func, reads, writes, **kw):
        return self.op("act", lambda e: e.activation(out=out, in_=in_, func=func, **kw), reads, writes)

    def tt(self, eng, out, in0, in1, op, reads, writes):
        return self.op(eng, lambda e: e.tensor_tensor(out=out, in0=in0, in1=in1, op=op), reads, writes)

    def ts(self, eng, out, in0, s1, op0, reads, writes, s2=None, op1=None):
        if op1 is None:
            return self.op(eng, lambda e: e.tensor_scalar(out=out, in0=in0, scalar1=s1, scalar2=None, op0=op0),
                           reads, writes)
        return self.op(eng, lambda e: e.tensor_scalar(out=out, in0=in0, scalar1=s1, scalar2=s2, op0=op0, op1=op1),
                       reads, writes)

    def stt(self, eng, out, in0, scalar, in1, op0, op1, reads, writes):
        return self.op(eng, lambda e: e.scalar_tensor_tensor(out=out, in0=in0, scalar=scalar, in1=in1,
                                                             op0=op0, op1=op1), reads, writes)

    def copy(self, eng, out, in_, reads, writes):
        if eng == "act":
            return self.act(out, in_, AF.Copy, reads, writes)
        return self.op(eng, lambda e: e.tensor_copy(out=out, in_=in_), reads, writes)

    def memset(self, eng, ap, val, writes):
        return self.op(eng, lambda e: e.memset(ap, val), (), writes)

    def recip(self, out, in_, reads, writes):
        return self.op("dve", lambda e: e.reciprocal(out=out, in_=in_), reads, writes)


class Buf:
    _uid = [0]

    def __init__(self, P, st, kind, name, shape, dtype, nsub=1):
        nc = P.nc
        Buf._uid[0] += 1
        name = f"{name}_u{Buf._uid[0]}"
        if kind == "sb":
            h = st.enter_context(nc.sbuf_tensor(name, list(shape), dtype))
        else:
            h = st.enter_context(nc.psum_tensor(name, list(shape), dtype))
        self.h = h
        self.ap = h[:]
        self.t = P.tile(name)
        self.sub = [P.tile(f"{name}.{i}") for i in range(nsub)] if nsub > 1 else None


class Seg:
    def __init__(self, name, S):
        self.name = name
        self.S = S
        self.d = {}


def build(seg_sizes, depth=DEPTH, phases=None, dbg=(), prune=None):
    prune = prune or {}
    nc = bass.Bass("TRN2", target_bir_lowering=False)
    P = Prog(nc)
    segs = [Seg(n, S) for n, S in seg_sizes]
    SMAX = max(S for _, S in seg_sizes)

    def din(name, shape, dt=F32):
        return nc.dram_tensor(name, list(shape), dt, kind="ExternalInput").ap()

    def dscr(name, shape, dt=BF16):
        if name in dbg:
            return nc.dram_tensor(name, list(shape), dt, kind="ExternalOutput").ap()
        return nc.dram_tensor(name, list(shape), dt).ap()

    for s in segs:
        s.d["x"] = din(f"x_{s.name}", [s.S, D_MODEL])
        s.own = prune.get(s.name, s.S)
        s.d["y"] = nc.dram_tensor(f"y_{s.name}", [s.own, D_MODEL], F32, kind="ExternalOutput").ap()
        s.d["keep"] = din(f"keep_{s.name}", [128, s.S // CHUNK, 16])
        s.d["rope"] = din(f"rope_{s.name}", [128, 2, s.S])
    w_in_f = din("w_in_ext", [depth, D_MODEL, IN_EXT])
    w_uq_f = din("w_uq_p", [depth, Q_LORA, 2048])
    w_ukv_f = din("w_ukv_p", [depth, KV_LORA, 2048])
    w_oa_f = din("w_oa", [depth, 1024, D_MODEL])
    w_ob_f = din("w_ob", [depth, 1024, D_MODEL])
    w_out_f = din("w_out", [depth, D_MODEL, D_MODEL])
    norm_in_t = din("norm_in_t", [depth, 128, 16])
    qan_t = din("q_a_norm_t", [depth, 128, 4])
    kvan_t = din("kv_a_norm_t", [depth, 128, 4])
    bgates_b = din("b_gates_b", [depth, 128, 32])
    mhn_b = din("m_head_norm_b", [depth, 128, 1024])
    normf_b = din("norm_f_b", [128, D_MODEL])
    consts_f = din("consts", [128, 6 * 128])

    w_in = dscr("w_in_bf", [depth, D_MODEL, IN_EXT])
    w_uq = dscr("w_uq_bf", [depth, Q_LORA, 2048])
    w_ukv = dscr("w_ukv_bf", [depth, KV_LORA, 2048])
    w_oa = dscr("w_oa_bf", [depth, 1024, D_MODEL])
    w_ob = dscr("w_ob_bf", [depth, 1024, D_MODEL])
    w_out = dscr("w_out_bf", [depth, D_MODEL, D_MODEL])

    for s in segs:
        S = s.S
        n = s.name
        d = s.d
        d["x1"] = dscr(f"x1_{n}", [S, D_MODEL], F32)
        d["qT"] = dscr(f"qT_{n}", [A_HEADS, 192, S])
        d["knT"] = dscr(f"knT_{n}", [A_HEADS, 128, S])
        d["krT"] = dscr(f"krT_{n}", [64, S])
        d["Vh"] = dscr(f"Vh_{n}", [A_HEADS, 128, S // 128, 128])
        d["zaT"] = dscr(f"zaT_{n}", [1024, S])
        d["qm"] = dscr(f"qm_{n}", [2, S, 1024])
        d["km"] = dscr(f"km_{n}", [2, S, 1024])
        d["vm"] = dscr(f"vm_{n}", [S, 1024])
        d["EG"] = dscr(f"EG_{n}", [128, S // CHUNK, 16], F32)
        d["ozm"] = dscr(f"ozm_{n}", [S, 1024])
        d["hfw"] = dscr(f"hfw_{n}", [S, 1024], F32)
        d["yaT"] = dscr(f"yaT_{n}", [1024, S])
        d["ybT"] = dscr(f"ybT_{n}", [1024, S])
        d["sgaT"] = dscr(f"sgaT_{n}", [D_MODEL, S])
        d["sgbT"] = dscr(f"sgbT_{n}", [D_MODEL, S])

    top = ExitStack()
    cst = Buf(P, top, "sb", "cst_f", [128, 6 * 128], F32)
    cbf = Buf(P, top, "sb", "cbf", [128, 6, 128], BF16)
    ident = Buf(P, top, "sb", "ident", [128, 128], BF16)
    ones_bf = Buf(P, top, "sb", "ones_bf", [128, 128], BF16)
    mask_bf = Buf(P, top, "sb", "mask_bf", [128, 2, 4, 128], BF16)
    zero_f = Buf(P, top, "sb", "zero_f", [128, 16], F32)
    U_fw = cst.ap[:, 128:256]
    U_bw = cst.ap[:, 256:384]
    ones_f = cst.ap[:, 384:512]

    wsem = nc.alloc_semaphore("wsem")
    P.semcnt[id(wsem)] = 0
    P.dma("sp", cst.ap, consts_f[:, :], writes=[cst.t])
    P.copy("dve", cbf.ap.rearrange("p a b -> p (a b)"), cst.ap, [cst.t], [cbf.t])
    P.copy("dve", ident.ap, cst.ap[:, 0:128], [cst.t], [ident.t])
    P.copy("dve", ones_bf.ap, ones_f, [cst.t], [ones_bf.t])
    for dr in range(2):
        for c in range(4):
            P.copy("dve", mask_bf.ap[:, dr, c, :], cst.ap[:, 128 * (1 + dr):128 * (2 + dr)], [cst.t], [mask_bf.t])
    P.memset("dve", zero_f.ap, 0.0, [zero_f.t])
    def cast_w(dst, src, rows, cols):
        for r0 in range(0, rows, 128):
            for c0 in range(0, cols, 2048):
                c1 = min(cols, c0 + 2048)
                P.dma_raw("pool", dst[r0:r0 + 128, c0:c1], src[r0:r0 + 128, c0:c1], wsem)

    for L in range(depth):
        cast_w(w_in[L], w_in_f[L], D_MODEL, IN_EXT)
        cast_w(w_out[L], w_out_f[L], D_MODEL, D_MODEL)
        cast_w(w_oa[L], w_oa_f[L], 1024, D_MODEL)
        cast_w(w_ob[L], w_ob_f[L], 1024, D_MODEL)
        cast_w(w_uq[L], w_uq_f[L], Q_LORA, 2048)
        cast_w(w_ukv[L], w_ukv_f[L], KV_LORA, 2048)
    P.end_phase()

    for L in range(depth):
        last = (L == depth - 1)
        for s in segs:
            xin = s.d["x"] if L == 0 else s.d["x1"]
            own = s.own if last else s.S
            NT = s.S // 128
            if phases is None or "p1" in phases:
                phase1(P, L, s, xin, w_in, w_uq, w_ukv, norm_in_t, qan_t, kvan_t, bgates_b, s.d["rope"],
                       ident, ones_bf, cbf, own)
            if phases is None or "att" in phases:
                phase_att(P, L, s, ones_bf, own)
            if phases is None or "ml" in phases:
                ot = own // 128
                if own == s.S and s.own == s.S:
                    plan_fw = [(list(range(NT)), True)]
                    plan_bw = [(list(range(NT - 1, -1, -1)), True)]
                elif own == s.S:
                    plan_fw = [(list(range(NT)), False), (list(range(NT)), True)]
                    plan_bw = [(list(range(NT - 1, -1, -1)), False), (list(range(NT - 1, -1, -1)), True)]
                else:
                    plan_fw = [(list(range(ot, NT)), False), (list(range(ot)), True)]
                    plan_bw = [(list(range(NT - 1, ot - 1, -1)), False), (list(range(ot - 1, -1, -1)), True)]
                phase_mlstm(P, L, s, 0, mhn_b, ident, mask_bf, plan_fw)
                phase_mlstm(P, L, s, 1, mhn_b, ident, mask_bf, plan_bw)
            if phases is None or "out" in phases:
                xout = s.d["y"] if last else s.d["x1"]
                phase_out(P, L, s, xin, xout, w_oa, w_ob, w_out, normf_b if last else None, own)
    top.close()
    return nc, P


def phase1(P, L, s, xin, w_in, w_uq, w_ukv, norm_in_t, qan_t, kvan_t, bgates_b, rope_cs,
           ident, ones_bf, cbf, own):
    st = ExitStack()
    S = s.S
    d = s.d
    TT = 1024
    xt = [Buf(P, st, "sb", f"xt{i}", [128, D_MODEL], F32) for i in range(2)]
    junk = Buf(P, st, "sb", "junk", [128, D_MODEL], BF16)
    hb = [Buf(P, st, "sb", f"hb{i}", [128, D_MODEL], BF16) for i in range(2)]
    hT = Buf(P, st, "sb", "hT", [128, 16, TT], BF16, nsub=8)
    wb = [Buf(P, st, "sb", f"wb{i}", [128, 16, 512], BF16) for i in range(3)]
    wuq = Buf(P, st, "sb", "wuq", [128, 4, 2048], BF16)
    wukv = Buf(P, st, "sb", "wukv", [128, 4, 2048], BF16)
    craw1 = Buf(P, st, "sb", "craw", [128, 4, TT], BF16)
    craw = [craw1, craw1]
    cn = [Buf(P, st, "sb", f"cn{i}", [128, 4, TT], BF16, nsub=8) for i in range(2)]
    sq = [Buf(P, st, "sb", f"sq{i}", [128, 512], BF16) for i in range(2)]
    rstdb = Buf(P, st, "sb", "rstdb", [128, 512], F32)
    gain = Buf(P, st, "sb", "gain", [128, 16], F32)
    qg = Buf(P, st, "sb", "qg", [128, 4], F32)
    kvg = Buf(P, st, "sb", "kvg", [128, 4], F32)
    bg = Buf(P, st, "sb", "bg", [128, 32], F32)
    cs = Buf(P, st, "sb", "cs", [128, 2, TT], F32)
    stg = [Buf(P, st, "sb", f"stg{i}", [128, TT], BF16) for i in range(4)]
    f1 = [Buf(P, st, "sb", f"f1_{i}", [128, 512], F32) for i in range(2)]
    f2 = [Buf(P, st, "sb", f"f2_{i}", [128, 512], F32) for i in range(2)]
    sA = [Buf(P, st, "sb", f"sA{i}", [128, 512], BF16) for i in range(2)]
    sB = [Buf(P, st, "sb", f"sB{i}", [128, 512], BF16) for i in range(2)]
    ss = Buf(P, st, "sb", "ss", [128, 8], F32)
    lnv = Buf(P, st, "sb", "lnv", [128, 8], F32)
    rstd = Buf(P, st, "sb", "rstd", [128, 8], F32)
    gt = [Buf(P, st, "sb", f"gt{i}", [128, 32], F32) for i in range(2)]
    lf = [Buf(P, st, "sb", f"lf{i}", [128, 16], F32) for i in range(2)]
    hl = [Buf(P, st, "sb", f"hl{i}", [128, 2, 16], BF16) for i in range(2)]
    ab = [Buf(P, st, "sb", f"ab{i}", [128, 32], F32) for i in range(2)]
    eab = Buf(P, st, "sb", "eab", [128, 8, 32], F32, nsub=8)
    egs = Buf(P, st, "sb", "egs", [128, 16, 16], F32)
    kps = Buf(P, st, "sb", "kps", [128, 16, 16], F32)
    tp = [Buf(P, st, "ps", f"tp{i}", [128, 1024], BF16) for i in range(2)]
    acc = [Buf(P, st, "ps", f"acc{i}", [128, 512], F32) for i in range(4)]
    sm = Buf(P, st, "ps", "sm", [128, 512], F32)
    sm_g = sm_b = sm_G = sm.t

    cnt = {"acc": 0, "stg": 0, "wb": 0, "f": 0, "s": 0}

    def nacc():
        a = acc[cnt["acc"] % 4]
        cnt["acc"] += 1
        return a

    def nstg():
        a = stg[cnt["stg"] % 4]
        cnt["stg"] += 1
        return a

    P.dma("sp", gain.ap, norm_in_t[L], writes=[gain.t])
    P.dma("sp", qg.ap, qan_t[L], writes=[qg.t])
    P.dma("sp", kvg.ap, kvan_t[L], writes=[kvg.t])
    P.dma("sp", bg.ap, bgates_b[L], writes=[bg.t])
    P.dma("sp", wuq.ap, w_uq[L].rearrange("(k p) n -> p k n", p=128), writes=[wuq.t])
    P.dma("sp", wukv.ap, w_ukv[L].rearrange("(k p) n -> p k n", p=128), writes=[wukv.t])

    def load_w(col0, ncols):
        b = wb[cnt["wb"] % 3]
        cnt["wb"] += 1
        P.dma("sp", b.ap[:, :, 0:ncols], w_in[L, :, col0:col0 + ncols].rearrange("(k p) n -> p k n", p=128),
              writes=[b.t])
        return b

    def gemmB(lhs_of_kc, nk, rhs_buf, rhs_tiles, hf, out_ap, out_tile, extra_reads):
        for kc in range(nk):
            P.mm(out_ap, lhs_of_kc(kc), rhs_buf.ap[:, kc, hf * 512:(hf + 1) * 512], kc == 0, kc == nk - 1,
                 list(extra_reads) + list(rhs_tiles), [out_tile])

    import os
    _STOP = int(os.environ.get('P1STOP', '99'))
    _G = int(os.environ.get('GSTOP', '99'))
    for tt in range(S // TT):
        t0 = tt * TT
        full = t0 < own
        P.dma("sp", cs.ap, rope_cs[:, :, t0:t0 + TT], writes=[cs.t])
        P.memset("dve", ss.ap, 0.0, [ss.t])
        for sub in range(8):
            x = xt[sub % 2]
            h_ = hb[sub % 2]
            P.dma("sp", x.ap, xin[t0 + sub * 128:t0 + (sub + 1) * 128, :], writes=[x.t])
            P.act(junk.ap, x.ap, AF.Square, [x.t], [junk.t, ss.t], accum_out=ss.ap[:, sub:sub + 1])
            P.act(lnv.ap[:, sub:sub + 1], ss.ap[:, sub:sub + 1], AF.Ln, [ss.t], [lnv.t], scale=1.0 / D_MODEL, bias=EPS)
            P.act(rstd.ap[:, sub:sub + 1], lnv.ap[:, sub:sub + 1], AF.Exp, [lnv.t], [rstd.t], scale=-0.5)
            P.ts("dve", h_.ap, x.ap, rstd.ap[:, sub:sub + 1], ALU.mult, [x.t, rstd.t], [h_.t])
            for half in range(2):
                tpb = tp[half]
                for k in range(8):
                    kc = half * 8 + k
                    P.tr(tpb.ap[:, k * 128:(k + 1) * 128], h_.ap[:, kc * 128:(kc + 1) * 128], ident.ap,
                         [h_.t, ident.t], [tpb.t])
                P.tt("dve", hT.ap[:, half * 8:half * 8 + 8, sub * 128:(sub + 1) * 128],
                     tpb.ap.rearrange("p (a b) -> p a b", a=8),
                     gain.ap[:, half * 8:half * 8 + 8].unsqueeze(2).broadcast_to([128, 8, 128]),
                     ALU.mult, [tpb.t, gain.t], [hT.sub[sub]])
        hT_all = hT.sub

        if _STOP <= 1:
            continue
        for g, (col0, gn) in enumerate(((O_CQ, qg), (O_CKV, kvg))):
            if g == 0 and not full:
                continue
            wbuf = load_w(col0, 512)
            for hf in range(2):
                for blk in range(4):
                    a = nacc()
                    gemmB(lambda kc, blk=blk: wbuf.ap[:, kc, blk * 128:(blk + 1) * 128], 16, hT,
                          hT_all[hf * 4:hf * 4 + 4], hf, a.ap, a.t, [wbuf.t])
                    P.act(craw[g].ap[:, blk, hf * 512:(hf + 1) * 512], a.ap, AF.Copy, [a.t], [craw[g].t])
                a = nacc()
                for blk in range(4):
                    q_ = sq[blk % 2]
                    P.tt("dve", q_.ap, craw[g].ap[:, blk, hf * 512:(hf + 1) * 512],
                         craw[g].ap[:, blk, hf * 512:(hf + 1) * 512], ALU.mult, [craw[g].t], [q_.t])
                    P.mm(a.ap, ones_bf.ap, q_.ap, blk == 0, blk == 3, [ones_bf.t, q_.t], [a.t])
                P.act(rstdb.ap, a.ap, AF.Ln, [a.t], [rstdb.t], scale=1.0 / 512, bias=EPS)
                P.act(rstdb.ap, rstdb.ap, AF.Exp, [rstdb.t], [rstdb.t], scale=-0.5)
                for blk in range(4):
                    P.stt("dve", cn[g].ap[:, blk, hf * 512:(hf + 1) * 512],
                          craw[g].ap[:, blk, hf * 512:(hf + 1) * 512], gn.ap[:, blk:blk + 1], rstdb.ap,
                          ALU.mult, ALU.mult, [craw[g].t, gn.t, rstdb.t], cn[g].sub[hf * 4:hf * 4 + 4])

        if _STOP <= 2:
            continue
        for h in range(A_HEADS if full else 0):
            o = nstg()
            for hf in range(2):
                a = nacc()
                gemmB(lambda kc, h=h: wuq.ap[:, kc, h * 128:(h + 1) * 128], 4, cn[0], cn[0].sub[hf * 4:hf * 4 + 4],
                      hf, a.ap, a.t, [wuq.t])
                P.act(o.ap[:, hf * 512:(hf + 1) * 512], a.ap, AF.Copy, [a.t], [o.t])
            P.dma("pool", d["qT"][h, 0:128, t0:t0 + TT], o.ap, reads=[o.t], load=False)
        for hp in range(A_HEADS // 2 if full else 0):
            o = nstg()
            for hf in range(2):
                a = nacc()
                b = nacc()
                gemmB(lambda kc, hp=hp: wuq.ap[:, kc, 1024 + hp * 128:1024 + (hp + 1) * 128], 4, cn[0],
                      cn[0].sub[hf * 4:hf * 4 + 4], hf, a.ap, a.t, [wuq.t])
                gemmB(lambda kc, hp=hp: wuq.ap[:, kc, 1536 + hp * 128:1536 + (hp + 1) * 128], 4, cn[0],
                      cn[0].sub[hf * 4:hf * 4 + 4], hf, b.ap, b.t, [wuq.t])
                u = f1[hf]
                v = f2[hf]
                P.tt("dve", u.ap, a.ap, cs.ap[:, 0, hf * 512:(hf + 1) * 512], ALU.mult, [a.t, cs.t], [u.t])
                P.tt("dve", v.ap, b.ap, cs.ap[:, 1, hf * 512:(hf + 1) * 512], ALU.mult, [b.t, cs.t], [v.t])
                P.tt("dve", o.ap[:, hf * 512:(hf + 1) * 512], u.ap, v.ap, ALU.add, [u.t, v.t], [o.t])
            P.dma("pool", d["qT"][2 * hp, 128:192, t0:t0 + TT], o.ap[0:64, :], reads=[o.t], load=False)
            P.dma("pool", d["qT"][2 * hp + 1, 128:192, t0:t0 + TT], o.ap[64:128, :], reads=[o.t], load=False)

        if _STOP <= 3:
            continue
        for h in range(A_HEADS):
            o = nstg()
            for hf in range(2):
                a = nacc()
                gemmB(lambda kc, h=h: wukv.ap[:, kc, h * 128:(h + 1) * 128], 4, cn[1], cn[1].sub[hf * 4:hf * 4 + 4],
                      hf, a.ap, a.t, [wukv.t])
                P.act(o.ap[:, hf * 512:(hf + 1) * 512], a.ap, AF.Copy, [a.t], [o.t])
            P.dma("pool", d["knT"][h, :, t0:t0 + TT], o.ap, reads=[o.t], load=False)
        for sub in range(8):
            o = nstg()
            for j in range(2):
                a = nacc()
                for kc in range(4):
                    P.mm(a.ap, cn[1].ap[:, kc, sub * 128:(sub + 1) * 128],
                         wukv.ap[:, kc, 1024 + j * 512:1024 + (j + 1) * 512], kc == 0, kc == 3,
                         [cn[1].sub[sub], wukv.t], [a.t])
                P.act(o.ap[:, j * 512:(j + 1) * 512], a.ap, AF.Copy, [a.t], [o.t])
            blk = (t0 + sub * 128) // 128
            P.dma("pool", d["Vh"][:, :, blk, :].rearrange("h p d -> p h d"),
                  o.ap.rearrange("p (h d) -> p h d", h=8), reads=[o.t], load=False)

        if _STOP <= 4:
            continue
        wk = load_w(O_KR, 64)
        wks = load_w(O_KRS, 64)
        o = nstg()
        for hf in range(2):
            a = nacc()
            b = nacc()
            gemmB(lambda kc: wk.ap[:, kc, 0:64], 16, hT, hT_all[hf * 4:hf * 4 + 4], hf, a.ap[0:64, :], a.t, [wk.t])
            gemmB(lambda kc: wks.ap[:, kc, 0:64], 16, hT, hT_all[hf * 4:hf * 4 + 4], hf, b.ap[0:64, :], b.t, [wks.t])
            u = f1[hf]
            v = f2[hf]
            P.tt("dve", u.ap[0:64, :], a.ap[0:64, :], cs.ap[0:64, 0, hf * 512:(hf + 1) * 512], ALU.mult,
                 [a.t, cs.t], [u.t])
            P.tt("dve", v.ap[0:64, :], b.ap[0:64, :], cs.ap[0:64, 1, hf * 512:(hf + 1) * 512], ALU.mult,
                 [b.t, cs.t], [v.t])
            P.tt("dve", o.ap[0:64, hf * 512:(hf + 1) * 512], u.ap[0:64, :], v.ap[0:64, :], ALU.add,
                 [u.t, v.t], [o.t])
        P.dma("pool", d["krT"][:, t0:t0 + TT], o.ap[0:64, :], reads=[o.t], load=False)

        if _STOP <= 5:
            continue
        for (col0, nblk, func, dst) in ((O_ZA, 8, AF.Silu, d["zaT"]), (O_GA, 16, AF.Sigmoid, d["sgaT"]),
                                        (O_GB, 16, AF.Sigmoid, d["sgbT"])):
            if not full:
                continue
            for b4 in range(nblk // 4):
                wbuf = load_w(col0 + b4 * 512, 512)
                for blk in range(4):
                    o = nstg()
                    for hf in range(2):
                        a = nacc()
                        gemmB(lambda kc, blk=blk: wbuf.ap[:, kc, blk * 128:(blk + 1) * 128], 16, hT,
                              hT_all[hf * 4:hf * 4 + 4], hf, a.ap, a.t, [wbuf.t])
                        P.act(o.ap[:, hf * 512:(hf + 1) * 512], a.ap, func, [a.t], [o.t])
                    r0 = (b4 * 4 + blk) * 128
                    P.dma("pool", dst[r0:r0 + 128, t0:t0 + TT], o.ap, reads=[o.t], load=False)

        if _STOP <= 6:
            continue
        wg = load_w(O_GT, 32)
        for sub in range(8):
            i2 = sub % 2
            for kc in range(16):
                P.mm(sm.ap[:, 0:32], hT.ap[:, kc, sub * 128:(sub + 1) * 128], wg.ap[:, kc, 0:32], kc == 0, kc == 15,
                     [hT.sub[sub], wg.t], [sm_g])
            g_ = gt[i2]
            P.tt("dve", g_.ap, sm.ap[:, 0:32], bg.ap, ALU.add, [sm_g, bg.t], [g_.t])
            if _G <= 1:
                continue
            l_ = lf[i2]
            P.act(l_.ap, g_.ap[:, 16:32], AF.Exp, [g_.t], [l_.t], scale=-1.0)
            P.act(l_.ap, l_.ap, AF.Ln, [l_.t], [l_.t], bias=1.0)
            P.ts("dve", l_.ap, l_.ap, -1.0, ALU.mult, [l_.t], [l_.t])
            if _G <= 2:
                continue
            hl_ = hl[i2]
            P.copy("dve", hl_.ap[:, 0, :], l_.ap, [l_.t], [hl_.t])
            P.tt("dve", hl_.ap[:, 1, :], l_.ap, hl_.ap[:, 0, :], ALU.subtract, [l_.t, hl_.t], [hl_.t])
            if _G <= 3:
                continue
            for part in range(2):
                P.mm(sm.ap[:, 64:72], cbf.ap[:, 1, :], hl_.ap[:, part, 0:8], part == 0, part == 1, [cbf.t, hl_.t], [sm_b])
            for part in range(2):
                P.mm(sm.ap[:, 72:80], cbf.ap[:, 2, :], hl_.ap[:, part, 8:16], part == 0, part == 1, [cbf.t, hl_.t], [sm_b])
            if _G <= 4:
                continue
            for c in range(2):
                for part in range(2):
                    P.mm(sm.ap[:, 128 + 16 * c:144 + 16 * c], cbf.ap[:, 4 + c, :], hl_.ap[:, part, :], part == 0,
                         part == 1, [cbf.t, hl_.t], [sm_G])
            if _G <= 5:
                continue
            a_ = ab[i2]
            _G2 = int(os.environ.get('G2', '99'))
            P.copy("dve", a_.ap[:, 0:16], sm.ap[:, 64:80], [sm_b], [a_.t])
            if _G2 <= 1:
                continue
            P.tt("dve", a_.ap[:, 16:32], g_.ap[:, 0:16], a_.ap[:, 0:16], ALU.subtract, [g_.t, a_.t], [a_.t])
            if _G2 <= 2:
                continue
            P.act(eab.ap[:, sub, :], a_.ap, AF.Exp, [a_.t], [eab.sub[sub]])
            if _G2 <= 3:
                continue
            P.ts("dve", eab.ap[:, sub, 0:16], eab.ap[:, sub, 0:16], MQ_SCALE, ALU.mult, [eab.sub[sub]], [eab.sub[sub]])
            if _G <= 6:
                continue
            P.act(egs.ap[:, 2 * sub:2 * sub + 2, :], sm.ap[:, 128:160].rearrange("p (c n) -> p c n", c=2), AF.Exp,
                  [sm_G], [egs.t])
        nck0 = t0 // CHUNK
        if _G > 6:
            P.dma("sp", kps.ap, d["keep"][:, nck0:nck0 + 16, :], writes=[kps.t])
            P.tt("dve", egs.ap, egs.ap, kps.ap, ALU.mult, [egs.t, kps.t], [egs.t])
            P.dma("pool", d["EG"][:, nck0:nck0 + 16, :], egs.ap, reads=[egs.t], load=False)

        if _STOP <= 7:
            continue
        for (col0, which, dst) in ((O_QM, 0, d["qm"]), (O_KM, 1, d["km"])):
            if which == 0 and not full:
                continue
            for j in range(2):
                wq = load_w(col0 + j * 512, 512)
                for sub in range(8):
                    a = nacc()
                    for kc in range(16):
                        P.mm(a.ap, hT.ap[:, kc, sub * 128:(sub + 1) * 128], wq.ap[:, kc, :], kc == 0, kc == 15,
                             [hT.sub[sub], wq.t], [a.t])
                    for dr in range(2):
                        o = nstg()
                        c0 = which * 16 + dr * 8 + j * 4
                        P.tt("dve", o.ap[:, 0:512].rearrange("p (h d) -> p h d", h=4),
                             a.ap.rearrange("p (h d) -> p h d", h=4),
                             eab.ap[:, sub, c0:c0 + 4].unsqueeze(2).broadcast_to([128, 4, 128]), ALU.mult,
                             [a.t, eab.sub[sub]], [o.t])
                        P.dma("pool", dst[dr, t0 + sub * 128:t0 + (sub + 1) * 128, j * 512:(j + 1) * 512],
                              o.ap[:, 0:512], reads=[o.t], load=False)
        for j in range(2):
            wv = load_w(O_VM + j * 512, 512)
            for sub in range(8):
                a = nacc()
                for kc in range(16):
                    P.mm(a.ap, hT.ap[:, kc, sub * 128:(sub + 1) * 128], wv.ap[:, kc, :], kc == 0, kc == 15,
                         [hT.sub[sub], wv.t], [a.t])
                o = nstg()
                P.act(o.ap[:, 0:512], a.ap, AF.Copy, [a.t], [o.t])
                P.dma("pool", d["vm"][t0 + sub * 128:t0 + (sub + 1) * 128, j * 512:(j + 1) * 512], o.ap[:, 0:512],
                      reads=[o.t], load=False)
        for j in range(2 if full else 0):
            wo = load_w(O_OM + j * 512, 512)
            wz = load_w(O_ZM + j * 512, 512)
            for sub in range(8):
                a = nacc()
                b = nacc()
                for kc in range(16):
                    P.mm(a.ap, hT.ap[:, kc, sub * 128:(sub + 1) * 128], wo.ap[:, kc, :], kc == 0, kc == 15,
                         [hT.sub[sub], wo.t], [a.t])
                for kc in range(16):
                    P.mm(b.ap, hT.ap[:, kc, sub * 128:(sub + 1) * 128], wz.ap[:, kc, :], kc == 0, kc == 15,
                         [hT.sub[sub], wz.t], [b.t])
                u = sA[sub % 2]
                v = sB[sub % 2]
                P.act(u.ap, a.ap, AF.Sigmoid, [a.t], [u.t])
                P.act(v.ap, b.ap, AF.Silu, [b.t], [v.t])
                o = nstg()
                P.tt("dve", o.ap[:, 0:512], u.ap, v.ap, ALU.mult, [u.t, v.t], [o.t])
                P.dma("pool", d["ozm"][t0 + sub * 128:t0 + (sub + 1) * 128, j * 512:(j + 1) * 512], o.ap[:, 0:512],
                      reads=[o.t], load=False)
    st.close()
    P.end_phase()


def phase_att(P, L, s, ones_bf, own):
    st = ExitStack()
    S = s.S
    d = s.d
    NCH = 4
    KC = S // NCH
    nb = KC // 128
    NKB = S // 128
    NQ = own // 512
    kn = [Buf(P, st, "sb", f"kn{c}", [128, KC], BF16) for c in range(NCH)]
    vv = [Buf(P, st, "sb", f"vv{c}", [128, nb, 128], BF16) for c in range(NCH)]
    kr = Buf(P, st, "sb", "kr", [128, S], BF16)
    qn = [Buf(P, st, "sb", f"qn{i}", [128, 512], BF16) for i in range(2)]
    qr = [Buf(P, st, "sb", f"qr{i}", [128, 512], BF16) for i in range(2)]
    za = [Buf(P, st, "sb", f"za{i}", [128, 512], BF16) for i in range(2)]
    pt = [Buf(P, st, "sb", f"pt{i}", [128, 512], BF16) for i in range(8)]
    p2 = [Buf(P, st, "sb", f"p2_{i}", [128, 512], BF16) for i in range(3)]
    s2 = [Buf(P, st, "sb", f"s2_{i}", [128, 512], F32) for i in range(2)]
    s4 = [Buf(P, st, "sb", f"s4_{i}", [128, 512], F32) for i in range(2)]
    rl = Buf(P, st, "sb", "rl", [128, 512], F32)
    yo = Buf(P, st, "sb", "yo", [128, 512], F32)
    yb = [Buf(P, st, "sb", f"yb{i}", [128, 512], BF16) for i in range(2)]
    sps = [Buf(P, st, "ps", f"sps{i}", [128, 512], F32) for i in range(3)]
    ops = [Buf(P, st, "ps", f"ops{i}", [128, 512], F32) for i in range(2)]
    lps = [Buf(P, st, "ps", f"lps{i}", [128, 512], F32) for i in range(2)]

    P.memset("dve", kr.ap[64:128, :], 0.0, [kr.t])
    for i in range(2):
        P.memset("dve", qr[i].ap[64:128, :], 0.0, [qr[i].t])
    P.dma("sp", kr.ap[0:64, :], d["krT"][:, :], writes=[kr.t])

    def load_kv(h, c):
        P.dma("sp", kn[c].ap, d["knT"][h, :, c * KC:(c + 1) * KC], writes=[kn[c].t])
        P.dma("sp", vv[c].ap, d["Vh"][h, :, c * nb:(c + 1) * nb, :], writes=[vv[c].t])

    for c in range(NCH):
        load_kv(0, c)

    def load_q(h, qt, i):
        q0 = qt * 512
        P.dma("sp", qn[i].ap, d["qT"][h, 0:128, q0:q0 + 512], writes=[qn[i].t])
        P.dma("sp", qr[i].ap[0:64, :], d["qT"][h, 128:192, q0:q0 + 512], writes=[qr[i].t])
        P.dma("sp", za[i].ap, d["zaT"][h * 128:(h + 1) * 128, q0:q0 + 512], writes=[za[i].t])

    items = [(h, qt, kb) for h in range(A_HEADS) for qt in range(NQ) for kb in range(NKB)]
    LAG = 2
    LAG2 = 2
    NPT = 8
    n = len(items)
    load_q(0, 0, 0)
    for i in range(n + LAG + LAG2):
        if i < n:
            h, qt, kb = items[i]
            qi = (h * NQ + qt)
            if kb == 5:
                nxt = qi + 1
                if nxt < A_HEADS * NQ:
                    load_q(nxt // NQ, nxt % NQ, nxt % 2)
            c, kk = kb // nb, kb % nb
            sp_ = sps[i % 3]
            P.mm(sp_.ap, kn[c].ap[:, kk * 128:(kk + 1) * 128], qn[qi % 2].ap, True, False,
                 [kn[c].t, qn[qi % 2].t], [sp_.t])
            P.mm(sp_.ap, kr.ap[:, kb * 128:(kb + 1) * 128], qr[qi % 2].ap, False, True,
                 [kr.t, qr[qi % 2].t], [sp_.t])
            P.act(pt[i % NPT].ap, sp_.ap, AF.Exp, [sp_.t], [pt[i % NPT].t], scale=ATT_SCALE)
        j = i - LAG
        if 0 <= j < n:
            h, qt, kb = items[j]
            qi = (h * NQ + qt)
            c, kk = kb // nb, kb % nb
            o_ps = ops[qi % 2]
            p_ = pt[j % NPT]
            P.mm(o_ps.ap, vv[c].ap[:, kk, :], p_.ap, kb == 0, kb == NKB - 1, [vv[c].t, p_.t], [o_ps.t])
            if kb % 2 == 1:
                pm = pt[(j - 1) % NPT]
                ps_ = s2[(kb // 2) % 2]
                P.tt("dve", ps_.ap, pm.ap, p_.ap, ALU.add, [pm.t, p_.t], [ps_.t])
                if kb % 4 == 3:
                    s4_ = s4[(kb // 4) % 2]
                    P.tt("dve", s4_.ap, s2[0].ap, s2[1].ap, ALU.add, [s2[0].t, s2[1].t], [s4_.t])
                    if kb % 8 == 7:
                        pp = p2[(j // 8) % 3]
                        P.tt("dve", pp.ap, s4[0].ap, s4[1].ap, ALU.add, [s4[0].t, s4[1].t], [pp.t])
            if qt == NQ - 1 and h < A_HEADS - 1 and kk == nb - 1:
                load_kv(h + 1, c)
        j = i - LAG - LAG2
        if 0 <= j < n:
            h, qt, kb = items[j]
            qi = (h * NQ + qt)
            if kb % 8 == 7:
                l_ps = lps[qi % 2]
                pp = p2[(j // 8) % 3]
                P.mm(l_ps.ap, ones_bf.ap, pp.ap, kb == 7, kb == NKB - 1, [ones_bf.t, pp.t], [l_ps.t])
            if kb == NKB - 1:
                o_ps = ops[qi % 2]
                l_ps = lps[qi % 2]
                q0 = qt * 512
                P.recip(rl.ap, l_ps.ap, [l_ps.t], [rl.t])
                P.tt("dve", yo.ap, o_ps.ap, rl.ap, ALU.mult, [o_ps.t, rl.t], [yo.t])
                y_ = yb[qi % 2]
                P.tt("dve", y_.ap, yo.ap, za[qi % 2].ap, ALU.mult, [yo.t, za[qi % 2].t], [y_.t])
                P.dma("pool", d["yaT"][h * 128:(h + 1) * 128, q0:q0 + 512], y_.ap, reads=[y_.t], load=False)
    st.close()
    P.end_phase()


def phase_mlstm(P, L, s, dr, mhn_b, ident, mask_bf, plan):
    st = ExitStack()
    S = s.S
    d = s.d
    NCK = S // CHUNK
    NB = 3
    qtm = [Buf(P, st, "sb", f"qtm{i}", [128, 1024], BF16) for i in range(NB)]
    ktm = [Buf(P, st, "sb", f"ktm{i}", [128, 1024], BF16) for i in range(NB)]
    vau = [Buf(P, st, "sb", f"vau{i}", [128, 8, 129], BF16) for i in range(NB)]
    egt = [Buf(P, st, "sb", f"egt{i}", [128, 3, 16], F32) for i in range(NB)]
    qT_ = [Buf(P, st, "sb", f"qT{i}", [128, 8, 128], BF16) for i in range(2)]
    kT_ = [Buf(P, st, "sb", f"kT{i}", [128, 8, 128], BF16) for i in range(2)]
    pT = [Buf(P, st, "sb", f"pT{i}", [128, 4, 128], BF16) for i in range(2)]
    Tst = Buf(P, st, "sb", "Tst", [128, 8, 129], F32, nsub=8)
    Cb = [Buf(P, st, "sb", f"Cb{i}", [128, 8, 129], BF16, nsub=8) for i in range(3)]
    hout = [Buf(P, st, "sb", f"hout{i}", [128, 1024], F32) for i in range(2)]
    dn = Buf(P, st, "sb", "dn", [128, 8], F32)
    nd = Buf(P, st, "sb", "nd", [128, 8], F32)
    rd = Buf(P, st, "sb", "rd", [128, 8], F32)
    sT = [Buf(P, st, "ps", f"sT{i}", [128, 512], F32) for i in range(2)]
    num = [Buf(P, st, "ps", f"num{i}", [128, 3, 129], F32) for i in range(3)]
    dCb = [Buf(P, st, "ps", f"dC{i}", [128, 3, 129], F32) for i in range(2)]
    tpm = Buf(P, st, "ps", "tpm", [128, 1024], BF16)
    if dr == 1:
        hfw = [Buf(P, st, "sb", f"hfw{i}", [128, 1024], F32) for i in range(NB)]
        ozm = [Buf(P, st, "sb", f"ozm{i}", [128, 1024], BF16) for i in range(NB)]
        mhn = Buf(P, st, "sb", "mhn", [128, 1024], F32)
        hsq = Buf(P, st, "sb", "hsq", [128, 1024], F32)
        ssm = Buf(P, st, "sb", "ssm", [128, 8], F32)
        rsm = Buf(P, st, "sb", "rsm", [128, 8], F32)
        ybt = Buf(P, st, "sb", "ybt", [128, 1024], BF16)
        ybst = [Buf(P, st, "sb", f"ybst{i}", [128, 8, 512], BF16) for i in range(2)]
        P.dma("sp", mhn.ap, mhn_b[L], writes=[mhn.t])

    rows = [(0, 64), (64, 128)] if dr == 0 else [(64, 128), (0, 64)]
    eidx = [(1, 0), (2, 1)] if dr == 0 else [(1, 2), (0, 1)]

    for i in range(NB):
        P.memset("dve", vau[i].ap[:, :, 128:129], 1.0, [vau[i].t])
    for i in range(3):
        P.memset("dve", Cb[i].ap, 0.0, Cb[i].sub)
    P.memset("dve", Tst.ap, 0.0, Tst.sub)
    cbi = [0]

    def load(ti, b, real):
        r0 = ti * 128
        if real:
            P.dma("sp", qtm[b].ap, d["qm"][dr, r0:r0 + 128, :], writes=[qtm[b].t])
        P.dma("sp", ktm[b].ap, d["km"][dr, r0:r0 + 128, :], writes=[ktm[b].t])
        P.dma("sp", vau[b].ap[:, :, 0:128], d["vm"][r0:r0 + 128, :].rearrange("p (h d) -> p h d", h=8),
              writes=[vau[b].t])
        if dr == 0:
            if ti > 0:
                P.dma("sp", egt[b].ap, d["EG"][:, 2 * ti - 1:2 * ti + 2, :], writes=[egt[b].t])
            else:
                P.dma("sp", egt[b].ap[:, 1:3, :], d["EG"][:, 0:2, :], writes=[egt[b].t])
                P.dma("sp", egt[b].ap[:, 0, :], d["EG"][:, NCK - 1, :], writes=[egt[b].t])
        else:
            if 2 * ti + 3 <= NCK:
                P.dma("sp", egt[b].ap, d["EG"][:, 2 * ti:2 * ti + 3, :], writes=[egt[b].t])
            else:
                P.dma("sp", egt[b].ap[:, 0:2, :], d["EG"][:, 2 * ti:2 * ti + 2, :], writes=[egt[b].t])
                P.dma("sp", egt[b].ap[:, 2, :], d["EG"][:, 0, :], writes=[egt[b].t])
        if real and dr == 1:
            P.dma("sp", hfw[b].ap, d["hfw"][r0:r0 + 128, :], writes=[hfw[b].t])
            P.dma("sp", ozm[b].ap, d["ozm"][r0:r0 + 128, :], writes=[ozm[b].t])

    dcn = [0]

    def chunk_updates(b, ci, need_cb=True):
        lo, hi = rows[ci]
        own, prev = eidx[ci]
        for grp in ((0, 1, 2), (3, 4, 5), (6, 7)):
            dc = dCb[dcn[0] % 2]
            dcn[0] += 1
            for k, h in enumerate(grp):
                P.mm(dc.ap[:, k, :], ktm[b].ap[lo:hi, h * 128:(h + 1) * 128], vau[b].ap[lo:hi, h, :], True, True,
                     [ktm[b].t, vau[b].t], [dc.t])
            for k, h in enumerate(grp):
                col = dr * 8 + h
                P.stt("dve", Tst.ap[:, h, :], Tst.ap[:, h, :], egt[b].ap[:, prev, col:col + 1], dc.ap[:, k, :],
                      ALU.mult, ALU.add, [Tst.sub[h], egt[b].t, dc.t], [Tst.sub[h]])
            if not need_cb:
                continue
            nxt = Cb[(cbi[0] + ci + 1) % 3]
            for k, h in enumerate(grp):
                col = dr * 8 + h
                P.act(nxt.ap[:, h, :], Tst.ap[:, h, :], AF.Copy, [Tst.sub[h], egt[b].t], [nxt.sub[h]],
                      scale=egt[b].ap[:, own, col:col + 1])

    seq = [(ti, real) for (tiles, real) in plan for ti in tiles]
    nseq = len(seq)
    for k_ in range(min(NB - 1, nseq)):
        load(seq[k_][0], k_ % NB, seq[k_][1])
    for n_, (ti, real) in enumerate(seq):
        b = n_ % NB
        b2 = n_ % 2
        if n_ + NB - 1 < nseq:
            load(seq[n_ + NB - 1][0], (n_ + NB - 1) % NB, seq[n_ + NB - 1][1])
        if not real:
            nxt_real = (n_ + 1 < nseq) and seq[n_ + 1][1]
            for ci in range(2):
                chunk_updates(b, ci, need_cb=(nxt_real and ci == 1))
            cbi[0] = (cbi[0] + 2) % 3
            continue
        for (src, dst) in ((qtm[b], qT_[b2]), (ktm[b], kT_[b2])):
            for h in range(8):
                P.tr(tpm.ap[:, h * 128:(h + 1) * 128], src.ap[:, h * 128:(h + 1) * 128], ident.ap,
                     [src.t, ident.t], [tpm.t])
            P.copy("act", dst.ap.rearrange("p h t -> p (h t)"), tpm.ap, [tpm.t], [dst.t])
        chunk_updates(b, 0)
        chunk_updates(b, 1)
        for g in range(2):
            for hh in range(4):
                h = g * 4 + hh
                P.mm(sT[g].ap[:, hh * 128:(hh + 1) * 128], kT_[b2].ap[:, h, :], qT_[b2].ap[:, h, :], True, True,
                     [kT_[b2].t, qT_[b2].t], [sT[g].t])
            P.tt("dve", pT[g].ap.rearrange("p h t -> p (h t)"), sT[g].ap,
                 mask_bf.ap[:, dr].rearrange("p c t -> p (c t)"), ALU.mult, [sT[g].t, mask_bf.t], [pT[g].t])
        for h in range(8):
            nbk = num[h // 3]
            o_ = nbk.ap[:, h % 3, :]
            P.mm(o_, pT[h // 4].ap[:, h % 4, :], vau[b].ap[:, h, :], True, False, [pT[h // 4].t, vau[b].t], [nbk.t])
            for ci in range(2):
                lo, hi = rows[ci]
                cbuf = Cb[(cbi[0] + ci) % 3]
                P.mm(nbk.ap[lo:hi, h % 3, :], qT_[b2].ap[:, h, lo:hi], cbuf.ap[:, h, :], False, True,
                     [qT_[b2].t, cbuf.sub[h]], [nbk.t])
        cbi[0] = (cbi[0] + 2) % 3
        ho = hout[b2]
        for k in range(3):
            nh = 3 if k < 2 else 2
            P.ts("dve", nd.ap[:, 3 * k:3 * k + nh], num[k].ap[:, 0:nh, 128], -1.0, ALU.mult, [num[k].t], [nd.t])
            P.stt("dve", dn.ap[:, 3 * k:3 * k + nh], num[k].ap[:, 0:nh, 128], 1.0, nd.ap[:, 3 * k:3 * k + nh],
                  ALU.max, ALU.max, [num[k].t, nd.t], [dn.t])
            P.recip(rd.ap[:, 3 * k:3 * k + nh], dn.ap[:, 3 * k:3 * k + nh], [dn.t], [rd.t])
            P.tt("dve", ho.ap[:, 384 * k:384 * k + 128 * nh].rearrange("p (h d) -> p h d", h=nh),
                 num[k].ap[:, 0:nh, 0:128], rd.ap[:, 3 * k:3 * k + nh].unsqueeze(2).broadcast_to([128, nh, 128]),
                 ALU.mult, [num[k].t, rd.t], [ho.t])
        r0 = ti * 128
        if dr == 0:
            P.dma("pool", d["hfw"][r0:r0 + 128, :], ho.ap, reads=[ho.t], load=False)
        else:
            P.tt("dve", ho.ap, ho.ap, hfw[b].ap, ALU.add, [ho.t, hfw[b].t], [ho.t])
            P.tt("dve", hsq.ap, ho.ap, ho.ap, ALU.mult, [ho.t], [hsq.t])
            P.op("dve", lambda e: e.tensor_reduce(out=ssm.ap, in_=hsq.ap.rearrange("p (h d) -> p h d", h=8),
                                                  axis=AX.X, op=ALU.add), [hsq.t], [ssm.t])
            P.act(rsm.ap, ssm.ap, AF.Ln, [ssm.t], [rsm.t], scale=1.0 / 128, bias=EPS)
            P.act(rsm.ap, rsm.ap, AF.Exp, [rsm.t], [rsm.t], scale=-0.5)
            P.tt("dve", ho.ap.rearrange("p (h d) -> p h d", h=8), ho.ap.rearrange("p (h d) -> p h d", h=8),
                 rsm.ap.unsqueeze(2).broadcast_to([128, 8, 128]), ALU.mult, [ho.t, rsm.t], [ho.t])
            P.tt("dve", ho.ap, ho.ap, mhn.ap, ALU.mult, [ho.t, mhn.t], [ho.t])
            P.tt("dve", ybt.ap, ho.ap, ozm[b].ap, ALU.mult, [ho.t, ozm[b].t], [ybt.t])
            slot = ti % 4
            sb_ = ybst[(ti // 4) % 2]
            for h in range(8):
                P.tr(tpm.ap[:, h * 128:(h + 1) * 128], ybt.ap[:, h * 128:(h + 1) * 128], ident.ap,
                     [ybt.t, ident.t], [tpm.t])
            P.copy("act", sb_.ap[:, :, slot * 128:(slot + 1) * 128], tpm.ap.rearrange("p (h t) -> p h t", h=8),
                   [tpm.t], [sb_.t])
            if slot == 0:
                t0 = ti * 128
                P.dma("pool", d["ybT"].rearrange("(c p) t -> p c t", p=128)[:, :, t0:t0 + 512], sb_.ap,
                      reads=[sb_.t], load=False)
    st.close()
    P.end_phase()


def phase_out(P, L, s, xin, xout, w_oa, w_ob, w_out, normf_b, own):
    st = ExitStack()
    S = s.S
    d = s.d
    TT = 512
    wa = Buf(P, st, "sb", "wa", [128, 8, D_MODEL], BF16, nsub=4)
    wbb = Buf(P, st, "sb", "wbb", [128, 8, D_MODEL], BF16, nsub=4)
    wo = Buf(P, st, "sb", "wo", [128, 16, D_MODEL], BF16, nsub=4)
    ya = Buf(P, st, "sb", "ya", [128, 8, TT], BF16)
    yb = Buf(P, st, "sb", "ybb", [128, 8, TT], BF16)
    mT = Buf(P, st, "sb", "mT", [128, 16, TT], BF16, nsub=16)
    sga = [Buf(P, st, "sb", f"sga{i}", [128, TT], BF16) for i in range(2)]
    sgb = [Buf(P, st, "sb", f"sgb{i}", [128, TT], BF16) for i in range(2)]
    t1_ = Buf(P, st, "sb", "t1_", [128, TT], F32)
    t2_ = Buf(P, st, "sb", "t2_", [128, TT], F32)
    t1 = [t1_, t1_]
    t2 = [t2_, t2_]
    xb = [Buf(P, st, "sb", f"xb{i}", [128, 512], F32) for i in range(2)]
    xo = [Buf(P, st, "sb", f"xo{i}", [128, D_MODEL], F32) for i in range(2)]
    pa = [Buf(P, st, "ps", f"pa{i}", [128, 512], F32) for i in range(2)]
    pb = [Buf(P, st, "ps", f"pb{i}", [128, 512], F32) for i in range(2)]
    po = [Buf(P, st, "ps", f"po{i}", [128, 512], F32) for i in range(3)]
    if normf_b is not None:
        nf = Buf(P, st, "sb", "nf", [128, D_MODEL], F32)
        junk = Buf(P, st, "sb", "junk2", [128, D_MODEL], BF16)
        ss = Buf(P, st, "sb", "ss2", [128, 4], F32)
        rs = Buf(P, st, "sb", "rs2", [128, 4], F32)
        P.dma("sp", nf.ap, normf_b[:, :], writes=[nf.t])
    cnt = {"sg": 0, "x": 0, "po": 0, "xo": 0}
    woaL = w_oa[L].rearrange("(k p) n -> p k n", p=128)
    wobL = w_ob[L].rearrange("(k p) n -> p k n", p=128)
    woutL = w_out[L].rearrange("(k p) n -> p k n", p=128)
    for c4 in range(4):
        P.dma("sp", wa.ap[:, :, c4 * 512:(c4 + 1) * 512], woaL[:, :, c4 * 512:(c4 + 1) * 512], writes=[wa.sub[c4]])
        P.dma("sp", wbb.ap[:, :, c4 * 512:(c4 + 1) * 512], wobL[:, :, c4 * 512:(c4 + 1) * 512], writes=[wbb.sub[c4]])
    for c4 in range(4):
        P.dma("sp", wo.ap[:, :, c4 * 512:(c4 + 1) * 512], woutL[:, :, c4 * 512:(c4 + 1) * 512], writes=[wo.sub[c4]])
    for tt in range(own // TT):
        t0 = tt * TT
        P.dma("sp", ya.ap, d["yaT"].rearrange("(c p) t -> p c t", p=128)[:, :, t0:t0 + TT], writes=[ya.t])
        P.dma("sp", yb.ap, d["ybT"].rearrange("(c p) t -> p c t", p=128)[:, :, t0:t0 + TT], writes=[yb.t])
        for fc in range(16):
            f4 = fc // 4
            si = cnt["sg"] % 2
            cnt["sg"] += 1
            P.dma("sp", sga[si].ap, d["sgaT"][fc * 128:(fc + 1) * 128, t0:t0 + TT], writes=[sga[si].t])
            P.dma("sp", sgb[si].ap, d["sgbT"][fc * 128:(fc + 1) * 128, t0:t0 + TT], writes=[sgb[si].t])
            a = pa[si]
            b = pb[si]
            for kc in range(8):
                P.mm(a.ap, wa.ap[:, kc, fc * 128:(fc + 1) * 128], ya.ap[:, kc, :], kc == 0, kc == 7,
                     [wa.sub[f4], ya.t], [a.t])
            for kc in range(8):
                P.mm(b.ap, wbb.ap[:, kc, fc * 128:(fc + 1) * 128], yb.ap[:, kc, :], kc == 0, kc == 7,
                     [wbb.sub[f4], yb.t], [b.t])
            P.tt("dve", t1[si].ap, a.ap, sga[si].ap, ALU.mult, [a.t, sga[si].t], [t1[si].t])
            P.tt("dve", t2[si].ap, b.ap, sgb[si].ap, ALU.mult, [b.t, sgb[si].t], [t2[si].t])
            P.tt("dve", mT.ap[:, fc, :], t1[si].ap, t2[si].ap, ALU.add, [t1[si].t, t2[si].t], [mT.sub[fc]])
        for sub in range(4):
            r0 = t0 + sub * 128
            xo_ = xo[cnt["xo"] % 2]
            cnt["xo"] += 1
            for cb in range(4):
                xi = cnt["x"] % 2
                cnt["x"] += 1
                P.dma("sp", xb[xi].ap, xin[r0:r0 + 128, cb * 512:(cb + 1) * 512], writes=[xb[xi].t])
                o = po[cnt["po"] % 3]
                cnt["po"] += 1
                for kc in range(16):
                    P.mm(o.ap, mT.ap[:, kc, sub * 128:(sub + 1) * 128], wo.ap[:, kc, cb * 512:(cb + 1) * 512],
                         kc == 0, kc == 15, [mT.sub[kc], wo.sub[cb]], [o.t])
                P.tt("dve", xo_.ap[:, cb * 512:(cb + 1) * 512], o.ap, xb[xi].ap, ALU.add, [o.t, xb[xi].t], [xo_.t])
            if normf_b is None:
                P.dma("pool", xout[r0:r0 + 128, :], xo_.ap, reads=[xo_.t], load=False)
            else:
                P.memset("dve", ss.ap[:, sub:sub + 1], 0.0, [ss.t])
                P.act(junk.ap, xo_.ap, AF.Square, [xo_.t, ss.t], [junk.t, ss.t], accum_out=ss.ap[:, sub:sub + 1])
                P.act(rs.ap[:, sub:sub + 1], ss.ap[:, sub:sub + 1], AF.Ln, [ss.t], [rs.t], scale=1.0 / D_MODEL,
                      bias=EPS)
                P.act(rs.ap[:, sub:sub + 1], rs.ap[:, sub:sub + 1], AF.Exp, [rs.t], [rs.t], scale=-0.5)
                P.stt("dve", xo_.ap, xo_.ap, rs.ap[:, sub:sub + 1], nf.ap, ALU.mult, ALU.mult,
                      [xo_.t, rs.t, nf.t], [xo_.t])
                P.dma("pool", xout[r0:r0 + 128, :], xo_.ap, reads=[xo_.t], load=False)
    st.close()
    P.end_phase()


def make_consts(smax):
    idx = np.arange(128)
    same = (idx[:, None] // CHUNK) == (idx[None, :] // CHUNK)
    ident = np.eye(128, dtype=np.float32)
    u_fw = (same & (idx[:, None] <= idx[None, :])).astype(np.float32)
    u_bw = (same & (idx[:, None] >= idx[None, :])).astype(np.float32)
    ones = np.ones((128, 128), np.float32)
    b0 = np.zeros((128, 128), np.float32)
    b0[0:64, :] = 1.0
    b1 = np.zeros((128, 128), np.float32)
    b1[64:128, :] = 1.0
    consts = np.concatenate([ident, u_fw, u_bw, ones, b0, b1], axis=1)
    inv = (np.float32(ROPE_THETA) ** (-np.arange(0, QK_ROPE, 2, dtype=np.float32) / np.float32(QK_ROPE))).astype(np.float32)
    ang = (np.arange(smax, dtype=np.float32)[:, None] * inv[None, :]).astype(np.float32)
    cos = np.cos(ang).astype(np.float32).T
    sin = np.sin(ang).astype(np.float32).T
    cos64 = np.concatenate([cos, cos], axis=0)
    sin64 = np.concatenate([-sin, sin], axis=0)
    rope = np.stack([np.concatenate([cos64, cos64], 0), np.concatenate([sin64, sin64], 0)], axis=1)
    return np.ascontiguousarray(consts), np.ascontiguousarray(rope.astype(np.float32))


def prep_weights(w_in, w_uq, w_ukv, norm_in, q_a_norm, kv_a_norm, b_gates, m_head_norm, norm_f):
    depth = w_in.shape[0]
    kr = w_in[:, :, O_KR:O_KR + 64]
    kr_sw = np.concatenate([kr[:, :, 32:64], kr[:, :, 0:32]], axis=2)
    w_in_ext = np.ascontiguousarray(np.concatenate([w_in, kr_sw], axis=2))
    uq = w_uq.reshape(depth, Q_LORA, A_HEADS, 192)
    nope = uq[..., :128].reshape(depth, Q_LORA, 1024)
    rope = uq[..., 128:]
    rope_sw = np.concatenate([rope[..., 32:], rope[..., :32]], axis=-1)
    w_uq_p = np.ascontiguousarray(np.concatenate([nope, rope.reshape(depth, Q_LORA, 512),
                                                  rope_sw.reshape(depth, Q_LORA, 512)], axis=2))
    ukv = w_ukv.reshape(depth, KV_LORA, A_HEADS, 256)
    w_ukv_p = np.ascontiguousarray(np.concatenate([ukv[..., :128].reshape(depth, KV_LORA, 1024),
                                                   ukv[..., 128:].reshape(depth, KV_LORA, 1024)], axis=2))
    out = {
        "w_in_ext": w_in_ext, "w_uq_p": w_uq_p, "w_ukv_p": w_ukv_p,
        "norm_in_t": np.ascontiguousarray(norm_in.reshape(depth, 16, 128).transpose(0, 2, 1)),
        "q_a_norm_t": np.ascontiguousarray(q_a_norm.reshape(depth, 4, 128).transpose(0, 2, 1)),
        "kv_a_norm_t": np.ascontiguousarray(kv_a_norm.reshape(depth, 4, 128).transpose(0, 2, 1)),
        "b_gates_b": np.ascontiguousarray(np.broadcast_to(b_gates[:, None, :], (depth, 128, 32))),
        "m_head_norm_b": np.ascontiguousarray(np.broadcast_to(m_head_norm[:, None, :], (depth, 128, 1024))),
        "norm_f_b": np.ascontiguousarray(np.broadcast_to(norm_f[None, :], (128, D_MODEL))),
    }
    return out


_CACHE = {}


def make_keep(S, reset_fw_chunk, reset_bw_chunk):
    nck = S // CHUNK
    keep = np.ones((nck, 16), np.float32)
    keep[(reset_fw_chunk - 1) % nck, 0:8] = 0.0
    keep[(reset_bw_chunk + 1) % nck, 8:16] = 0.0
    return np.ascontiguousarray(np.broadcast_to(keep[None], (128, nck, 16)))


def make_rope(positions):
    inv = (np.float32(ROPE_THETA) ** (-np.arange(0, QK_ROPE, 2, dtype=np.float32) / np.float32(QK_ROPE))).astype(np.float32)
    ang = (positions.astype(np.float32)[:, None] * inv[None, :]).astype(np.float32)
    cos = np.cos(ang).astype(np.float32).T
    sin = np.sin(ang).astype(np.float32).T
    cos64 = np.concatenate([cos, cos], axis=0)
    sin64 = np.concatenate([-sin, sin], axis=0)
    rope = np.stack([np.concatenate([cos64, cos64], 0), np.concatenate([sin64, sin64], 0)], axis=1)
    return np.ascontiguousarray(rope.astype(np.float32))


def core_inputs(c, x_prompt, x_sample, nq=4):
    B, S, _ = x_prompt.shape
    DB, DS, _ = x_sample.shape
    p, r = c % B, (c // B) % nq
    Q = S // nq
    m = {}
    m["x_s"] = x_sample[c % DB]
    m["x_p"] = np.ascontiguousarray(np.roll(x_prompt[p], -r * Q, axis=0))
    m["rope_s"] = make_rope(np.arange(DS))
    m["rope_p"] = make_rope((np.arange(S) + r * Q) % S)
    m["keep_s"] = make_keep(DS, 0, DS // CHUNK - 1)
    j0 = (nq - r) % nq
    je = (nq - 1 - r) % nq
    m["keep_p"] = make_keep(S, j0 * (Q // CHUNK), (je + 1) * (Q // CHUNK) - 1)
    return m


def kernel(x_prompt, x_sample, norm_in, w_in, b_gates, q_a_norm, w_uq, kv_a_norm, w_ukv,
           w_oa, m_head_norm, w_ob, w_out, norm_f):
    f = lambda a: np.ascontiguousarray(np.asarray(a, dtype=np.float32))
    x_prompt, x_sample = f(x_prompt), f(x_sample)
    B, S, _ = x_prompt.shape
    DB, DS, _ = x_sample.shape
    n = 8
    nq = n // B
    Q = S // nq
    key = (S, DS)
    if key not in _CACHE:
        _CACHE[key] = build([("s", DS), ("p", S)], prune={"p": Q})
    nc, _ = _CACHE[key]
    common = prep_weights(f(w_in), f(w_uq), f(w_ukv), f(norm_in), f(q_a_norm), f(kv_a_norm), f(b_gates),
                          f(m_head_norm), f(norm_f))
    common["w_oa"] = f(w_oa)
    common["w_ob"] = f(w_ob)
    common["w_out"] = f(w_out)
    common["consts"] = make_consts(128)[0]
    in_maps = []
    for c in range(n):
        m = dict(common)
        m.update(core_inputs(c, x_prompt, x_sample, nq))
        in_maps.append(m)
    res = run_bass_kernel_spmd(nc, in_maps, core_ids=list(range(n)))
    y_prompt = np.empty((B, S, D_MODEL), np.float32)
    for c in range(n):
        p, r = c % B, (c // B) % nq
        y_prompt[p, r * Q:(r + 1) * Q] = res.results[c]["y_p"]
    y_sample = np.stack([res.results[c]["y_s"] for c in range(DB)], axis=0)
    return (y_prompt, y_sample.astype(np.float32))
```

```python
import numpy as np
from contextlib import ExitStack
import ml_dtypes

import concourse.bass as bass
import concourse.mybir as mybir
from concourse.bass_utils import run_bass_kernel_spmd

F32 = mybir.dt.float32
BF16 = mybir.dt.bfloat16
AF = mybir.ActivationFunctionType
ALU = mybir.AluOpType
AX = mybir.AxisListType

D_MODEL = 2048
DEPTH = 2
A_HEADS = 8
Q_LORA = 512
KV_LORA = 512
QK_NOPE = 128
QK_ROPE = 64
V_HEAD = 128
ROPE_THETA = 10000.0
M_HEADS = 8
CHUNK = 64
EPS = 1e-6
IN_COLS = 11360
IN_EXT = IN_COLS + 64
O_CQ, O_CKV, O_KR, O_ZA = 0, 512, 1024, 1088
O_QM, O_KM, O_VM, O_OM, O_ZM, O_GT, O_GA, O_GB = 2112, 3136, 4160, 5184, 6208, 7232, 7264, 9312
O_KRS = IN_COLS
ATT_SCALE = float((QK_NOPE + QK_ROPE) ** -0.5)
MQ_SCALE = float(128 ** -0.5)

COMPUTE = ("pe", "act", "dve", "pool")
ENGS = ("pe", "act", "dve", "pool", "sp")


class Tile:
    __slots__ = ("name", "w", "r", "lsem", "ssem")

    def __init__(self, name):
        self.name = name
        self.w = {}
        self.r = {}
        self.lsem = None
        self.ssem = None


class Op:
    __slots__ = ("eng", "fn", "waits", "sig", "idx", "val", "dma")

    def __init__(self, eng, fn):
        self.eng = eng
        self.fn = fn
        self.waits = []
        self.sig = False
        self.idx = -1
        self.val = -1
        self.dma = None


class Prog:
    def __init__(self, nc):
        self.nc = nc
        self.q = {e: [] for e in ENGS}
        self.esem = {e: nc.alloc_semaphore(f"cnt_{e}") for e in COMPUTE}
        self.bar = nc.alloc_semaphore("bar")
        self.nbar = 0
        self.count = {e: 0 for e in COMPUTE}
        self.nops = {e: 0 for e in ENGS}
        self.waited = {e: {} for e in ENGS}
        self.dma_out = {e: {} for e in ENGS}
        self.tiles = []
        self.free_sems = {}
        self.semq = {}
        self.semcnt = {}
        self.semobj = {}
        self.n_inst = 0

    def tile(self, name):
        t = Tile(name)
        self.tiles.append(t)
        return t

    def _get_sem(self, queue):
        fl = self.free_sems.setdefault(queue, [])
        if fl:
            return fl.pop()
        s = self.nc.alloc_semaphore(f"dsem{len(self.semobj)}")
        self.semobj[id(s)] = s
        self.semcnt[id(s)] = 0
        self.semq[id(s)] = queue
        return s

    @staticmethod
    def _newer(a, b):
        if isinstance(a, Op):
            return a.idx > b.idx
        return a[1] > b[1]

    def _collect(self, reads, writes):
        need = {}
        for t in reads:
            for k, v in t.w.items():
                c = need.get(k)
                if c is None or self._newer(v, c):
                    need[k] = v
        for t in writes:
            for d in (t.w, t.r):
                for k, v in d.items():
                    c = need.get(k)
                    if c is None or self._newer(v, c):
                        need[k] = v
        return need

    def _filter(self, eng, need):
        waits = []
        wd = self.waited[eng]
        for k, v in need.items():
            if isinstance(v, Op):
                if k == eng and eng == "pe":
                    continue
                if wd.get(k, -1) >= v.idx:
                    continue
                wd[k] = v.idx
                v.sig = True
                waits.append(v)
            else:
                if wd.get(k, 0) >= v[1]:
                    continue
                wd[k] = v[1]
                waits.append(v)
        return waits

    def op(self, eng, fn, reads=(), writes=()):
        o = Op(eng, fn)
        o.idx = self.nops[eng]
        self.nops[eng] += 1
        o.waits = self._filter(eng, self._collect(reads, writes))
        for t in reads:
            t.r[eng] = o
        for t in writes:
            t.w = {eng: o}
            t.r = {}
        self.q[eng].append(o)
        return o

    def dma(self, queue, out, in_, reads=(), writes=(), load=True):
        o = Op(queue, lambda e: e.dma_start(out=out, in_=in_))
        o.idx = self.nops[queue]
        self.nops[queue] += 1
        o.waits = self._filter(queue, self._collect(reads, writes))
        if load:
            t = writes[0]
            if t.lsem is None:
                t.lsem = self._get_sem(queue)
            assert self.semq[id(t.lsem)] == queue
            sem = t.lsem
        else:
            t = reads[0]
            if t.ssem is None:
                t.ssem = self._get_sem(queue)
            assert self.semq[id(t.ssem)] == queue
            sem = t.ssem
        self.semcnt[id(sem)] += 16
        dep = (sem, self.semcnt[id(sem)])
        o.dma = dep
        key = ("d", id(sem))
        for t in reads:
            t.r[key] = dep
        for t in writes:
            t.w = {key: dep}
            t.r = {}
        self.dma_out[queue][id(sem)] = dep
        self.q[queue].append(o)
        return o

    def dma_raw(self, queue, out, in_, sem):
        o = Op(queue, lambda e: e.dma_start(out=out, in_=in_))
        o.idx = self.nops[queue]
        self.nops[queue] += 1
        self.semcnt[id(sem)] += 16
        o.dma = (sem, self.semcnt[id(sem)])
        self.dma_out[queue][id(sem)] = o.dma
        self.q[queue].append(o)

    def end_phase(self):
        nc = self.nc
        self.nbar += 1
        for e in COMPUTE:
            ql = self.q[e]
            for o in reversed(ql):
                if o.dma is None:
                    o.sig = True
                    break
        for e in COMPUTE:
            c = self.count[e]
            for o in self.q[e]:
                if o.dma is None and o.sig:
                    c += 1
                    o.val = c
            self.count[e] = c
        names = {"sp": "sync", "act": "scalar", "dve": "vector", "pe": "tensor", "pool": "gpsimd"}
        with nc.Block() as block:
            for e in ENGS:
                deco = getattr(block, names[e])

                def body(eng, e=e):
                    self._emit_engine(e, eng)

                deco(body)
        for e in ENGS:
            self.n_inst += len(self.q[e])
            self.q[e] = []
            self.dma_out[e] = {}
        for t in self.tiles:
            if t.lsem is not None:
                self.free_sems[self.semq[id(t.lsem)]].append(t.lsem)
            if t.ssem is not None:
                self.free_sems[self.semq[id(t.ssem)]].append(t.ssem)
        self.tiles = []

    def _emit_engine(self, e, eng):
        esem = self.esem
        for o in self.q[e]:
            for w in o.waits:
                if isinstance(w, Op):
                    eng.wait_ge(esem[w.eng], w.val)
                else:
                    eng.wait_ge(w[0], w[1])
            inst = o.fn(eng)
            if o.dma is not None:
                inst.then_inc(o.dma[0], 16)
            elif o.sig:
                inst.then_inc(esem[e], 1)
        if e in COMPUTE and self.count[e] > 0:
            eng.wait_ge(esem[e], self.count[e])
        for sem, val in self.dma_out[e].values():
            eng.wait_ge(sem, val)
        eng.sem_inc(self.bar, 1)
        eng.wait_ge(self.bar, len(ENGS) * self.nbar)

    def mm(self, out, lhsT, rhs, start, stop, reads, writes):
        return self.op("pe", lambda e: e.matmul(out, lhsT=lhsT, rhs=rhs, start=start, stop=stop), reads, writes)

    def tr(self, out, in_, ident, reads, writes):
        return self.op("pe", lambda e: e.transpose(out, in_, ident), reads, writes)

    def act(self, out, in_, func, reads, writes, **kw):
        return self.op("act", lambda e: e.activation(out=out, in_=in_, func=func, **kw), reads, writes)

    def tt(self, eng, out, in0, in1, op, reads, writes):
        return self.op(eng, lambda e: e.tensor_tensor(out=out, in0=in0, in1=in1, op=op), reads, writes)

    def ts(self, eng, out, in0, s1, op0, reads, writes, s2=None, op1=None):
        if op1 is None:
            return self.op(eng, lambda e: e.tensor_scalar(out=out, in0=in0, scalar1=s1, scalar2=None, op0=op0),
                           reads, writes)
        return self.op(eng, lambda e: e.tensor_scalar(out=out, in0=in0, scalar1=s1, scalar2=s2, op0=op0, op1=op1),
                       reads, writes)

    def stt(self, eng, out, in0, scalar, in1, op0, op1, reads, writes):
        return self.op(eng, lambda e: e.scalar_tensor_tensor(out=out, in0=in0, scalar=scalar, in1=in1,
                                                             op0=op0, op1=op1), reads, writes)

    def copy(self, eng, out, in_, reads, writes):
        if eng == "act":
            return self.act(out, in_, AF.Copy, reads, writes)
        return self.op(eng, lambda e: e.tensor_copy(out=out, in_=in_), reads, writes)

    def memset(self, eng, ap, val, writes):
        return self.op(eng, lambda e: e.memset(ap, val), (), writes)

    def recip(self, out, in_, reads, writes):
        return self.op("dve", lambda e: e.reciprocal(out=out, in_=in_), reads, writes)


class Buf:
    _uid = [0]

    def __init__(self, P, st, kind, name, shape, dtype, nsub=1):
        nc = P.nc
        Buf._uid[0] += 1
        name = f"{name}_u{Buf._uid[0]}"
        if kind == "sb":
            h = st.enter_context(nc.sbuf_tensor(name, list(shape), dtype))
        else:
            h = st.enter_context(nc.psum_tensor(name, list(shape), dtype))
        self.h = h
        self.ap = h[:]
        self.t = P.tile(name)
        self.sub = [P.tile(f"{name}.{i}") for i in range(nsub)] if nsub > 1 else None


class Seg:
    def __init__(self, name, S):
        self.name = name
        self.S = S
        self.d = {}


def build(seg_sizes, depth=DEPTH, phases=None, dbg=(), prune=None):
    prune = prune or {}
    nc = bass.Bass("TRN2", target_bir_lowering=False)
    P = Prog(nc)
    segs = [Seg(n, S) for n, S in seg_sizes]
    SMAX = max(S for _, S in seg_sizes)

    def din(name, shape, dt=F32):
        return nc.dram_tensor(name, list(shape), dt, kind="ExternalInput").ap()

    def dscr(name, shape, dt=BF16):
        if name in dbg:
            return nc.dram_tensor(name, list(shape), dt, kind="ExternalOutput").ap()
        return nc.dram_tensor(name, list(shape), dt).ap()

    for s in segs:
        s.d["x"] = din(f"x_{s.name}", [s.S, D_MODEL])
        s.own = prune.get(s.name, s.S)
        s.d["y"] = nc.dram_tensor(f"y_{s.name}", [s.own, D_MODEL], F32, kind="ExternalOutput").ap()
        s.d["keep"] = din(f"keep_{s.name}", [128, s.S // CHUNK, 16])
        s.d["rope"] = din(f"rope_{s.name}", [128, 2, s.S])
    w_in_f = din("w_in_ext", [depth, D_MODEL, IN_EXT])
    w_uq_f = din("w_uq_p", [depth, Q_LORA, 2048])
    w_ukv_f = din("w_ukv_p", [depth, KV_LORA, 2048])
    w_oa_f = din("w_oa", [depth, 1024, D_MODEL])
    w_ob_f = din("w_ob", [depth, 1024, D_MODEL])
    w_out_f = din("w_out", [depth, D_MODEL, D_MODEL])
    norm_in_t = din("norm_in_t", [depth, 128, 16])
    qan_t = din("q_a_norm_t", [depth, 128, 4])
    kvan_t = din("kv_a_norm_t", [depth, 128, 4])
    bgates_b = din("b_gates_b", [depth, 128, 32])
    mhn_b = din("m_head_norm_b", [depth, 128, 1024])
    normf_b = din("norm_f_b", [128, D_MODEL])
    consts_f = din("consts", [128, 6 * 128])

    w_in = dscr("w_in_bf", [depth, D_MODEL, IN_EXT])
    w_uq = dscr("w_uq_bf", [depth, Q_LORA, 2048])
    w_ukv = dscr("w_ukv_bf", [depth, KV_LORA, 2048])
    w_oa = dscr("w_oa_bf", [depth, 1024, D_MODEL])
    w_ob = dscr("w_ob_bf", [depth, 1024, D_MODEL])
    w_out = dscr("w_out_bf", [depth, D_MODEL, D_MODEL])

    for s in segs:
        S = s.S
        n = s.name
        d = s.d
        d["x1"] = dscr(f"x1_{n}", [S, D_MODEL], F32)
        d["qT"] = dscr(f"qT_{n}", [A_HEADS, 192, S])
        d["knT"] = dscr(f"knT_{n}", [A_HEADS, 128, S])
        d["krT"] = dscr(f"krT_{n}", [64, S])
        d["Vh"] = dscr(f"Vh_{n}", [A_HEADS, 128, S // 128, 128])
        d["zaT"] = dscr(f"zaT_{n}", [1024, S])
        d["qm"] = dscr(f"qm_{n}", [2, S, 1024])
        d["km"] = dscr(f"km_{n}", [2, S, 1024])
        d["vm"] = dscr(f"vm_{n}", [S, 1024])
        d["EG"] = dscr(f"EG_{n}", [128, S // CHUNK, 16], F32)
        d["ozm"] = dscr(f"ozm_{n}", [S, 1024])
        d["hfw"] = dscr(f"hfw_{n}", [S, 1024], F32)
        d["yaT"] = dscr(f"yaT_{n}", [1024, S])
        d["ybT"] = dscr(f"ybT_{n}", [1024, S])
        d["sgaT"] = dscr(f"sgaT_{n}", [D_MODEL, S])
        d["sgbT"] = dscr(f"sgbT_{n}", [D_MODEL, S])

    top = ExitStack()
    cst = Buf(P, top, "sb", "cst_f", [128, 6 * 128], F32)
    cbf = Buf(P, top, "sb", "cbf", [128, 6, 128], BF16)
    ident = Buf(P, top, "sb", "ident", [128, 128], BF16)
    ones_bf = Buf(P, top, "sb", "ones_bf", [128, 128], BF16)
    mask_bf = Buf(P, top, "sb", "mask_bf", [128, 2, 4, 128], BF16)
    zero_f = Buf(P, top, "sb", "zero_f", [128, 16], F32)
    U_fw = cst.ap[:, 128:256]
    U_bw = cst.ap[:, 256:384]
    ones_f = cst.ap[:, 384:512]

    wsem = nc.alloc_semaphore("wsem")
    P.semcnt[id(wsem)] = 0
    P.dma("sp", cst.ap, consts_f[:, :], writes=[cst.t])
    P.copy("dve", cbf.ap.rearrange("p a b -> p (a b)"), cst.ap, [cst.t], [cbf.t])
    P.copy("dve", ident.ap, cst.ap[:, 0:128], [cst.t], [ident.t])
    P.copy("dve", ones_bf.ap, ones_f, [cst.t], [ones_bf.t])
    for dr in range(2):
        for c in range(4):
            P.copy("dve", mask_bf.ap[:, dr, c, :], cst.ap[:, 128 * (1 + dr):128 * (2 + dr)], [cst.t], [mask_bf.t])
    P.memset("dve", zero_f.ap, 0.0, [zero_f.t])
    def cast_w(dst, src, rows, cols):
        for r0 in range(0, rows, 128):
            for c0 in range(0, cols, 2048):
                c1 = min(cols, c0 + 2048)
                P.dma_raw("pool", dst[r0:r0 + 128, c0:c1], src[r0:r0 + 128, c0:c1], wsem)

    for L in range(depth):
        cast_w(w_in[L], w_in_f[L], D_MODEL, IN_EXT)
        cast_w(w_out[L], w_out_f[L], D_MODEL, D_MODEL)
        cast_w(w_oa[L], w_oa_f[L], 1024, D_MODEL)
        cast_w(w_ob[L], w_ob_f[L], 1024, D_MODEL)
        cast_w(w_uq[L], w_uq_f[L], Q_LORA, 2048)
        cast_w(w_ukv[L], w_ukv_f[L], KV_LORA, 2048)
    P.end_phase()

    for L in range(depth):
        last = (L == depth - 1)
        for s in segs:
            xin = s.d["x"] if L == 0 else s.d["x1"]
            own = s.own if last else s.S
            NT = s.S // 128
            if phases is None or "p1" in phases:
                phase1(P, L, s, xin, w_in, w_uq, w_ukv, norm_in_t, qan_t, kvan_t, bgates_b, s.d["rope"],
                       ident, ones_bf, cbf, own)
            if phases is None or "att" in phases:
                phase_att(P, L, s, ones_bf, own)
            if phases is None or "ml" in phases:
                ot = own // 128
                if own == s.S and s.own == s.S:
                    plan_fw = [(list(range(NT)), True)]
                    plan_bw = [(list(range(NT - 1, -1, -1)), True)]
                elif own == s.S:
                    plan_fw = [(list(range(NT)), False), (list(range(NT)), True)]
                    plan_bw = [(list(range(NT - 1, -1, -1)), False), (list(range(NT - 1, -1, -1)), True)]
                else:
                    plan_fw = [(list(range(ot, NT)), False), (list(range(ot)), True)]
                    plan_bw = [(list(range(NT - 1, ot - 1, -1)), False), (list(range(ot - 1, -1, -1)), True)]
                phase_mlstm(P, L, s, 0, mhn_b, ident, mask_bf, plan_fw)
                phase_mlstm(P, L, s, 1, mhn_b, ident, mask_bf, plan_bw)
            if phases is None or "out" in phases:
                xout = s.d["y"] if last else s.d["x1"]
                phase_out(P, L, s, xin, xout, w_oa, w_ob, w_out, normf_b if last else None, own)
    top.close()
    return nc, P


def phase1(P, L, s, xin, w_in, w_uq, w_ukv, norm_in_t, qan_t, kvan_t, bgates_b, rope_cs,
           ident, ones_bf, cbf, own):
    st = ExitStack()
    S = s.S
    d = s.d
    TT = 1024
    xt = [Buf(P, st, "sb", f"xt{i}", [128, D_MODEL], F32) for i in range(2)]
    junk = Buf(P, st, "sb", "junk", [128, D_MODEL], BF16)
    hb = [Buf(P, st, "sb", f"hb{i}", [128, D_MODEL], BF16) for i in range(2)]
    hT = Buf(P, st, "sb", "hT", [128, 16, TT], BF16, nsub=8)
    wb = [Buf(P, st, "sb", f"wb{i}", [128, 16, 512], BF16) for i in range(3)]
    wuq = Buf(P, st, "sb", "wuq", [128, 4, 2048], BF16)
    wukv = Buf(P, st, "sb", "wukv", [128, 4, 2048], BF16)
    craw1 = Buf(P, st, "sb", "craw", [128, 4, TT], BF16)
    craw = [craw1, craw1]
    cn = [Buf(P, st, "sb", f"cn{i}", [128, 4, TT], BF16, nsub=8) for i in range(2)]
    sq = [Buf(P, st, "sb", f"sq{i}", [128, 512], BF16) for i in range(2)]
    rstdb = Buf(P, st, "sb", "rstdb", [128, 512], F32)
    gain = Buf(P, st, "sb", "gain", [128, 16], F32)
    qg = Buf(P, st, "sb", "qg", [128, 4], F32)
    kvg = Buf(P, st, "sb", "kvg", [128, 4], F32)
    bg = Buf(P, st, "sb", "bg", [128, 32], F32)
    cs = Buf(P, st, "sb", "cs", [128, 2, TT], F32)
    stg = [Buf(P, st, "sb", f"stg{i}", [128, TT], BF16) for i in range(4)]
    f1 = [Buf(P, st, "sb", f"f1_{i}", [128, 512], F32) for i in range(2)]
    f2 = [Buf(P, st, "sb", f"f2_{i}", [128, 512], F32) for i in range(2)]
    sA = [Buf(P, st, "sb", f"sA{i}", [128, 512], BF16) for i in range(2)]
    sB = [Buf(P, st, "sb", f"sB{i}", [128, 512], BF16) for i in range(2)]
    ss = Buf(P, st, "sb", "ss", [128, 8], F32)
    lnv = Buf(P, st, "sb", "lnv", [128, 8], F32)
    rstd = Buf(P, st, "sb", "rstd", [128, 8], F32)
    gt = [Buf(P, st, "sb", f"gt{i}", [128, 32], F32) for i in range(2)]
    lf = [Buf(P, st, "sb", f"lf{i}", [128, 16], F32) for i in range(2)]
    hl = [Buf(P, st, "sb", f"hl{i}", [128, 2, 16], BF16) for i in range(2)]
    ab = [Buf(P, st, "sb", f"ab{i}", [128, 32], F32) for i in range(2)]
    eab = Buf(P, st, "sb", "eab", [128, 8, 32], F32, nsub=8)
    egs = Buf(P, st, "sb", "egs", [128, 16, 16], F32)
    kps = Buf(P, st, "sb", "kps", [128, 16, 16], F32)
    tp = [Buf(P, st, "ps", f"tp{i}", [128, 1024], BF16) for i in range(2)]
    acc = [Buf(P, st, "ps", f"acc{i}", [128, 512], F32) for i in range(4)]
    sm = Buf(P, st, "ps", "sm", [128, 512], F32)
    sm_g = sm_b = sm_G = sm.t

    cnt = {"acc": 0, "stg": 0, "wb": 0, "f": 0, "s": 0}

    def nacc():
        a = acc[cnt["acc"] % 4]
        cnt["acc"] += 1
        return a

    def nstg():
        a = stg[cnt["stg"] % 4]
        cnt["stg"] += 1
        return a

    P.dma("sp", gain.ap, norm_in_t[L], writes=[gain.t])
    P.dma("sp", qg.ap, qan_t[L], writes=[qg.t])
    P.dma("sp", kvg.ap, kvan_t[L], writes=[kvg.t])
    P.dma("sp", bg.ap, bgates_b[L], writes=[bg.t])
    P.dma("sp", wuq.ap, w_uq[L].rearrange("(k p) n -> p k n", p=128), writes=[wuq.t])
    P.dma("sp", wukv.ap, w_ukv[L].rearrange("(k p) n -> p k n", p=128), writes=[wukv.t])

    def load_w(col0, ncols):
        b = wb[cnt["wb"] % 3]
        cnt["wb"] += 1
        P.dma("sp", b.ap[:, :, 0:ncols], w_in[L, :, col0:col0 + ncols].rearrange("(k p) n -> p k n", p=128),
              writes=[b.t])
        return b

    def gemmB(lhs_of_kc, nk, rhs_buf, rhs_tiles, hf, out_ap, out_tile, extra_reads):
        for kc in range(nk):
            P.mm(out_ap, lhs_of_kc(kc), rhs_buf.ap[:, kc, hf * 512:(hf + 1) * 512], kc == 0, kc == nk - 1,
                 list(extra_reads) + list(rhs_tiles), [out_tile])

    import os
    _STOP = int(os.environ.get('P1STOP', '99'))
    _G = int(os.environ.get('GSTOP', '99'))
    for tt in range(S // TT):
        t0 = tt * TT
        full = t0 < own
        P.dma("sp", cs.ap, rope_cs[:, :, t0:t0 + TT], writes=[cs.t])
        P.memset("dve", ss.ap, 0.0, [ss.t])
        for sub in range(8):
            x = xt[sub % 2]
            h_ = hb[sub % 2]
            P.dma("sp", x.ap, xin[t0 + sub * 128:t0 + (sub + 1) * 128, :], writes=[x.t])
            P.act(junk.ap, x.ap, AF.Square, [x.t], [junk.t, ss.t], accum_out=ss.ap[:, sub:sub + 1])
            P.act(lnv.ap[:, sub:sub + 1], ss.ap[:, sub:sub + 1], AF.Ln, [ss.t], [lnv.t], scale=1.0 / D_MODEL, bias=EPS)
            P.act(rstd.ap[:, sub:sub + 1], lnv.ap[:, sub:sub + 1], AF.Exp, [lnv.t], [rstd.t], scale=-0.5)
            P.ts("dve", h_.ap, x.ap, rstd.ap[:, sub:sub + 1], ALU.mult, [x.t, rstd.t], [h_.t])
            for half in range(2):
                tpb = tp[half]
                for k in range(8):
                    kc = half * 8 + k
                    P.tr(tpb.ap[:, k * 128:(k + 1) * 128], h_.ap[:, kc * 128:(kc + 1) * 128], ident.ap,
                         [h_.t, ident.t], [tpb.t])
                P.tt("dve", hT.ap[:, half * 8:half * 8 + 8, sub * 128:(sub + 1) * 128],
                     tpb.ap.rearrange("p (a b) -> p a b", a=8),
                     gain.ap[:, half * 8:half * 8 + 8].unsqueeze(2).broadcast_to([128, 8, 128]),
                     ALU.mult, [tpb.t, gain.t], [hT.sub[sub]])
        hT_all = hT.sub

        if _STOP <= 1:
            continue
        for g, (col0, gn) in enumerate(((O_CQ, qg), (O_CKV, kvg))):
            if g == 0 and not full:
                continue
            wbuf = load_w(col0, 512)
            for hf in range(2):
                for blk in range(4):
                    a = nacc()
                    gemmB(lambda kc, blk=blk: wbuf.ap[:, kc, blk * 128:(blk + 1) * 128], 16, hT,
                          hT_all[hf * 4:hf * 4 + 4], hf, a.ap, a.t, [wbuf.t])
                    P.act(craw[g].ap[:, blk, hf * 512:(hf + 1) * 512], a.ap, AF.Copy, [a.t], [craw[g].t])
                a = nacc()
                for blk in range(4):
                    q_ = sq[blk % 2]
                    P.tt("dve", q_.ap, craw[g].ap[:, blk, hf * 512:(hf + 1) * 512],
                         craw[g].ap[:, blk, hf * 512:(hf + 1) * 512], ALU.mult, [craw[g].t], [q_.t])
                    P.mm(a.ap, ones_bf.ap, q_.ap, blk == 0, blk == 3, [ones_bf.t, q_.t], [a.t])
                P.act(rstdb.ap, a.ap, AF.Ln, [a.t], [rstdb.t], scale=1.0 / 512, bias=EPS)
                P.act(rstdb.ap, rstdb.ap, AF.Exp, [rstdb.t], [rstdb.t], scale=-0.5)
                for blk in range(4):
                    P.stt("dve", cn[g].ap[:, blk, hf * 512:(hf + 1) * 512],
                          craw[g].ap[:, blk, hf * 512:(hf + 1) * 512], gn.ap[:, blk:blk + 1], rstdb.ap,
                          ALU.mult, ALU.mult, [craw[g].t, gn.t, rstdb.t], cn[g].sub[hf * 4:hf * 4 + 4])

        if _STOP <= 2:
            continue
        for h in range(A_HEADS if full else 0):
            o = nstg()
            for hf in range(2):
                a = nacc()
                gemmB(lambda kc, h=h: wuq.ap[:, kc, h * 128:(h + 1) * 128], 4, cn[0], cn[0].sub[hf * 4:hf * 4 + 4],
                      hf, a.ap, a.t, [wuq.t])
                P.act(o.ap[:, hf * 512:(hf + 1) * 512], a.ap, AF.Copy, [a.t], [o.t])
            P.dma("pool", d["qT"][h, 0:128, t0:t0 + TT], o.ap, reads=[o.t], load=False)
        for hp in range(A_HEADS // 2 if full else 0):
            o = nstg()
            for hf in range(2):
                a = nacc()
                b = nacc()
                gemmB(lambda kc, hp=hp: wuq.ap[:, kc, 1024 + hp * 128:1024 + (hp + 1) * 128], 4, cn[0],
                      cn[0].sub[hf * 4:hf * 4 + 4], hf, a.ap, a.t, [wuq.t])
                gemmB(lambda kc, hp=hp: wuq.ap[:, kc, 1536 + hp * 128:1536 + (hp + 1) * 128], 4, cn[0],
                      cn[0].sub[hf * 4:hf * 4 + 4], hf, b.ap, b.t, [wuq.t])
                u = f1[hf]
                v = f2[hf]
                P.tt("dve", u.ap, a.ap, cs.ap[:, 0, hf * 512:(hf + 1) * 512], ALU.mult, [a.t, cs.t], [u.t])
                P.tt("dve", v.ap, b.ap, cs.ap[:, 1, hf * 512:(hf + 1) * 512], ALU.mult, [b.t, cs.t], [v.t])
                P.tt("dve", o.ap[:, hf * 512:(hf + 1) * 512], u.ap, v.ap, ALU.add, [u.t, v.t], [o.t])
            P.dma("pool", d["qT"][2 * hp, 128:192, t0:t0 + TT], o.ap[0:64, :], reads=[o.t], load=False)
            P.dma("pool", d["qT"][2 * hp + 1, 128:192, t0:t0 + TT], o.ap[64:128, :], reads=[o.t], load=False)

        if _STOP <= 3:
            continue
        for h in range(A_HEADS):
            o = nstg()
            for hf in range(2):
                a = nacc()
                gemmB(lambda kc, h=h: wukv.ap[:, kc, h * 128:(h + 1) * 128], 4, cn[1], cn[1].sub[hf * 4:hf * 4 + 4],
                      hf, a.ap, a.t, [wukv.t])
                P.act(o.ap[:, hf * 512:(hf + 1) * 512], a.ap, AF.Copy, [a.t], [o.t])
            P.dma("pool", d["knT"][h, :, t0:t0 + TT], o.ap, reads=[o.t], load=False)
        for sub in range(8):
            o = nstg()
            for j in range(2):
                a = nacc()
                for kc in range(4):
                    P.mm(a.ap, cn[1].ap[:, kc, sub * 128:(sub + 1) * 128],
                         wukv.ap[:, kc, 1024 + j * 512:1024 + (j + 1) * 512], kc == 0, kc == 3,
                         [cn[1].sub[sub], wukv.t], [a.t])
                P.act(o.ap[:, j * 512:(j + 1) * 512], a.ap, AF.Copy, [a.t], [o.t])
            blk = (t0 + sub * 128) // 128
            P.dma("pool", d["Vh"][:, :, blk, :].rearrange("h p d -> p h d"),
                  o.ap.rearrange("p (h d) -> p h d", h=8), reads=[o.t], load=False)

        if _STOP <= 4:
            continue
        wk = load_w(O_KR, 64)
        wks = load_w(O_KRS, 64)
        o = nstg()
        for hf in range(2):
            a = nacc()
            b = nacc()
            gemmB(lambda kc: wk.ap[:, kc, 0:64], 16, hT, hT_all[hf * 4:hf * 4 + 4], hf, a.ap[0:64, :], a.t, [wk.t])
            gemmB(lambda kc: wks.ap[:, kc, 0:64], 16, hT, hT_all[hf * 4:hf * 4 + 4], hf, b.ap[0:64, :], b.t, [wks.t])
            u = f1[hf]
            v = f2[hf]
            P.tt("dve", u.ap[0:64, :], a.ap[0:64, :], cs.ap[0:64, 0, hf * 512:(hf + 1) * 512], ALU.mult,
                 [a.t, cs.t], [u.t])
            P.tt("dve", v.ap[0:64, :], b.ap[0:64, :], cs.ap[0:64, 1, hf * 512:(hf + 1) * 512], ALU.mult,
                 [b.t, cs.t], [v.t])
            P.tt("dve", o.ap[0:64, hf * 512:(hf + 1) * 512], u.ap[0:64, :], v.ap[0:64, :], ALU.add,
                 [u.t, v.t], [o.t])
        P.dma("pool", d["krT"][:, t0:t0 + TT], o.ap[0:64, :], reads=[o.t], load=False)

        if _STOP <= 5:
            continue
        for (col0, nblk, func, dst) in ((O_ZA, 8, AF.Silu, d["zaT"]), (O_GA, 16, AF.Sigmoid, d["sgaT"]),
                                        (O_GB, 16, AF.Sigmoid, d["sgbT"])):
            if not full:
                continue
            for b4 in range(nblk // 4):
                wbuf = load_w(col0 + b4 * 512, 512)
                for blk in range(4):
                    o = nstg()
                    for hf in range(2):
                        a = nacc()
                        gemmB(lambda kc, blk=blk: wbuf.ap[:, kc, blk * 128:(blk + 1) * 128], 16, hT,
                              hT_all[hf * 4:hf * 4 + 4], hf, a.ap, a.t, [wbuf.t])
                        P.act(o.ap[:, hf * 512:(hf + 1) * 512], a.ap, func, [a.t], [o.t])
                    r0 = (b4 * 4 + blk) * 128
                    P.dma("pool", dst[r0:r0 + 128, t0:t0 + TT], o.ap, reads=[o.t], load=False)

        if _STOP <= 6:
            continue
        wg = load_w(O_GT, 32)
        for sub in range(8):
            i2 = sub % 2
            for kc in range(16):
                P.mm(sm.ap[:, 0:32], hT.ap[:, kc, sub * 128:(sub + 1) * 128], wg.ap[:, kc, 0:32], kc == 0, kc == 15,
                     [hT.sub[sub], wg.t], [sm_g])
            g_ = gt[i2]
            P.tt("dve", g_.ap, sm.ap[:, 0:32], bg.ap, ALU.add, [sm_g, bg.t], [g_.t])
            if _G <= 1:
                continue
            l_ = lf[i2]
            P.act(l_.ap, g_.ap[:, 16:32], AF.Exp, [g_.t], [l_.t], scale=-1.0)
            P.act(l_.ap, l_.ap, AF.Ln, [l_.t], [l_.t], bias=1.0)
            P.ts("dve", l_.ap, l_.ap, -1.0, ALU.mult, [l_.t], [l_.t])
            if _G <= 2:
                continue
            hl_ = hl[i2]
            P.copy("dve", hl_.ap[:, 0, :], l_.ap, [l_.t], [hl_.t])
            P.tt("dve", hl_.ap[:, 1, :], l_.ap, hl_.ap[:, 0, :], ALU.subtract, [l_.t, hl_.t], [hl_.t])
            if _G <= 3:
                continue
            for part in range(2):
                P.mm(sm.ap[:, 64:72], cbf.ap[:, 1, :], hl_.ap[:, part, 0:8], part == 0, part == 1, [cbf.t, hl_.t], [sm_b])
            for part in range(2):
                P.mm(sm.ap[:, 72:80], cbf.ap[:, 2, :], hl_.ap[:, part, 8:16], part == 0, part == 1, [cbf.t, hl_.t], [sm_b])
            if _G <= 4:
                continue
            for c in range(2):
                for part in range(2):
                    P.mm(sm.ap[:, 128 + 16 * c:144 + 16 * c], cbf.ap[:, 4 + c, :], hl_.ap[:, part, :], part == 0,
                         part == 1, [cbf.t, hl_.t], [sm_G])
            if _G <= 5:
                continue
            a_ = ab[i2]
            _G2 = int(os.environ.get('G2', '99'))
            P.copy("dve", a_.ap[:, 0:16], sm.ap[:, 64:80], [sm_b], [a_.t])
            if _G2 <= 1:
                continue
            P.tt("dve", a_.ap[:, 16:32], g_.ap[:, 0:16], a_.ap[:, 0:16], ALU.subtract, [g_.t, a_.t], [a_.t])
            if _G2 <= 2:
                continue
            P.act(eab.ap[:, sub, :], a_.ap, AF.Exp, [a_.t], [eab.sub[sub]])
            if _G2 <= 3:
                continue
            P.ts("dve", eab.ap[:, sub, 0:16], eab.ap[:, sub, 0:16], MQ_SCALE, ALU.mult, [eab.sub[sub]], [eab.sub[sub]])
            if _G <= 6:
                continue
            P.act(egs.ap[:, 2 * sub:2 * sub + 2, :], sm.ap[:, 128:160].rearrange("p (c n) -> p c n", c=2), AF.Exp,
                  [sm_G], [egs.t])
        nck0 = t0 // CHUNK
        if _G > 6:
            P.dma("sp", kps.ap, d["keep"][:, nck0:nck0 + 16, :], writes=[kps.t])
            P.tt("dve", egs.ap, egs.ap, kps.ap, ALU.mult, [egs.t, kps.t], [egs.t])
            P.dma("pool", d["EG"][:, nck0:nck0 + 16, :], egs.ap, reads=[egs.t], load=False)

        if _STOP <= 7:
            continue
        for (col0, which, dst) in ((O_QM, 0, d["qm"]), (O_KM, 1, d["km"])):
            if which == 0 and not full:
                continue
            for j in range(2):
                wq = load_w(col0 + j * 512, 512)
                for sub in range(8):
                    a = nacc()
                    for kc in range(16):
                        P.mm(a.ap, hT.ap[:, kc, sub * 128:(sub + 1) * 128], wq.ap[:, kc, :], kc == 0, kc == 15,
                             [hT.sub[sub], wq.t], [a.t])
                    for dr in range(2):
                        o = nstg()
                        c0 = which * 16 + dr * 8 + j * 4
                        P.tt("dve", o.ap[:, 0:512].rearrange("p (h d) -> p h d", h=4),
                             a.ap.rearrange("p (h d) -> p h d", h=4),
                             eab.ap[:, sub, c0:c0 + 4].unsqueeze(2).broadcast_to([128, 4, 128]), ALU.mult,
                             [a.t, eab.sub[sub]], [o.t])
                        P.dma("pool", dst[dr, t0 + sub * 128:t0 + (sub + 1) * 128, j * 512:(j + 1) * 512],
                              o.ap[:, 0:512], reads=[o.t], load=False)
        for j in range(2):
            wv = load_w(O_VM + j * 512, 512)
            for sub in range(8):
                a = nacc()
                for kc in range(16):
                    P.mm(a.ap, hT.ap[:, kc, sub * 128:(sub + 1) * 128], wv.ap[:, kc, :], kc == 0, kc == 15,
                         [hT.sub[sub], wv.t], [a.t])
                o = nstg()
                P.act(o.ap[:, 0:512], a.ap, AF.Copy, [a.t], [o.t])
                P.dma("pool", d["vm"][t0 + sub * 128:t0 + (sub + 1) * 128, j * 512:(j + 1) * 512], o.ap[:, 0:512],
                      reads=[o.t], load=False)
        for j in range(2 if full else 0):
            wo = load_w(O_OM + j * 512, 512)
            wz = load_w(O_ZM + j * 512, 512)
            for sub in range(8):
                a = nacc()
                b = nacc()
                for kc in range(16):
                    P.mm(a.ap, hT.ap[:, kc, sub * 128:(sub + 1) * 128], wo.ap[:, kc, :], kc == 0, kc == 15,
                         [hT.sub[sub], wo.t], [a.t])
                for kc in range(16):
                    P.mm(b.ap, hT.ap[:, kc, sub * 128:(sub + 1) * 128], wz.ap[:, kc, :], kc == 0, kc == 15,
                         [hT.sub[sub], wz.t], [b.t])
                u = sA[sub % 2]
                v = sB[sub % 2]
                P.act(u.ap, a.ap, AF.Sigmoid, [a.t], [u.t])
                P.act(v.ap, b.ap, AF.Silu, [b.t], [v.t])
                o = nstg()
                P.tt("dve", o.ap[:, 0:512], u.ap, v.ap, ALU.mult, [u.t, v.t], [o.t])
                P.dma("pool", d["ozm"][t0 + sub * 128:t0 + (sub + 1) * 128, j * 512:(j + 1) * 512], o.ap[:, 0:512],
                      reads=[o.t], load=False)
    st.close()
    P.end_phase()


def phase_att(P, L, s, ones_bf, own):
    st = ExitStack()
    S = s.S
    d = s.d
    NCH = 4
    KC = S // NCH
    nb = KC // 128
    NKB = S // 128
    NQ = own // 512
    kn = [Buf(P, st, "sb", f"kn{c}", [128, KC], BF16) for c in range(NCH)]
    vv = [Buf(P, st, "sb", f"vv{c}", [128, nb, 128], BF16) for c in range(NCH)]
    kr = Buf(P, st, "sb", "kr", [128, S], BF16)
    qn = [Buf(P, st, "sb", f"qn{i}", [128, 512], BF16) for i in range(2)]
    qr = [Buf(P, st, "sb", f"qr{i}", [128, 512], BF16) for i in range(2)]
    za = [Buf(P, st, "sb", f"za{i}", [128, 512], BF16) for i in range(2)]
    pt = [Buf(P, st, "sb", f"pt{i}", [128, 512], BF16) for i in range(8)]
    p2 = [Buf(P, st, "sb", f"p2_{i}", [128, 512], BF16) for i in range(3)]
    s2 = [Buf(P, st, "sb", f"s2_{i}", [128, 512], BF16) for i in range(2)]
    s4 = [Buf(P, st, "sb", f"s4_{i}", [128, 512], BF16) for i in range(2)]
    rl = Buf(P, st, "sb", "rl", [128, 512], F32)
    yo = Buf(P, st, "sb", "yo", [128, 512], F32)
    yb = [Buf(P, st, "sb", f"yb{i}", [128, 512], BF16) for i in range(2)]
    sps = [Buf(P, st, "ps", f"sps{i}", [128, 512], F32) for i in range(3)]
    ops = [Buf(P, st, "ps", f"ops{i}", [128, 512], F32) for i in range(2)]
    lps = [Buf(P, st, "ps", f"lps{i}", [128, 512], F32) for i in range(2)]

    P.memset("dve", kr.ap[64:128, :], 0.0, [kr.t])
    for i in range(2):
        P.memset("dve", qr[i].ap[64:128, :], 0.0, [qr[i].t])
    P.dma("sp", kr.ap[0:64, :], d["krT"][:, :], writes=[kr.t])

    def load_kv(h, c):
        P.dma("sp", kn[c].ap, d["knT"][h, :, c * KC:(c + 1) * KC], writes=[kn[c].t])
        P.dma("sp", vv[c].ap, d["Vh"][h, :, c * nb:(c + 1) * nb, :], writes=[vv[c].t])

    for c in range(NCH):
        load_kv(0, c)

    def load_q(h, qt, i):
        q0 = qt * 512
        P.dma("sp", qn[i].ap, d["qT"][h, 0:128, q0:q0 + 512], writes=[qn[i].t])
        P.dma("sp", qr[i].ap[0:64, :], d["qT"][h, 128:192, q0:q0 + 512], writes=[qr[i].t])
        P.dma("sp", za[i].ap, d["zaT"][h * 128:(h + 1) * 128, q0:q0 + 512], writes=[za[i].t])

    items = [(h, qt, kb) for h in range(A_HEADS) for qt in range(NQ) for kb in range(NKB)]
    LAG = 2
    LAG2 = 2
    NPT = 8
    n = len(items)
    load_q(0, 0, 0)
    for i in range(n + LAG + LAG2):
        if i < n:
            h, qt, kb = items[i]
            qi = (h * NQ + qt)
            if kb == 5:
                nxt = qi + 1
                if nxt < A_HEADS * NQ:
                    load_q(nxt // NQ, nxt % NQ, nxt % 2)
            c, kk = kb // nb, kb % nb
            sp_ = sps[i % 3]
            P.mm(sp_.ap, kn[c].ap[:, kk * 128:(kk + 1) * 128], qn[qi % 2].ap, True, False,
                 [kn[c].t, qn[qi % 2].t], [sp_.t])
            P.mm(sp_.ap, kr.ap[:, kb * 128:(kb + 1) * 128], qr[qi % 2].ap, False, True,
                 [kr.t, qr[qi % 2].t], [sp_.t])
            P.act(pt[i % NPT].ap, sp_.ap, AF.Exp, [sp_.t], [pt[i % NPT].t], scale=ATT_SCALE)
        j = i - LAG
        if 0 <= j < n:
            h, qt, kb = items[j]
            qi = (h * NQ + qt)
            c, kk = kb // nb, kb % nb
            o_ps = ops[qi % 2]
            p_ = pt[j % NPT]
            P.mm(o_ps.ap, vv[c].ap[:, kk, :], p_.ap, kb == 0, kb == NKB - 1, [vv[c].t, p_.t], [o_ps.t])
            if kb % 2 == 1:
                pm = pt[(j - 1) % NPT]
                ps_ = s2[(kb // 2) % 2]
                P.tt("dve", ps_.ap, pm.ap, p_.ap, ALU.add, [pm.t, p_.t], [ps_.t])
                if kb % 4 == 3:
                    s4_ = s4[(kb // 4) % 2]
                    P.tt("dve", s4_.ap, s2[0].ap, s2[1].ap, ALU.add, [s2[0].t, s2[1].t], [s4_.t])
                    if kb % 8 == 7:
                        pp = p2[(j // 8) % 3]
                        P.tt("dve", pp.ap, s4[0].ap, s4[1].ap, ALU.add, [s4[0].t, s4[1].t], [pp.t])
            if qt == NQ - 1 and h < A_HEADS - 1 and kk == nb - 1:
                load_kv(h + 1, c)
        j = i - LAG - LAG2
        if 0 <= j < n:
            h, qt, kb = items[j]
            qi = (h * NQ + qt)
            if kb % 8 == 7:
                l_ps = lps[qi % 2]
                pp = p2[(j // 8) % 3]
                P.mm(l_ps.ap, ones_bf.ap, pp.ap, kb == 7, kb == NKB - 1, [ones_bf.t, pp.t], [l_ps.t])
            if kb == NKB - 1:
                o_ps = ops[qi % 2]
                l_ps = lps[qi % 2]
                q0 = qt * 512
                P.recip(rl.ap, l_ps.ap, [l_ps.t], [rl.t])
                P.tt("dve", yo.ap, o_ps.ap, rl.ap, ALU.mult, [o_ps.t, rl.t], [yo.t])
                y_ = yb[qi % 2]
                P.tt("dve", y_.ap, yo.ap, za[qi % 2].ap, ALU.mult, [yo.t, za[qi % 2].t], [y_.t])
                P.dma("pool", d["yaT"][h * 128:(h + 1) * 128, q0:q0 + 512], y_.ap, reads=[y_.t], load=False)
    st.close()
    P.end_phase()


def phase_mlstm(P, L, s, dr, mhn_b, ident, mask_bf, plan):
    st = ExitStack()
    S = s.S
    d = s.d
    NCK = S // CHUNK
    NB = 3
    qtm = [Buf(P, st, "sb", f"qtm{i}", [128, 1024], BF16) for i in range(NB)]
    ktm = [Buf(P, st, "sb", f"ktm{i}", [128, 1024], BF16) for i in range(NB)]
    vau = [Buf(P, st, "sb", f"vau{i}", [128, 8, 129], BF16) for i in range(NB)]
    egt = [Buf(P, st, "sb", f"egt{i}", [128, 3, 16], F32) for i in range(NB)]
    qT_ = [Buf(P, st, "sb", f"qT{i}", [128, 8, 128], BF16) for i in range(2)]
    kT_ = [Buf(P, st, "sb", f"kT{i}", [128, 8, 128], BF16) for i in range(2)]
    pT = [Buf(P, st, "sb", f"pT{i}", [128, 4, 128], BF16) for i in range(2)]
    Tst = Buf(P, st, "sb", "Tst", [128, 8, 129], F32, nsub=8)
    Cb = [Buf(P, st, "sb", f"Cb{i}", [128, 8, 129], BF16, nsub=8) for i in range(3)]
    hout = [Buf(P, st, "sb", f"hout{i}", [128, 1024], F32) for i in range(2)]
    dn = Buf(P, st, "sb", "dn", [128, 8], F32)
    nd = Buf(P, st, "sb", "nd", [128, 8], F32)
    rd = Buf(P, st, "sb", "rd", [128, 8], F32)
    sT = [Buf(P, st, "ps", f"sT{i}", [128, 512], F32) for i in range(2)]
    num = [Buf(P, st, "ps", f"num{i}", [128, 3, 129], F32) for i in range(3)]
    dCb = [Buf(P, st, "ps", f"dC{i}", [128, 3, 129], F32) for i in range(2)]
    tpm = Buf(P, st, "ps", "tpm", [128, 1024], BF16)
    if dr == 1:
        hfw = [Buf(P, st, "sb", f"hfw{i}", [128, 1024], F32) for i in range(NB)]
        ozm = [Buf(P, st, "sb", f"ozm{i}", [128, 1024], BF16) for i in range(NB)]
        mhn = Buf(P, st, "sb", "mhn", [128, 1024], F32)
        hsq = Buf(P, st, "sb", "hsq", [128, 1024], F32)
        ssm = Buf(P, st, "sb", "ssm", [128, 8], F32)
        rsm = Buf(P, st, "sb", "rsm", [128, 8], F32)
        ybt = Buf(P, st, "sb", "ybt", [128, 1024], BF16)
        ybst = [Buf(P, st, "sb", f"ybst{i}", [128, 8, 512], BF16) for i in range(2)]
        P.dma("sp", mhn.ap, mhn_b[L], writes=[mhn.t])

    rows = [(0, 64), (64, 128)] if dr == 0 else [(64, 128), (0, 64)]
    eidx = [(1, 0), (2, 1)] if dr == 0 else [(1, 2), (0, 1)]

    for i in range(NB):
        P.memset("dve", vau[i].ap[:, :, 128:129], 1.0, [vau[i].t])
    for i in range(3):
        P.memset("dve", Cb[i].ap, 0.0, Cb[i].sub)
    P.memset("dve", Tst.ap, 0.0, Tst.sub)
    cbi = [0]

    def load(ti, b, real):
        r0 = ti * 128
        if real:
            P.dma("sp", qtm[b].ap, d["qm"][dr, r0:r0 + 128, :], writes=[qtm[b].t])
        P.dma("sp", ktm[b].ap, d["km"][dr, r0:r0 + 128, :], writes=[ktm[b].t])
        P.dma("sp", vau[b].ap[:, :, 0:128], d["vm"][r0:r0 + 128, :].rearrange("p (h d) -> p h d", h=8),
              writes=[vau[b].t])
        if dr == 0:
            if ti > 0:
                P.dma("sp", egt[b].ap, d["EG"][:, 2 * ti - 1:2 * ti + 2, :], writes=[egt[b].t])
            else:
                P.dma("sp", egt[b].ap[:, 1:3, :], d["EG"][:, 0:2, :], writes=[egt[b].t])
                P.dma("sp", egt[b].ap[:, 0, :], d["EG"][:, NCK - 1, :], writes=[egt[b].t])
        else:
            if 2 * ti + 3 <= NCK:
                P.dma("sp", egt[b].ap, d["EG"][:, 2 * ti:2 * ti + 3, :], writes=[egt[b].t])
            else:
                P.dma("sp", egt[b].ap[:, 0:2, :], d["EG"][:, 2 * ti:2 * ti + 2, :], writes=[egt[b].t])
                P.dma("sp", egt[b].ap[:, 2, :], d["EG"][:, 0, :], writes=[egt[b].t])
        if real and dr == 1:
            P.dma("sp", hfw[b].ap, d["hfw"][r0:r0 + 128, :], writes=[hfw[b].t])
            P.dma("sp", ozm[b].ap, d["ozm"][r0:r0 + 128, :], writes=[ozm[b].t])

    dcn = [0]

    def chunk_updates(b, ci, need_cb=True):
        lo, hi = rows[ci]
        own, prev = eidx[ci]
        for grp in ((0, 1, 2), (3, 4, 5), (6, 7)):
            dc = dCb[dcn[0] % 2]
            dcn[0] += 1
            for k, h in enumerate(grp):
                P.mm(dc.ap[:, k, :], ktm[b].ap[lo:hi, h * 128:(h + 1) * 128], vau[b].ap[lo:hi, h, :], True, True,
                     [ktm[b].t, vau[b].t], [dc.t])
            for k, h in enumerate(grp):
                col = dr * 8 + h
                P.stt("dve", Tst.ap[:, h, :], Tst.ap[:, h, :], egt[b].ap[:, prev, col:col + 1], dc.ap[:, k, :],
                      ALU.mult, ALU.add, [Tst.sub[h], egt[b].t, dc.t], [Tst.sub[h]])
            if not need_cb:
                continue
            nxt = Cb[(cbi[0] + ci + 1) % 3]
            for k, h in enumerate(grp):
                col = dr * 8 + h
                P.act(nxt.ap[:, h, :], Tst.ap[:, h, :], AF.Copy, [Tst.sub[h], egt[b].t], [nxt.sub[h]],
                      scale=egt[b].ap[:, own, col:col + 1])

    seq = [(ti, real) for (tiles, real) in plan for ti in tiles]
    nseq = len(seq)
    for k_ in range(min(NB - 1, nseq)):
        load(seq[k_][0], k_ % NB, seq[k_][1])
    for n_, (ti, real) in enumerate(seq):
        b = n_ % NB
        b2 = n_ % 2
        if n_ + NB - 1 < nseq:
            load(seq[n_ + NB - 1][0], (n_ + NB - 1) % NB, seq[n_ + NB - 1][1])
        if not real:
            nxt_real = (n_ + 1 < nseq) and seq[n_ + 1][1]
            for ci in range(2):
                chunk_updates(b, ci, need_cb=(nxt_real and ci == 1))
            cbi[0] = (cbi[0] + 2) % 3
            continue
        for (src, dst) in ((qtm[b], qT_[b2]), (ktm[b], kT_[b2])):
            for h in range(8):
                P.tr(tpm.ap[:, h * 128:(h + 1) * 128], src.ap[:, h * 128:(h + 1) * 128], ident.ap,
                     [src.t, ident.t], [tpm.t])
            P.copy("act", dst.ap.rearrange("p h t -> p (h t)"), tpm.ap, [tpm.t], [dst.t])
        chunk_updates(b, 0)
        chunk_updates(b, 1)
        for g in range(2):
            for hh in range(4):
                h = g * 4 + hh
                P.mm(sT[g].ap[:, hh * 128:(hh + 1) * 128], kT_[b2].ap[:, h, :], qT_[b2].ap[:, h, :], True, True,
                     [kT_[b2].t, qT_[b2].t], [sT[g].t])
            P.tt("dve", pT[g].ap.rearrange("p h t -> p (h t)"), sT[g].ap,
                 mask_bf.ap[:, dr].rearrange("p c t -> p (c t)"), ALU.mult, [sT[g].t, mask_bf.t], [pT[g].t])
        for h in range(8):
            nbk = num[h // 3]
            o_ = nbk.ap[:, h % 3, :]
            P.mm(o_, pT[h // 4].ap[:, h % 4, :], vau[b].ap[:, h, :], True, False, [pT[h // 4].t, vau[b].t], [nbk.t])
            for ci in range(2):
                lo, hi = rows[ci]
                cbuf = Cb[(cbi[0] + ci) % 3]
                P.mm(nbk.ap[lo:hi, h % 3, :], qT_[b2].ap[:, h, lo:hi], cbuf.ap[:, h, :], False, True,
                     [qT_[b2].t, cbuf.sub[h]], [nbk.t])
        cbi[0] = (cbi[0] + 2) % 3
        ho = hout[b2]
        for k in range(3):
            nh = 3 if k < 2 else 2
            P.ts("dve", nd.ap[:, 3 * k:3 * k + nh], num[k].ap[:, 0:nh, 128], -1.0, ALU.mult, [num[k].t], [nd.t])
            P.stt("dve", dn.ap[:, 3 * k:3 * k + nh], num[k].ap[:, 0:nh, 128], 1.0, nd.ap[:, 3 * k:3 * k + nh],
                  ALU.max, ALU.max, [num[k].t, nd.t], [dn.t])
            P.recip(rd.ap[:, 3 * k:3 * k + nh], dn.ap[:, 3 * k:3 * k + nh], [dn.t], [rd.t])
            P.tt("dve", ho.ap[:, 384 * k:384 * k + 128 * nh].rearrange("p (h d) -> p h d", h=nh),
                 num[k].ap[:, 0:nh, 0:128], rd.ap[:, 3 * k:3 * k + nh].unsqueeze(2).broadcast_to([128, nh, 128]),
                 ALU.mult, [num[k].t, rd.t], [ho.t])
        r0 = ti * 128
        if dr == 0:
            P.dma("pool", d["hfw"][r0:r0 + 128, :], ho.ap, reads=[ho.t], load=False)
        else:
            P.tt("dve", ho.ap, ho.ap, hfw[b].ap, ALU.add, [ho.t, hfw[b].t], [ho.t])
            P.tt("dve", hsq.ap, ho.ap, ho.ap, ALU.mult, [ho.t], [hsq.t])
            P.op("dve", lambda e: e.tensor_reduce(out=ssm.ap, in_=hsq.ap.rearrange("p (h d) -> p h d", h=8),
                                                  axis=AX.X, op=ALU.add), [hsq.t], [ssm.t])
            P.act(rsm.ap, ssm.ap, AF.Ln, [ssm.t], [rsm.t], scale=1.0 / 128, bias=EPS)
            P.act(rsm.ap, rsm.ap, AF.Exp, [rsm.t], [rsm.t], scale=-0.5)
            P.tt("dve", ho.ap.rearrange("p (h d) -> p h d", h=8), ho.ap.rearrange("p (h d) -> p h d", h=8),
                 rsm.ap.unsqueeze(2).broadcast_to([128, 8, 128]), ALU.mult, [ho.t, rsm.t], [ho.t])
            P.tt("pool", ho.ap, ho.ap, mhn.ap, ALU.mult, [ho.t, mhn.t], [ho.t])
            P.tt("pool", ybt.ap, ho.ap, ozm[b].ap, ALU.mult, [ho.t, ozm[b].t], [ybt.t])
            slot = ti % 4
            sb_ = ybst[(ti // 4) % 2]
            for h in range(8):
                P.tr(tpm.ap[:, h * 128:(h + 1) * 128], ybt.ap[:, h * 128:(h + 1) * 128], ident.ap,
                     [ybt.t, ident.t], [tpm.t])
            P.copy("act", sb_.ap[:, :, slot * 128:(slot + 1) * 128], tpm.ap.rearrange("p (h t) -> p h t", h=8),
                   [tpm.t], [sb_.t])
            if slot == 0:
                t0 = ti * 128
                P.dma("pool", d["ybT"].rearrange("(c p) t -> p c t", p=128)[:, :, t0:t0 + 512], sb_.ap,
                      reads=[sb_.t], load=False)
    st.close()
    P.end_phase()


def phase_out(P, L, s, xin, xout, w_oa, w_ob, w_out, normf_b, own):
    st = ExitStack()
    S = s.S
    d = s.d
    TT = 512
    wa = Buf(P, st, "sb", "wa", [128, 8, D_MODEL], BF16, nsub=4)
    wbb = Buf(P, st, "sb", "wbb", [128, 8, D_MODEL], BF16, nsub=4)
    wo = Buf(P, st, "sb", "wo", [128, 16, D_MODEL], BF16, nsub=4)
    ya = Buf(P, st, "sb", "ya", [128, 8, TT], BF16)
    yb = Buf(P, st, "sb", "ybb", [128, 8, TT], BF16)
    mT = Buf(P, st, "sb", "mT", [128, 16, TT], BF16, nsub=16)
    sga = [Buf(P, st, "sb", f"sga{i}", [128, TT], BF16) for i in range(2)]
    sgb = [Buf(P, st, "sb", f"sgb{i}", [128, TT], BF16) for i in range(2)]
    t1_ = Buf(P, st, "sb", "t1_", [128, TT], F32)
    t2_ = Buf(P, st, "sb", "t2_", [128, TT], F32)
    t1 = [t1_, t1_]
    t2 = [t2_, t2_]
    xb = [Buf(P, st, "sb", f"xb{i}", [128, 512], F32) for i in range(2)]
    xo = [Buf(P, st, "sb", f"xo{i}", [128, D_MODEL], F32) for i in range(2)]
    pa = [Buf(P, st, "ps", f"pa{i}", [128, 512], F32) for i in range(2)]
    pb = [Buf(P, st, "ps", f"pb{i}", [128, 512], F32) for i in range(2)]
    po = [Buf(P, st, "ps", f"po{i}", [128, 512], F32) for i in range(3)]
    if normf_b is not None:
        nf = Buf(P, st, "sb", "nf", [128, D_MODEL], F32)
        junk = Buf(P, st, "sb", "junk2", [128, D_MODEL], BF16)
        ss = Buf(P, st, "sb", "ss2", [128, 4], F32)
        rs = Buf(P, st, "sb", "rs2", [128, 4], F32)
        P.dma("sp", nf.ap, normf_b[:, :], writes=[nf.t])
    cnt = {"sg": 0, "x": 0, "po": 0, "xo": 0}
    woaL = w_oa[L].rearrange("(k p) n -> p k n", p=128)
    wobL = w_ob[L].rearrange("(k p) n -> p k n", p=128)
    woutL = w_out[L].rearrange("(k p) n -> p k n", p=128)
    for c4 in range(4):
        P.dma("sp", wa.ap[:, :, c4 * 512:(c4 + 1) * 512], woaL[:, :, c4 * 512:(c4 + 1) * 512], writes=[wa.sub[c4]])
        P.dma("sp", wbb.ap[:, :, c4 * 512:(c4 + 1) * 512], wobL[:, :, c4 * 512:(c4 + 1) * 512], writes=[wbb.sub[c4]])
    for c4 in range(4):
        P.dma("sp", wo.ap[:, :, c4 * 512:(c4 + 1) * 512], woutL[:, :, c4 * 512:(c4 + 1) * 512], writes=[wo.sub[c4]])
    for tt in range(own // TT):
        t0 = tt * TT
        P.dma("sp", ya.ap, d["yaT"].rearrange("(c p) t -> p c t", p=128)[:, :, t0:t0 + TT], writes=[ya.t])
        P.dma("sp", yb.ap, d["ybT"].rearrange("(c p) t -> p c t", p=128)[:, :, t0:t0 + TT], writes=[yb.t])
        for fc in range(16):
            f4 = fc // 4
            si = cnt["sg"] % 2
            cnt["sg"] += 1
            P.dma("sp", sga[si].ap, d["sgaT"][fc * 128:(fc + 1) * 128, t0:t0 + TT], writes=[sga[si].t])
            P.dma("sp", sgb[si].ap, d["sgbT"][fc * 128:(fc + 1) * 128, t0:t0 + TT], writes=[sgb[si].t])
            a = pa[si]
            b = pb[si]
            for kc in range(8):
                P.mm(a.ap, wa.ap[:, kc, fc * 128:(fc + 1) * 128], ya.ap[:, kc, :], kc == 0, kc == 7,
                     [wa.sub[f4], ya.t], [a.t])
            for kc in range(8):
                P.mm(b.ap, wbb.ap[:, kc, fc * 128:(fc + 1) * 128], yb.ap[:, kc, :], kc == 0, kc == 7,
                     [wbb.sub[f4], yb.t], [b.t])
            P.tt("dve", t1[si].ap, a.ap, sga[si].ap, ALU.mult, [a.t, sga[si].t], [t1[si].t])
            P.tt("dve", t2[si].ap, b.ap, sgb[si].ap, ALU.mult, [b.t, sgb[si].t], [t2[si].t])
            P.tt("dve", mT.ap[:, fc, :], t1[si].ap, t2[si].ap, ALU.add, [t1[si].t, t2[si].t], [mT.sub[fc]])
        for sub in range(4):
            r0 = t0 + sub * 128
            xo_ = xo[cnt["xo"] % 2]
            cnt["xo"] += 1
            for cb in range(4):
                xi = cnt["x"] % 2
                cnt["x"] += 1
                P.dma("sp", xb[xi].ap, xin[r0:r0 + 128, cb * 512:(cb + 1) * 512], writes=[xb[xi].t])
                o = po[cnt["po"] % 3]
                cnt["po"] += 1
                for kc in range(16):
                    P.mm(o.ap, mT.ap[:, kc, sub * 128:(sub + 1) * 128], wo.ap[:, kc, cb * 512:(cb + 1) * 512],
                         kc == 0, kc == 15, [mT.sub[kc], wo.sub[cb]], [o.t])
                P.tt("dve", xo_.ap[:, cb * 512:(cb + 1) * 512], o.ap, xb[xi].ap, ALU.add, [o.t, xb[xi].t], [xo_.t])
            if normf_b is None:
                P.dma("pool", xout[r0:r0 + 128, :], xo_.ap, reads=[xo_.t], load=False)
            else:
                P.memset("dve", ss.ap[:, sub:sub + 1], 0.0, [ss.t])
                P.act(junk.ap, xo_.ap, AF.Square, [xo_.t, ss.t], [junk.t, ss.t], accum_out=ss.ap[:, sub:sub + 1])
                P.act(rs.ap[:, sub:sub + 1], ss.ap[:, sub:sub + 1], AF.Ln, [ss.t], [rs.t], scale=1.0 / D_MODEL,
                      bias=EPS)
                P.act(rs.ap[:, sub:sub + 1], rs.ap[:, sub:sub + 1], AF.Exp, [rs.t], [rs.t], scale=-0.5)
                P.stt("dve", xo_.ap, xo_.ap, rs.ap[:, sub:sub + 1], nf.ap, ALU.mult, ALU.mult,
                      [xo_.t, rs.t, nf.t], [xo_.t])
                P.dma("pool", xout[r0:r0 + 128, :], xo_.ap, reads=[xo_.t], load=False)
    st.close()
    P.end_phase()


def make_consts(smax):
    idx = np.arange(128)
    same = (idx[:, None] // CHUNK) == (idx[None, :] // CHUNK)
    ident = np.eye(128, dtype=np.float32)
    u_fw = (same & (idx[:, None] <= idx[None, :])).astype(np.float32)
    u_bw = (same & (idx[:, None] >= idx[None, :])).astype(np.float32)
    ones = np.ones((128, 128), np.float32)
    b0 = np.zeros((128, 128), np.float32)
    b0[0:64, :] = 1.0
    b1 = np.zeros((128, 128), np.float32)
    b1[64:128, :] = 1.0
    consts = np.concatenate([ident, u_fw, u_bw, ones, b0, b1], axis=1)
    inv = (np.float32(ROPE_THETA) ** (-np.arange(0, QK_ROPE, 2, dtype=np.float32) / np.float32(QK_ROPE))).astype(np.float32)
    ang = (np.arange(smax, dtype=np.float32)[:, None] * inv[None, :]).astype(np.float32)
    cos = np.cos(ang).astype(np.float32).T
    sin = np.sin(ang).astype(np.float32).T
    cos64 = np.concatenate([cos, cos], axis=0)
    sin64 = np.concatenate([-sin, sin], axis=0)
    rope = np.stack([np.concatenate([cos64, cos64], 0), np.concatenate([sin64, sin64], 0)], axis=1)
    return np.ascontiguousarray(consts), np.ascontiguousarray(rope.astype(np.float32))


def prep_weights(w_in, w_uq, w_ukv, norm_in, q_a_norm, kv_a_norm, b_gates, m_head_norm, norm_f):
    depth = w_in.shape[0]
    kr = w_in[:, :, O_KR:O_KR + 64]
    kr_sw = np.concatenate([kr[:, :, 32:64], kr[:, :, 0:32]], axis=2)
    w_in_ext = np.ascontiguousarray(np.concatenate([w_in, kr_sw], axis=2))
    uq = w_uq.reshape(depth, Q_LORA, A_HEADS, 192)
    nope = uq[..., :128].reshape(depth, Q_LORA, 1024)
    rope = uq[..., 128:]
    rope_sw = np.concatenate([rope[..., 32:], rope[..., :32]], axis=-1)
    w_uq_p = np.ascontiguousarray(np.concatenate([nope, rope.reshape(depth, Q_LORA, 512),
                                                  rope_sw.reshape(depth, Q_LORA, 512)], axis=2))
    ukv = w_ukv.reshape(depth, KV_LORA, A_HEADS, 256)
    w_ukv_p = np.ascontiguousarray(np.concatenate([ukv[..., :128].reshape(depth, KV_LORA, 1024),
                                                   ukv[..., 128:].reshape(depth, KV_LORA, 1024)], axis=2))
    out = {
        "w_in_ext": w_in_ext, "w_uq_p": w_uq_p, "w_ukv_p": w_ukv_p,
        "norm_in_t": np.ascontiguousarray(norm_in.reshape(depth, 16, 128).transpose(0, 2, 1)),
        "q_a_norm_t": np.ascontiguousarray(q_a_norm.reshape(depth, 4, 128).transpose(0, 2, 1)),
        "kv_a_norm_t": np.ascontiguousarray(kv_a_norm.reshape(depth, 4, 128).transpose(0, 2, 1)),
        "b_gates_b": np.ascontiguousarray(np.broadcast_to(b_gates[:, None, :], (depth, 128, 32))),
        "m_head_norm_b": np.ascontiguousarray(np.broadcast_to(m_head_norm[:, None, :], (depth, 128, 1024))),
        "norm_f_b": np.ascontiguousarray(np.broadcast_to(norm_f[None, :], (128, D_MODEL))),
    }
    return out


_CACHE = {}


def make_keep(S, reset_fw_chunk, reset_bw_chunk):
    nck = S // CHUNK
    keep = np.ones((nck, 16), np.float32)
    keep[(reset_fw_chunk - 1) % nck, 0:8] = 0.0
    keep[(reset_bw_chunk + 1) % nck, 8:16] = 0.0
    return np.ascontiguousarray(np.broadcast_to(keep[None], (128, nck, 16)))


def make_rope(positions):
    inv = (np.float32(ROPE_THETA) ** (-np.arange(0, QK_ROPE, 2, dtype=np.float32) / np.float32(QK_ROPE))).astype(np.float32)
    ang = (positions.astype(np.float32)[:, None] * inv[None, :]).astype(np.float32)
    cos = np.cos(ang).astype(np.float32).T
    sin = np.sin(ang).astype(np.float32).T
    cos64 = np.concatenate([cos, cos], axis=0)
    sin64 = np.concatenate([-sin, sin], axis=0)
    rope = np.stack([np.concatenate([cos64, cos64], 0), np.concatenate([sin64, sin64], 0)], axis=1)
    return np.ascontiguousarray(rope.astype(np.float32))


def core_inputs(c, x_prompt, x_sample, nq=4):
    B, S, _ = x_prompt.shape
    DB, DS, _ = x_sample.shape
    p, r = c % B, (c // B) % nq
    Q = S // nq
    m = {}
    m["x_s"] = x_sample[c % DB]
    m["x_p"] = np.ascontiguousarray(np.roll(x_prompt[p], -r * Q, axis=0))
    m["rope_s"] = make_rope(np.arange(DS))
    m["rope_p"] = make_rope((np.arange(S) + r * Q) % S)
    m["keep_s"] = make_keep(DS, 0, DS // CHUNK - 1)
    j0 = (nq - r) % nq
    je = (nq - 1 - r) % nq
    m["keep_p"] = make_keep(S, j0 * (Q // CHUNK), (je + 1) * (Q // CHUNK) - 1)
    return m


def kernel(x_prompt, x_sample, norm_in, w_in, b_gates, q_a_norm, w_uq, kv_a_norm, w_ukv,
           w_oa, m_head_norm, w_ob, w_out, norm_f):
    f = lambda a: np.ascontiguousarray(np.asarray(a, dtype=np.float32))
    x_prompt, x_sample = f(x_prompt), f(x_sample)
    B, S, _ = x_prompt.shape
    DB, DS, _ = x_sample.shape
    n = 8
    nq = n // B
    Q = S // nq
    key = (S, DS)
    if key not in _CACHE:
        _CACHE[key] = build([("s", DS), ("p", S)], prune={"p": Q})
    nc, _ = _CACHE[key]
    common = prep_weights(f(w_in), f(w_uq), f(w_ukv), f(norm_in), f(q_a_norm), f(kv_a_norm), f(b_gates),
                          f(m_head_norm), f(norm_f))
    common["w_oa"] = f(w_oa)
    common["w_ob"] = f(w_ob)
    common["w_out"] = f(w_out)
    common["consts"] = make_consts(128)[0]
    in_maps = []
    for c in range(n):
        m = dict(common)
        m.update(core_inputs(c, x_prompt, x_sample, nq))
        in_maps.append(m)
    res = run_bass_kernel_spmd(nc, in_maps, core_ids=list(range(n)))
    y_prompt = np.empty((B, S, D_MODEL), np.float32)
    for c in range(n):
        p, r = c % B, (c // B) % nq
        y_prompt[p, r * Q:(r + 1) * Q] = res.results[c]["y_p"]
    y_sample = np.stack([res.results[c]["y_s"] for c in range(DB)], axis=0)
    return (y_prompt, y_sample.astype(np.float32))
```

```python
import numpy as np
from contextlib import ExitStack
import ml_dtypes

import concourse.bass as bass
import concourse.mybir as mybir
from concourse.bass_utils import run_bass_kernel_spmd

F32 = mybir.dt.float32
BF16 = mybir.dt.bfloat16
AF = mybir.ActivationFunctionType
ALU = mybir.AluOpType
AX = mybir.AxisListType

D_MODEL = 2048
DEPTH = 2
A_HEADS = 8
Q_LORA = 512
KV_LORA = 512
QK_NOPE = 128
QK_ROPE = 64
V_HEAD = 128
ROPE_THETA = 10000.0
M_HEADS = 8
CHUNK = 64
EPS = 1e-6
IN_COLS = 11360
IN_EXT = IN_COLS + 64
O_CQ, O_CKV, O_KR, O_ZA = 0, 512, 1024, 1088
O_QM, O_KM, O_VM, O_OM, O_ZM, O_GT, O_GA, O_GB = 2112, 3136, 4160, 5184, 6208, 7232, 7264, 9312
O_KRS = IN_COLS
ATT_SCALE = float((QK_NOPE + QK_ROPE) ** -0.5)
MQ_SCALE = float(128 ** -0.5)

COMPUTE = ("pe", "act", "dve", "pool")
ENGS = ("pe", "act", "dve", "pool", "sp")


class Tile:
    __slots__ = ("name", "w", "r", "lsem", "ssem")

    def __init__(self, name):
        self.name = name
        self.w = {}
        self.r = {}
        self.lsem = None
        self.ssem = None


class Op:
    __slots__ = ("eng", "fn", "waits", "sig", "idx", "val", "dma")

    def __init__(self, eng, fn):
        self.eng = eng
        self.fn = fn
        self.waits = []
        self.sig = False
        self.idx = -1
        self.val = -1
        self.dma = None


class Prog:
    def __init__(self, nc):
        self.nc = nc
        self.q = {e: [] for e in ENGS}
        self.esem = {e: nc.alloc_semaphore(f"cnt_{e}") for e in COMPUTE}
        self.bar = nc.alloc_semaphore("bar")
        self.nbar = 0
        self.count = {e: 0 for e in COMPUTE}
        self.nops = {e: 0 for e in ENGS}
        self.waited = {e: {} for e in ENGS}
        self.dma_out = {e: {} for e in ENGS}
        self.tiles = []
        self.free_sems = {}
        self.semq = {}
        self.semcnt = {}
        self.semobj = {}
        self.n_inst = 0

    def tile(self, name):
        t = Tile(name)
        self.tiles.append(t)
        return t

    def _get_sem(self, queue):
        fl = self.free_sems.setdefault(queue, [])
        if fl:
            return fl.pop()
        s = self.nc.alloc_semaphore(f"dsem{len(self.semobj)}")
        self.semobj[id(s)] = s
        self.semcnt[id(s)] = 0
        self.semq[id(s)] = queue
        return s

    @staticmethod
    def _newer(a, b):
        if isinstance(a, Op):
            return a.idx > b.idx
        return a[1] > b[1]

    def _collect(self, reads, writes):
        need = {}
        for t in reads:
            for k, v in t.w.items():
                c = need.get(k)
                if c is None or self._newer(v, c):
                    need[k] = v
        for t in writes:
            for d in (t.w, t.r):
                for k, v in d.items():
                    c = need.get(k)
                    if c is None or self._newer(v, c):
                        need[k] = v
        return need

    def _filter(self, eng, need):
        waits = []
        wd = self.waited[eng]
        for k, v in need.items():
            if isinstance(v, Op):
                if k == eng and eng == "pe":
                    continue
                if wd.get(k, -1) >= v.idx:
                    continue
                wd[k] = v.idx
                v.sig = True
                waits.append(v)
            else:
                if wd.get(k, 0) >= v[1]:
                    continue
                wd[k] = v[1]
                waits.append(v)
        return waits

    def op(self, eng, fn, reads=(), writes=()):
        o = Op(eng, fn)
        o.idx = self.nops[eng]
        self.nops[eng] += 1
        o.waits = self._filter(eng, self._collect(reads, writes))
        for t in reads:
            t.r[eng] = o
        for t in writes:
            t.w = {eng: o}
            t.r = {}
        self.q[eng].append(o)
        return o

    def dma(self, queue, out, in_, reads=(), writes=(), load=True):
        o = Op(queue, lambda e: e.dma_start(out=out, in_=in_))
        o.idx = self.nops[queue]
        self.nops[queue] += 1
        o.waits = self._filter(queue, self._collect(reads, writes))
        if load:
            t = writes[0]
            if t.lsem is None:
                t.lsem = self._get_sem(queue)
            assert self.semq[id(t.lsem)] == queue
            sem = t.lsem
        else:
            t = reads[0]
            if t.ssem is None:
                t.ssem = self._get_sem(queue)
            assert self.semq[id(t.ssem)] == queue
            sem = t.ssem
        self.semcnt[id(sem)] += 16
        dep = (sem, self.semcnt[id(sem)])
        o.dma = dep
        key = ("d", id(sem))
        for t in reads:
            t.r[key] = dep
        for t in writes:
            t.w = {key: dep}
            t.r = {}
        self.dma_out[queue][id(sem)] = dep
        self.q[queue].append(o)
        return o

    def dma_raw(self, queue, out, in_, sem):
        o = Op(queue, lambda e: e.dma_start(out=out, in_=in_))
        o.idx = self.nops[queue]
        self.nops[queue] += 1
        self.semcnt[id(sem)] += 16
        o.dma = (sem, self.semcnt[id(sem)])
        self.dma_out[queue][id(sem)] = o.dma
        self.q[queue].append(o)

    def end_phase(self):
        nc = self.nc
        self.nbar += 1
        for e in COMPUTE:
            ql = self.q[e]
            for o in reversed(ql):
                if o.dma is None:
                    o.sig = True
                    break
        for e in COMPUTE:
            c = self.count[e]
            for o in self.q[e]:
                if o.dma is None and o.sig:
                    c += 1
                    o.val = c
            self.count[e] = c
        names = {"sp": "sync", "act": "scalar", "dve": "vector", "pe": "tensor", "pool": "gpsimd"}
        with nc.Block() as block:
            for e in ENGS:
                deco = getattr(block, names[e])

                def body(eng, e=e):
                    self._emit_engine(e, eng)

                deco(body)
        for e in ENGS:
            self.n_inst += len(self.q[e])
            self.q[e] = []
            self.dma_out[e] = {}
        for t in self.tiles:
            if t.lsem is not None:
                self.free_sems[self.semq[id(t.lsem)]].append(t.lsem)
            if t.ssem is not None:
                self.free_sems[self.semq[id(t.ssem)]].append(t.ssem)
        self.tiles = []

    def _emit_engine(self, e, eng):
        esem = self.esem
        for o in self.q[e]:
            for w in o.waits:
                if isinstance(w, Op):
                    eng.wait_ge(esem[w.eng], w.val)
                else:
                    eng.wait_ge(w[0], w[1])
            inst = o.fn(eng)
            if o.dma is not None:
                inst.then_inc(o.dma[0], 16)
            elif o.sig:
                inst.then_inc(esem[e], 1)
        if e in COMPUTE and self.count[e] > 0:
            eng.wait_ge(esem[e], self.count[e])
        for sem, val in self.dma_out[e].values():
            eng.wait_ge(sem, val)
        eng.sem_inc(self.bar, 1)
        eng.wait_ge(self.bar, len(ENGS) * self.nbar)

    def mm(self, out, lhsT, rhs, start, stop, reads, writes):
        return self.op("pe", lambda e: e.matmul(out, lhsT=lhsT, rhs=rhs, start=start, stop=stop), reads, writes)

    def tr(self, out, in_, ident, reads, writes):
        return self.op("pe", lambda e: e.transpose(out, in_, ident), reads, writes)

    def act(self, out, in_, func, reads, writes, **kw):
        return self.op("act", lambda e: e.activation(out=out, in_=in_, func=func, **kw), reads, writes)

    def tt(self, eng, out, in0, in1, op, reads, writes):
        return self.op(eng, lambda e: e.tensor_tensor(out=out, in0=in0, in1=in1, op=op), reads, writes)

    def ts(self, eng, out, in0, s1, op0, reads, writes, s2=None, op1=None):
        if op1 is None:
            return self.op(eng, lambda e: e.tensor_scalar(out=out, in0=in0, scalar1=s1, scalar2=None, op0=op0),
                           reads, writes)
        return self.op(eng, lambda e: e.tensor_scalar(out=out, in0=in0, scalar1=s1, scalar2=s2, op0=op0, op1=op1),
                       reads, writes)

    def stt(self, eng, out, in0, scalar, in1, op0, op1, reads, writes):
        return self.op(eng, lambda e: e.scalar_tensor_tensor(out=out, in0=in0, scalar=scalar, in1=in1,
                                                             op0=op0, op1=op1), reads, writes)

    def copy(self, eng, out, in_, reads, writes):
        if eng == "act":
            return self.act(out, in_, AF.Copy, reads, writes)
        return self.op(eng, lambda e: e.tensor_copy(out=out, in_=in_), reads, writes)

    def memset(self, eng, ap, val, writes):
        return self.op(eng, lambda e: e.memset(ap, val), (), writes)

    def recip(self, out, in_, reads, writes):
        return self.op("dve", lambda e: e.reciprocal(out=out, in_=in_), reads, writes)


class Buf:
    _uid = [0]

    def __init__(self, P, st, kind, name, shape, dtype, nsub=1):
        nc = P.nc
        Buf._uid[0] += 1
        name = f"{name}_u{Buf._uid[0]}"
        if kind == "sb":
            h = st.enter_context(nc.sbuf_tensor(name, list(shape), dtype))
        else:
            h = st.enter_context(nc.psum_tensor(name, list(shape), dtype))
        self.h = h
        self.ap = h[:]
        self.t = P.tile(name)
        self.sub = [P.tile(f"{name}.{i}") for i in range(nsub)] if nsub > 1 else None


class Seg:
    def __init__(self, name, S):
        self.name = name
        self.S = S
        self.d = {}


def build(seg_sizes, depth=DEPTH, phases=None, dbg=(), prune=None):
    prune = prune or {}
    nc = bass.Bass("TRN2", target_bir_lowering=False)
    P = Prog(nc)
    segs = [Seg(n, S) for n, S in seg_sizes]
    SMAX = max(S for _, S in seg_sizes)

    def din(name, shape, dt=F32):
        return nc.dram_tensor(name, list(shape), dt, kind="ExternalInput").ap()

    def dscr(name, shape, dt=BF16):
        if name in dbg:
            return nc.dram_tensor(name, list(shape), dt, kind="ExternalOutput").ap()
        return nc.dram_tensor(name, list(shape), dt).ap()

    for s in segs:
        s.d["x"] = din(f"x_{s.name}", [s.S, D_MODEL])
        s.own = prune.get(s.name, s.S)
        s.d["y"] = nc.dram_tensor(f"y_{s.name}", [s.own, D_MODEL], F32, kind="ExternalOutput").ap()
        s.d["keep"] = din(f"keep_{s.name}", [128, s.S // CHUNK, 16])
        s.d["rope"] = din(f"rope_{s.name}", [128, 2, s.S])
    w_in_f = din("w_in_ext", [depth, D_MODEL, IN_EXT])
    w_uq_f = din("w_uq_p", [depth, Q_LORA, 2048])
    w_ukv_f = din("w_ukv_p", [depth, KV_LORA, 2048])
    w_oa_f = din("w_oa", [depth, 1024, D_MODEL])
    w_ob_f = din("w_ob", [depth, 1024, D_MODEL])
    w_out_f = din("w_out", [depth, D_MODEL, D_MODEL])
    norm_in_t = din("norm_in_t", [depth, 128, 16])
    qan_t = din("q_a_norm_t", [depth, 128, 4])
    kvan_t = din("kv_a_norm_t", [depth, 128, 4])
    bgates_b = din("b_gates_b", [depth, 128, 32])
    mhn_b = din("m_head_norm_b", [depth, 128, 1024])
    normf_b = din("norm_f_b", [128, D_MODEL])
    consts_f = din("consts", [128, 6 * 128])

    w_in = dscr("w_in_bf", [depth, D_MODEL, IN_EXT])
    w_uq = dscr("w_uq_bf", [depth, Q_LORA, 2048])
    w_ukv = dscr("w_ukv_bf", [depth, KV_LORA, 2048])
    w_oa = dscr("w_oa_bf", [depth, 1024, D_MODEL])
    w_ob = dscr("w_ob_bf", [depth, 1024, D_MODEL])
    w_out = dscr("w_out_bf", [depth, D_MODEL, D_MODEL])

    for s in segs:
        S = s.S
        n = s.name
        d = s.d
        d["x1"] = dscr(f"x1_{n}", [S, D_MODEL], F32)
        d["qT"] = dscr(f"qT_{n}", [A_HEADS, 192, S])
        d["knT"] = dscr(f"knT_{n}", [A_HEADS, 128, S])
        d["krT"] = dscr(f"krT_{n}", [64, S])
        d["Vh"] = dscr(f"Vh_{n}", [A_HEADS, 128, S // 128, 128])
        d["zaT"] = dscr(f"zaT_{n}", [1024, S])
        d["qm"] = dscr(f"qm_{n}", [2, S, 1024])
        d["km"] = dscr(f"km_{n}", [2, S, 1024])
        d["vm"] = dscr(f"vm_{n}", [S, 1024])
        d["EG"] = dscr(f"EG_{n}", [128, S // CHUNK, 16], F32)
        d["ozm"] = dscr(f"ozm_{n}", [S, 1024])
        d["hfw"] = dscr(f"hfw_{n}", [S, 1024], F32)
        d["yaT"] = dscr(f"yaT_{n}", [1024, S])
        d["ybT"] = dscr(f"ybT_{n}", [1024, S])
        d["sgaT"] = dscr(f"sgaT_{n}", [D_MODEL, S])
        d["sgbT"] = dscr(f"sgbT_{n}", [D_MODEL, S])

    top = ExitStack()
    cst = Buf(P, top, "sb", "cst_f", [128, 6 * 128], F32)
    cbf = Buf(P, top, "sb", "cbf", [128, 6, 128], BF16)
    ident = Buf(P, top, "sb", "ident", [128, 128], BF16)
    ones_bf = Buf(P, top, "sb", "ones_bf", [128, 128], BF16)
    mask_bf = Buf(P, top, "sb", "mask_bf", [128, 2, 4, 128], BF16)
    zero_f = Buf(P, top, "sb", "zero_f", [128, 16], F32)
    U_fw = cst.ap[:, 128:256]
    U_bw = cst.ap[:, 256:384]
    ones_f = cst.ap[:, 384:512]

    wsem = nc.alloc_semaphore("wsem")
    P.semcnt[id(wsem)] = 0
    P.dma("sp", cst.ap, consts_f[:, :], writes=[cst.t])
    P.copy("dve", cbf.ap.rearrange("p a b -> p (a b)"), cst.ap, [cst.t], [cbf.t])
    P.copy("dve", ident.ap, cst.ap[:, 0:128], [cst.t], [ident.t])
    P.copy("dve", ones_bf.ap, ones_f, [cst.t], [ones_bf.t])
    for dr in range(2):
        for c in range(4):
            P.copy("dve", mask_bf.ap[:, dr, c, :], cst.ap[:, 128 * (1 + dr):128 * (2 + dr)], [cst.t], [mask_bf.t])
    P.memset("dve", zero_f.ap, 0.0, [zero_f.t])
    def cast_w(dst, src, rows, cols):
        for r0 in range(0, rows, 128):
            for c0 in range(0, cols, 2048):
                c1 = min(cols, c0 + 2048)
                P.dma_raw("pool", dst[r0:r0 + 128, c0:c1], src[r0:r0 + 128, c0:c1], wsem)

    for L in range(depth):
        cast_w(w_in[L], w_in_f[L], D_MODEL, IN_EXT)
        cast_w(w_out[L], w_out_f[L], D_MODEL, D_MODEL)
        cast_w(w_oa[L], w_oa_f[L], 1024, D_MODEL)
        cast_w(w_ob[L], w_ob_f[L], 1024, D_MODEL)
        cast_w(w_uq[L], w_uq_f[L], Q_LORA, 2048)
        cast_w(w_ukv[L], w_ukv_f[L], KV_LORA, 2048)
    P.end_phase()

    for L in range(depth):
        last = (L == depth - 1)
        for s in segs:
            xin = s.d["x"] if L == 0 else s.d["x1"]
            own = s.own if last else s.S
            NT = s.S // 128
            if phases is None or "p1" in phases:
                phase1(P, L, s, xin, w_in, w_uq, w_ukv, norm_in_t, qan_t, kvan_t, bgates_b, s.d["rope"],
                       ident, ones_bf, cbf, own)
            if phases is None or "att" in phases:
                phase_att(P, L, s, ones_bf, own)
            if phases is None or "ml" in phases:
                ot = own // 128
                if own == s.S and s.own == s.S:
                    plan_fw = [(list(range(NT)), True)]
                    plan_bw = [(list(range(NT - 1, -1, -1)), True)]
                elif own == s.S:
                    plan_fw = [(list(range(NT)), False), (list(range(NT)), True)]
                    plan_bw = [(list(range(NT - 1, -1, -1)), False), (list(range(NT - 1, -1, -1)), True)]
                else:
                    plan_fw = [(list(range(ot, NT)), False), (list(range(ot)), True)]
                    plan_bw = [(list(range(NT - 1, ot - 1, -1)), False), (list(range(ot - 1, -1, -1)), True)]
                phase_mlstm(P, L, s, 0, mhn_b, ident, mask_bf, plan_fw)
                phase_mlstm(P, L, s, 1, mhn_b, ident, mask_bf, plan_bw)
            if phases is None or "out" in phases:
                xout = s.d["y"] if last else s.d["x1"]
                phase_out(P, L, s, xin, xout, w_oa, w_ob, w_out, normf_b if last else None, own)
    top.close()
    return nc, P


def phase1(P, L, s, xin, w_in, w_uq, w_ukv, norm_in_t, qan_t, kvan_t, bgates_b, rope_cs,
           ident, ones_bf, cbf, own):
    st = ExitStack()
    S = s.S
    d = s.d
    TT = 1024
    xt = [Buf(P, st, "sb", f"xt{i}", [128, D_MODEL], F32) for i in range(2)]
    junk = Buf(P, st, "sb", "junk", [128, D_MODEL], BF16)
    hb = [Buf(P, st, "sb", f"hb{i}", [128, D_MODEL], BF16) for i in range(2)]
    hT = Buf(P, st, "sb", "hT", [128, 16, TT], BF16, nsub=8)
    wb = [Buf(P, st, "sb", f"wb{i}", [128, 16, 512], BF16) for i in range(3)]
    wuq = Buf(P, st, "sb", "wuq", [128, 4, 2048], BF16)
    wukv = Buf(P, st, "sb", "wukv", [128, 4, 2048], BF16)
    craw1 = Buf(P, st, "sb", "craw", [128, 4, TT], BF16)
    craw = [craw1, craw1]
    cn = [Buf(P, st, "sb", f"cn{i}", [128, 4, TT], BF16, nsub=8) for i in range(2)]
    sq = [Buf(P, st, "sb", f"sq{i}", [128, 512], BF16) for i in range(2)]
    rstdb = Buf(P, st, "sb", "rstdb", [128, 512], F32)
    gain = Buf(P, st, "sb", "gain", [128, 16], F32)
    qg = Buf(P, st, "sb", "qg", [128, 4], F32)
    kvg = Buf(P, st, "sb", "kvg", [128, 4], F32)
    bg = Buf(P, st, "sb", "bg", [128, 32], F32)
    cs = Buf(P, st, "sb", "cs", [128, 2, TT], F32)
    stg = [Buf(P, st, "sb", f"stg{i}", [128, TT], BF16) for i in range(4)]
    f1 = [Buf(P, st, "sb", f"f1_{i}", [128, 512], F32) for i in range(2)]
    f2 = [Buf(P, st, "sb", f"f2_{i}", [128, 512], F32) for i in range(2)]
    sA = [Buf(P, st, "sb", f"sA{i}", [128, 512], BF16) for i in range(2)]
    sB = [Buf(P, st, "sb", f"sB{i}", [128, 512], BF16) for i in range(2)]
    ss = Buf(P, st, "sb", "ss", [128, 8], F32)
    lnv = Buf(P, st, "sb", "lnv", [128, 8], F32)
    rstd = Buf(P, st, "sb", "rstd", [128, 8], F32)
    gt = [Buf(P, st, "sb", f"gt{i}", [128, 32], F32) for i in range(2)]
    lf = [Buf(P, st, "sb", f"lf{i}", [128, 16], F32) for i in range(2)]
    hl = [Buf(P, st, "sb", f"hl{i}", [128, 2, 16], BF16) for i in range(2)]
    ab = [Buf(P, st, "sb", f"ab{i}", [128, 32], F32) for i in range(2)]
    eab = Buf(P, st, "sb", "eab", [128, 8, 32], F32, nsub=8)
    egs = Buf(P, st, "sb", "egs", [128, 16, 16], F32)
    kps = Buf(P, st, "sb", "kps", [128, 16, 16], F32)
    tp = [Buf(P, st, "ps", f"tp{i}", [128, 1024], BF16) for i in range(2)]
    acc = [Buf(P, st, "ps", f"acc{i}", [128, 512], F32) for i in range(4)]
    sm = Buf(P, st, "ps", "sm", [128, 512], F32)
    sm_g = sm_b = sm_G = sm.t

    cnt = {"acc": 0, "stg": 0, "wb": 0, "f": 0, "s": 0}

    def nacc():
        a = acc[cnt["acc"] % 4]
        cnt["acc"] += 1
        return a

    def nstg():
        a = stg[cnt["stg"] % 4]
        cnt["stg"] += 1
        return a

    P.dma("sp", gain.ap, norm_in_t[L], writes=[gain.t])
    P.dma("sp", qg.ap, qan_t[L], writes=[qg.t])
    P.dma("sp", kvg.ap, kvan_t[L], writes=[kvg.t])
    P.dma("sp", bg.ap, bgates_b[L], writes=[bg.t])
    P.dma("sp", wuq.ap, w_uq[L].rearrange("(k p) n -> p k n", p=128), writes=[wuq.t])
    P.dma("sp", wukv.ap, w_ukv[L].rearrange("(k p) n -> p k n", p=128), writes=[wukv.t])

    def load_w(col0, ncols):
        b = wb[cnt["wb"] % 3]
        cnt["wb"] += 1
        P.dma("sp", b.ap[:, :, 0:ncols], w_in[L, :, col0:col0 + ncols].rearrange("(k p) n -> p k n", p=128),
              writes=[b.t])
        return b

    def gemmB(lhs_of_kc, nk, rhs_buf, rhs_tiles, hf, out_ap, out_tile, extra_reads):
        for kc in range(nk):
            P.mm(out_ap, lhs_of_kc(kc), rhs_buf.ap[:, kc, hf * 512:(hf + 1) * 512], kc == 0, kc == nk - 1,
                 list(extra_reads) + list(rhs_tiles), [out_tile])

    import os
    _STOP = int(os.environ.get('P1STOP', '99'))
    _G = int(os.environ.get('GSTOP', '99'))
    for tt in range(S // TT):
        t0 = tt * TT
        full = t0 < own
        P.dma("sp", cs.ap, rope_cs[:, :, t0:t0 + TT], writes=[cs.t])
        P.memset("dve", ss.ap, 0.0, [ss.t])
        for sub in range(8):
            x = xt[sub % 2]
            h_ = hb[sub % 2]
            P.dma("sp", x.ap, xin[t0 + sub * 128:t0 + (sub + 1) * 128, :], writes=[x.t])
            P.act(junk.ap, x.ap, AF.Square, [x.t], [junk.t, ss.t], accum_out=ss.ap[:, sub:sub + 1])
            P.act(lnv.ap[:, sub:sub + 1], ss.ap[:, sub:sub + 1], AF.Ln, [ss.t], [lnv.t], scale=1.0 / D_MODEL, bias=EPS)
            P.act(rstd.ap[:, sub:sub + 1], lnv.ap[:, sub:sub + 1], AF.Exp, [lnv.t], [rstd.t], scale=-0.5)
            P.ts("dve", h_.ap, x.ap, rstd.ap[:, sub:sub + 1], ALU.mult, [x.t, rstd.t], [h_.t])
            for half in range(2):
                tpb = tp[half]
                for k in range(8):
                    kc = half * 8 + k
                    P.tr(tpb.ap[:, k * 128:(k + 1) * 128], h_.ap[:, kc * 128:(kc + 1) * 128], ident.ap,
                         [h_.t, ident.t], [tpb.t])
                P.tt("dve", hT.ap[:, half * 8:half * 8 + 8, sub * 128:(sub + 1) * 128],
                     tpb.ap.rearrange("p (a b) -> p a b", a=8),
                     gain.ap[:, half * 8:half * 8 + 8].unsqueeze(2).broadcast_to([128, 8, 128]),
                     ALU.mult, [tpb.t, gain.t], [hT.sub[sub]])
        hT_all = hT.sub

        if _STOP <= 1:
            continue
        for g, (col0, gn) in enumerate(((O_CQ, qg), (O_CKV, kvg))):
            if g == 0 and not full:
                continue
            wbuf = load_w(col0, 512)
            for hf in range(2):
                for blk in range(4):
                    a = nacc()
                    gemmB(lambda kc, blk=blk: wbuf.ap[:, kc, blk * 128:(blk + 1) * 128], 16, hT,
                          hT_all[hf * 4:hf * 4 + 4], hf, a.ap, a.t, [wbuf.t])
                    P.act(craw[g].ap[:, blk, hf * 512:(hf + 1) * 512], a.ap, AF.Copy, [a.t], [craw[g].t])
                a = nacc()
                for blk in range(4):
                    q_ = sq[blk % 2]
                    P.tt("dve", q_.ap, craw[g].ap[:, blk, hf * 512:(hf + 1) * 512],
                         craw[g].ap[:, blk, hf * 512:(hf + 1) * 512], ALU.mult, [craw[g].t], [q_.t])
                    P.mm(a.ap, ones_bf.ap, q_.ap, blk == 0, blk == 3, [ones_bf.t, q_.t], [a.t])
                P.act(rstdb.ap, a.ap, AF.Ln, [a.t], [rstdb.t], scale=1.0 / 512, bias=EPS)
                P.act(rstdb.ap, rstdb.ap, AF.Exp, [rstdb.t], [rstdb.t], scale=-0.5)
                for blk in range(4):
                    P.stt("dve", cn[g].ap[:, blk, hf * 512:(hf + 1) * 512],
                          craw[g].ap[:, blk, hf * 512:(hf + 1) * 512], gn.ap[:, blk:blk + 1], rstdb.ap,
                          ALU.mult, ALU.mult, [craw[g].t, gn.t, rstdb.t], cn[g].sub[hf * 4:hf * 4 + 4])

        if _STOP <= 2:
            continue
        for h in range(A_HEADS if full else 0):
            o = nstg()
            for hf in range(2):
                a = nacc()
                gemmB(lambda kc, h=h: wuq.ap[:, kc, h * 128:(h + 1) * 128], 4, cn[0], cn[0].sub[hf * 4:hf * 4 + 4],
                      hf, a.ap, a.t, [wuq.t])
                P.act(o.ap[:, hf * 512:(hf + 1) * 512], a.ap, AF.Copy, [a.t], [o.t])
            P.dma("pool", d["qT"][h, 0:128, t0:t0 + TT], o.ap, reads=[o.t], load=False)
        for hp in range(A_HEADS // 2 if full else 0):
            o = nstg()
            for hf in range(2):
                a = nacc()
                b = nacc()
                gemmB(lambda kc, hp=hp: wuq.ap[:, kc, 1024 + hp * 128:1024 + (hp + 1) * 128], 4, cn[0],
                      cn[0].sub[hf * 4:hf * 4 + 4], hf, a.ap, a.t, [wuq.t])
                gemmB(lambda kc, hp=hp: wuq.ap[:, kc, 1536 + hp * 128:1536 + (hp + 1) * 128], 4, cn[0],
                      cn[0].sub[hf * 4:hf * 4 + 4], hf, b.ap, b.t, [wuq.t])
                u = f1[hf]
                v = f2[hf]
                P.tt("dve", u.ap, a.ap, cs.ap[:, 0, hf * 512:(hf + 1) * 512], ALU.mult, [a.t, cs.t], [u.t])
                P.tt("dve", v.ap, b.ap, cs.ap[:, 1, hf * 512:(hf + 1) * 512], ALU.mult, [b.t, cs.t], [v.t])
                P.tt("dve", o.ap[:, hf * 512:(hf + 1) * 512], u.ap, v.ap, ALU.add, [u.t, v.t], [o.t])
            P.dma("pool", d["qT"][2 * hp, 128:192, t0:t0 + TT], o.ap[0:64, :], reads=[o.t], load=False)
            P.dma("pool", d["qT"][2 * hp + 1, 128:192, t0:t0 + TT], o.ap[64:128, :], reads=[o.t], load=False)

        if _STOP <= 3:
            continue
        for h in range(A_HEADS):
            o = nstg()
            for hf in range(2):
                a = nacc()
                gemmB(lambda kc, h=h: wukv.ap[:, kc, h * 128:(h + 1) * 128], 4, cn[1], cn[1].sub[hf * 4:hf * 4 + 4],
                      hf, a.ap, a.t, [wukv.t])
                P.act(o.ap[:, hf * 512:(hf + 1) * 512], a.ap, AF.Copy, [a.t], [o.t])
            P.dma("pool", d["knT"][h, :, t0:t0 + TT], o.ap, reads=[o.t], load=False)
        for sub in range(8):
            o = nstg()
            for j in range(2):
                a = nacc()
                for kc in range(4):
                    P.mm(a.ap, cn[1].ap[:, kc, sub * 128:(sub + 1) * 128],
                         wukv.ap[:, kc, 1024 + j * 512:1024 + (j + 1) * 512], kc == 0, kc == 3,
                         [cn[1].sub[sub], wukv.t], [a.t])
                P.act(o.ap[:, j * 512:(j + 1) * 512], a.ap, AF.Copy, [a.t], [o.t])
            blk = (t0 + sub * 128) // 128
            P.dma("pool", d["Vh"][:, :, blk, :].rearrange("h p d -> p h d"),
                  o.ap.rearrange("p (h d) -> p h d", h=8), reads=[o.t], load=False)

        if _STOP <= 4:
            continue
        wk = load_w(O_KR, 64)
        wks = load_w(O_KRS, 64)
        o = nstg()
        for hf in range(2):
            a = nacc()
            b = nacc()
            gemmB(lambda kc: wk.ap[:, kc, 0:64], 16, hT, hT_all[hf * 4:hf * 4 + 4], hf, a.ap[0:64, :], a.t, [wk.t])
            gemmB(lambda kc: wks.ap[:, kc, 0:64], 16, hT, hT_all[hf * 4:hf * 4 + 4], hf, b.ap[0:64, :], b.t, [wks.t])
            u = f1[hf]
            v = f2[hf]
            P.tt("dve", u.ap[0:64, :], a.ap[0:64, :], cs.ap[0:64, 0, hf * 512:(hf + 1) * 512], ALU.mult,
                 [a.t, cs.t], [u.t])
            P.tt("dve", v.ap[0:64, :], b.ap[0:64, :], cs.ap[0:64, 1, hf * 512:(hf + 1) * 512], ALU.mult,
                 [b.t, cs.t], [v.t])
            P.tt("dve", o.ap[0:64, hf * 512:(hf + 1) * 512], u.ap[0:64, :], v.ap[0:64, :], ALU.add,
                 [u.t, v.t], [o.t])
        P.dma("pool", d["krT"][:, t0:t0 + TT], o.ap[0:64, :], reads=[o.t], load=False)

        if _STOP <= 5:
            continue
        for (col0, nblk, func, dst) in ((O_ZA, 8, AF.Silu, d["zaT"]), (O_GA, 16, AF.Sigmoid, d["sgaT"]),
                                        (O_GB, 16, AF.Sigmoid, d["sgbT"])):
            if not full:
                continue
            for b4 in range(nblk // 4):
                wbuf = load_w(col0 + b4 * 512, 512)
                for blk in range(4):
                    o = nstg()
                    for hf in range(2):
                        a = nacc()
                        gemmB(lambda kc, blk=blk: wbuf.ap[:, kc, blk * 128:(blk + 1) * 128], 16, hT,
                              hT_all[hf * 4:hf * 4 + 4], hf, a.ap, a.t, [wbuf.t])
                        P.act(o.ap[:, hf * 512:(hf + 1) * 512], a.ap, func, [a.t], [o.t])
                    r0 = (b4 * 4 + blk) * 128
                    P.dma("pool", dst[r0:r0 + 128, t0:t0 + TT], o.ap, reads=[o.t], load=False)

        if _STOP <= 6:
            continue
        wg = load_w(O_GT, 32)
        for sub in range(8):
            i2 = sub % 2
            for kc in range(16):
                P.mm(sm.ap[:, 0:32], hT.ap[:, kc, sub * 128:(sub + 1) * 128], wg.ap[:, kc, 0:32], kc == 0, kc == 15,
                     [hT.sub[sub], wg.t], [sm_g])
            g_ = gt[i2]
            P.tt("dve", g_.ap, sm.ap[:, 0:32], bg.ap, ALU.add, [sm_g, bg.t], [g_.t])
            if _G <= 1:
                continue
            l_ = lf[i2]
            P.act(l_.ap, g_.ap[:, 16:32], AF.Exp, [g_.t], [l_.t], scale=-1.0)
            P.act(l_.ap, l_.ap, AF.Ln, [l_.t], [l_.t], bias=1.0)
            P.ts("dve", l_.ap, l_.ap, -1.0, ALU.mult, [l_.t], [l_.t])
            if _G <= 2:
                continue
            hl_ = hl[i2]
            P.copy("dve", hl_.ap[:, 0, :], l_.ap, [l_.t], [hl_.t])
            P.tt("dve", hl_.ap[:, 1, :], l_.ap, hl_.ap[:, 0, :], ALU.subtract, [l_.t, hl_.t], [hl_.t])
            if _G <= 3:
                continue
            for part in range(2):
                P.mm(sm.ap[:, 64:72], cbf.ap[:, 1, :], hl_.ap[:, part, 0:8], part == 0, part == 1, [cbf.t, hl_.t], [sm_b])
            for part in range(2):
                P.mm(sm.ap[:, 72:80], cbf.ap[:, 2, :], hl_.ap[:, part, 8:16], part == 0, part == 1, [cbf.t, hl_.t], [sm_b])
            if _G <= 4:
                continue
            for c in range(2):
                for part in range(2):
                    P.mm(sm.ap[:, 128 + 16 * c:144 + 16 * c], cbf.ap[:, 4 + c, :], hl_.ap[:, part, :], part == 0,
                         part == 1, [cbf.t, hl_.t], [sm_G])
            if _G <= 5:
                continue
            a_ = ab[i2]
            _G2 = int(os.environ.get('G2', '99'))
            P.copy("dve", a_.ap[:, 0:16], sm.ap[:, 64:80], [sm_b], [a_.t])
            if _G2 <= 1:
                continue
            P.tt("dve", a_.ap[:, 16:32], g_.ap[:, 0:16], a_.ap[:, 0:16], ALU.subtract, [g_.t, a_.t], [a_.t])
            if _G2 <= 2:
                continue
            P.act(eab.ap[:, sub, :], a_.ap, AF.Exp, [a_.t], [eab.sub[sub]])
            if _G2 <= 3:
                continue
            P.ts("dve", eab.ap[:, sub, 0:16], eab.ap[:, sub, 0:16], MQ_SCALE, ALU.mult, [eab.sub[sub]], [eab.sub[sub]])
            if _G <= 6:
                continue
            P.act(egs.ap[:, 2 * sub:2 * sub + 2, :], sm.ap[:, 128:160].rearrange("p (c n) -> p c n", c=2), AF.Exp,
                  [sm_G], [egs.t])
        nck0 = t0 // CHUNK
        if _G > 6:
            P.dma("sp", kps.ap, d["keep"][:, nck0:nck0 + 16, :], writes=[kps.t])
            P.tt("dve", egs.ap, egs.ap, kps.ap, ALU.mult, [egs.t, kps.t], [egs.t])
            P.dma("pool", d["EG"][:, nck0:nck0 + 16, :], egs.ap, reads=[egs.t], load=False)

        if _STOP <= 7:
            continue
        for (col0, which, dst) in ((O_QM, 0, d["qm"]), (O_KM, 1, d["km"])):
            if which == 0 and not full:
                continue
            for j in range(2):
                wq = load_w(col0 + j * 512, 512)
                for sub in range(8):
                    a = nacc()
                    for kc in range(16):
                        P.mm(a.ap, hT.ap[:, kc, sub * 128:(sub + 1) * 128], wq.ap[:, kc, :], kc == 0, kc == 15,
                             [hT.sub[sub], wq.t], [a.t])
                    for dr in range(2):
                        o = nstg()
                        c0 = which * 16 + dr * 8 + j * 4
                        P.tt("dve", o.ap[:, 0:512].rearrange("p (h d) -> p h d", h=4),
                             a.ap.rearrange("p (h d) -> p h d", h=4),
                             eab.ap[:, sub, c0:c0 + 4].unsqueeze(2).broadcast_to([128, 4, 128]), ALU.mult,
                             [a.t, eab.sub[sub]], [o.t])
                        P.dma("pool", dst[dr, t0 + sub * 128:t0 + (sub + 1) * 128, j * 512:(j + 1) * 512],
                              o.ap[:, 0:512], reads=[o.t], load=False)
        for j in range(2):
            wv = load_w(O_VM + j * 512, 512)
            for sub in range(8):
                a = nacc()
                for kc in range(16):
                    P.mm(a.ap, hT.ap[:, kc, sub * 128:(sub + 1) * 128], wv.ap[:, kc, :], kc == 0, kc == 15,
                         [hT.sub[sub], wv.t], [a.t])
                o = nstg()
                P.act(o.ap[:, 0:512], a.ap, AF.Copy, [a.t], [o.t])
                P.dma("pool", d["vm"][t0 + sub * 128:t0 + (sub + 1) * 128, j * 512:(j + 1) * 512], o.ap[:, 0:512],
                      reads=[o.t], load=False)
        for j in range(2 if full else 0):
            wo = load_w(O_OM + j * 512, 512)
            wz = load_w(O_ZM + j * 512, 512)
            for sub in range(8):
                a = nacc()
                b = nacc()
                for kc in range(16):
                    P.mm(a.ap, hT.ap[:, kc, sub * 128:(sub + 1) * 128], wo.ap[:, kc, :], kc == 0, kc == 15,
                         [hT.sub[sub], wo.t], [a.t])
                for kc in range(16):
                    P.mm(b.ap, hT.ap[:, kc, sub * 128:(sub + 1) * 128], wz.ap[:, kc, :], kc == 0, kc == 15,
                         [hT.sub[sub], wz.t], [b.t])
                u = sA[sub % 2]
                v = sB[sub % 2]
                P.act(u.ap, a.ap, AF.Sigmoid, [a.t], [u.t])
                P.act(v.ap, b.ap, AF.Silu, [b.t], [v.t])
                o = nstg()
                P.tt("dve", o.ap[:, 0:512], u.ap, v.ap, ALU.mult, [u.t, v.t], [o.t])
                P.dma("pool", d["ozm"][t0 + sub * 128:t0 + (sub + 1) * 128, j * 512:(j + 1) * 512], o.ap[:, 0:512],
                      reads=[o.t], load=False)
    st.close()
    P.end_phase()


def phase_att(P, L, s, ones_bf, own):
    st = ExitStack()
    S = s.S
    d = s.d
    NCH = 4
    KC = S // NCH
    nb = KC // 128
    NKB = S // 128
    NQ = own // 512
    kn = [Buf(P, st, "sb", f"kn{c}", [128, KC], BF16) for c in range(NCH)]
    vv = [Buf(P, st, "sb", f"vv{c}", [128, nb, 128], BF16) for c in range(NCH)]
    kr = Buf(P, st, "sb", "kr", [128, S], BF16)
    qn = [Buf(P, st, "sb", f"qn{i}", [128, 512], BF16) for i in range(2)]
    qr = [Buf(P, st, "sb", f"qr{i}", [128, 512], BF16) for i in range(2)]
    za = [Buf(P, st, "sb", f"za{i}", [128, 512], BF16) for i in range(2)]
    pt = [Buf(P, st, "sb", f"pt{i}", [128, 512], BF16) for i in range(8)]
    p2 = [Buf(P, st, "sb", f"p2_{i}", [128, 512], BF16) for i in range(3)]
    s2 = [Buf(P, st, "sb", f"s2_{i}", [128, 512], BF16) for i in range(2)]
    s4 = [Buf(P, st, "sb", f"s4_{i}", [128, 512], BF16) for i in range(2)]
    rl = Buf(P, st, "sb", "rl", [128, 512], F32)
    yo = Buf(P, st, "sb", "yo", [128, 512], F32)
    yb = [Buf(P, st, "sb", f"yb{i}", [128, 512], BF16) for i in range(2)]
    sps = [Buf(P, st, "ps", f"sps{i}", [128, 512], F32) for i in range(3)]
    ops = [Buf(P, st, "ps", f"ops{i}", [128, 512], F32) for i in range(2)]
    lps = [Buf(P, st, "ps", f"lps{i}", [128, 512], F32) for i in range(2)]

    P.memset("dve", kr.ap[64:128, :], 0.0, [kr.t])
    for i in range(2):
        P.memset("dve", qr[i].ap[64:128, :], 0.0, [qr[i].t])
    P.dma("sp", kr.ap[0:64, :], d["krT"][:, :], writes=[kr.t])

    def load_kv(h, c):
        P.dma("sp", kn[c].ap, d["knT"][h, :, c * KC:(c + 1) * KC], writes=[kn[c].t])
        P.dma("sp", vv[c].ap, d["Vh"][h, :, c * nb:(c + 1) * nb, :], writes=[vv[c].t])

    for c in range(NCH):
        load_kv(0, c)

    def load_q(h, qt, i):
        q0 = qt * 512
        P.dma("sp", qn[i].ap, d["qT"][h, 0:128, q0:q0 + 512], writes=[qn[i].t])
        P.dma("sp", qr[i].ap[0:64, :], d["qT"][h, 128:192, q0:q0 + 512], writes=[qr[i].t])
        P.dma("sp", za[i].ap, d["zaT"][h * 128:(h + 1) * 128, q0:q0 + 512], writes=[za[i].t])

    items = [(h, qt, kb) for h in range(A_HEADS) for qt in range(NQ) for kb in range(NKB)]
    LAG = 2
    LAG2 = 2
    NPT = 8
    n = len(items)
    load_q(0, 0, 0)
    for i in range(n + LAG + LAG2):
        if i < n:
            h, qt, kb = items[i]
            qi = (h * NQ + qt)
            if kb == 5:
                nxt = qi + 1
                if nxt < A_HEADS * NQ:
                    load_q(nxt // NQ, nxt % NQ, nxt % 2)
            c, kk = kb // nb, kb % nb
            sp_ = sps[i % 3]
            P.mm(sp_.ap, kn[c].ap[:, kk * 128:(kk + 1) * 128], qn[qi % 2].ap, True, False,
                 [kn[c].t, qn[qi % 2].t], [sp_.t])
            P.mm(sp_.ap, kr.ap[:, kb * 128:(kb + 1) * 128], qr[qi % 2].ap, False, True,
                 [kr.t, qr[qi % 2].t], [sp_.t])
            P.act(pt[i % NPT].ap, sp_.ap, AF.Exp, [sp_.t], [pt[i % NPT].t], scale=ATT_SCALE)
        j = i - LAG
        if 0 <= j < n:
            h, qt, kb = items[j]
            qi = (h * NQ + qt)
            c, kk = kb // nb, kb % nb
            o_ps = ops[qi % 2]
            p_ = pt[j % NPT]
            P.mm(o_ps.ap, vv[c].ap[:, kk, :], p_.ap, kb == 0, kb == NKB - 1, [vv[c].t, p_.t], [o_ps.t])
            if kb % 2 == 1:
                pm = pt[(j - 1) % NPT]
                ps_ = s2[(kb // 2) % 2]
                P.tt("dve", ps_.ap, pm.ap, p_.ap, ALU.add, [pm.t, p_.t], [ps_.t])
                if kb % 4 == 3:
                    s4_ = s4[(kb // 4) % 2]
                    P.tt("dve", s4_.ap, s2[0].ap, s2[1].ap, ALU.add, [s2[0].t, s2[1].t], [s4_.t])
                    if kb % 8 == 7:
                        pp = p2[(j // 8) % 3]
                        P.tt("dve", pp.ap, s4[0].ap, s4[1].ap, ALU.add, [s4[0].t, s4[1].t], [pp.t])
            if qt == NQ - 1 and h < A_HEADS - 1 and kk == nb - 1:
                load_kv(h + 1, c)
        j = i - LAG - LAG2
        if 0 <= j < n:
            h, qt, kb = items[j]
            qi = (h * NQ + qt)
            if kb % 8 == 7:
                l_ps = lps[qi % 2]
                pp = p2[(j // 8) % 3]
                P.mm(l_ps.ap, ones_bf.ap, pp.ap, kb == 7, kb == NKB - 1, [ones_bf.t, pp.t], [l_ps.t])
            if kb == NKB - 1:
                o_ps = ops[qi % 2]
                l_ps = lps[qi % 2]
                q0 = qt * 512
                P.recip(rl.ap, l_ps.ap, [l_ps.t], [rl.t])
                P.tt("dve", yo.ap, o_ps.ap, rl.ap, ALU.mult, [o_ps.t, rl.t], [yo.t])
                y_ = yb[qi % 2]
                P.tt("dve", y_.ap, yo.ap, za[qi % 2].ap, ALU.mult, [yo.t, za[qi % 2].t], [y_.t])
                P.dma("pool", d["yaT"][h * 128:(h + 1) * 128, q0:q0 + 512], y_.ap, reads=[y_.t], load=False)
    st.close()
    P.end_phase()


def phase_mlstm(P, L, s, dr, mhn_b, ident, mask_bf, plan):
    st = ExitStack()
    S = s.S
    d = s.d
    NCK = S // CHUNK
    NB = 3
    qtm = [Buf(P, st, "sb", f"qtm{i}", [128, 1024], BF16) for i in range(NB)]
    ktm = [Buf(P, st, "sb", f"ktm{i}", [128, 1024], BF16) for i in range(NB)]
    vau = [Buf(P, st, "sb", f"vau{i}", [128, 8, 129], BF16) for i in range(NB)]
    egt = [Buf(P, st, "sb", f"egt{i}", [128, 3, 16], F32) for i in range(NB)]
    qT_ = [Buf(P, st, "sb", f"qT{i}", [128, 8, 128], BF16) for i in range(2)]
    kT_ = [Buf(P, st, "sb", f"kT{i}", [128, 8, 128], BF16) for i in range(2)]
    pT = [Buf(P, st, "sb", f"pT{i}", [128, 4, 128], BF16) for i in range(2)]
    Tst = Buf(P, st, "sb", "Tst", [128, 8, 129], F32, nsub=8)
    Cb = [Buf(P, st, "sb", f"Cb{i}", [128, 8, 129], BF16, nsub=8) for i in range(3)]
    hout = [Buf(P, st, "sb", f"hout{i}", [128, 1024], F32) for i in range(2)]
    dn = Buf(P, st, "sb", "dn", [128, 8], F32)
    nd = Buf(P, st, "sb", "nd", [128, 8], F32)
    rd = Buf(P, st, "sb", "rd", [128, 8], F32)
    sT = [Buf(P, st, "ps", f"sT{i}", [128, 512], F32) for i in range(2)]
    num = [Buf(P, st, "ps", f"num{i}", [128, 3, 129], F32) for i in range(3)]
    dCb = [Buf(P, st, "ps", f"dC{i}", [128, 3, 129], F32) for i in range(2)]
    tpm = Buf(P, st, "ps", "tpm", [128, 1024], BF16)
    if dr == 1:
        hfw = [Buf(P, st, "sb", f"hfw{i}", [128, 1024], F32) for i in range(NB)]
        ozm = [Buf(P, st, "sb", f"ozm{i}", [128, 1024], BF16) for i in range(NB)]
        mhn = Buf(P, st, "sb", "mhn", [128, 1024], F32)
        hsq = Buf(P, st, "sb", "hsq", [128, 1024], F32)
        ssm = Buf(P, st, "sb", "ssm", [128, 8], F32)
        rsm = Buf(P, st, "sb", "rsm", [128, 8], F32)
        ybt = Buf(P, st, "sb", "ybt", [128, 1024], BF16)
        ybst = [Buf(P, st, "sb", f"ybst{i}", [128, 8, 512], BF16) for i in range(2)]
        P.dma("sp", mhn.ap, mhn_b[L], writes=[mhn.t])

    rows = [(0, 64), (64, 128)] if dr == 0 else [(64, 128), (0, 64)]
    eidx = [(1, 0), (2, 1)] if dr == 0 else [(1, 2), (0, 1)]

    for i in range(NB):
        P.memset("dve", vau[i].ap[:, :, 128:129], 1.0, [vau[i].t])
    for i in range(3):
        P.memset("dve", Cb[i].ap, 0.0, Cb[i].sub)
    P.memset("dve", Tst.ap, 0.0, Tst.sub)
    cbi = [0]

    def load(ti, b, real):
        r0 = ti * 128
        if real:
            P.dma("sp", qtm[b].ap, d["qm"][dr, r0:r0 + 128, :], writes=[qtm[b].t])
        P.dma("sp", ktm[b].ap, d["km"][dr, r0:r0 + 128, :], writes=[ktm[b].t])
        P.dma("sp", vau[b].ap[:, :, 0:128], d["vm"][r0:r0 + 128, :].rearrange("p (h d) -> p h d", h=8),
              writes=[vau[b].t])
        if dr == 0:
            if ti > 0:
                P.dma("sp", egt[b].ap, d["EG"][:, 2 * ti - 1:2 * ti + 2, :], writes=[egt[b].t])
            else:
                P.dma("sp", egt[b].ap[:, 1:3, :], d["EG"][:, 0:2, :], writes=[egt[b].t])
                P.dma("sp", egt[b].ap[:, 0, :], d["EG"][:, NCK - 1, :], writes=[egt[b].t])
        else:
            if 2 * ti + 3 <= NCK:
                P.dma("sp", egt[b].ap, d["EG"][:, 2 * ti:2 * ti + 3, :], writes=[egt[b].t])
            else:
                P.dma("sp", egt[b].ap[:, 0:2, :], d["EG"][:, 2 * ti:2 * ti + 2, :], writes=[egt[b].t])
                P.dma("sp", egt[b].ap[:, 2, :], d["EG"][:, 0, :], writes=[egt[b].t])
        if real and dr == 1:
            P.dma("sp", hfw[b].ap, d["hfw"][r0:r0 + 128, :], writes=[hfw[b].t])
            P.dma("sp", ozm[b].ap, d["ozm"][r0:r0 + 128, :], writes=[ozm[b].t])

    dcn = [0]

    def chunk_updates(b, ci, need_cb=True):
        lo, hi = rows[ci]
        own, prev = eidx[ci]
        for grp in ((0, 1, 2), (3, 4, 5), (6, 7)):
            dc = dCb[dcn[0] % 2]
            dcn[0] += 1
            for k, h in enumerate(grp):
                P.mm(dc.ap[:, k, :], ktm[b].ap[lo:hi, h * 128:(h + 1) * 128], vau[b].ap[lo:hi, h, :], True, True,
                     [ktm[b].t, vau[b].t], [dc.t])
            for k, h in enumerate(grp):
                col = dr * 8 + h
                P.stt("dve", Tst.ap[:, h, :], Tst.ap[:, h, :], egt[b].ap[:, prev, col:col + 1], dc.ap[:, k, :],
                      ALU.mult, ALU.add, [Tst.sub[h], egt[b].t, dc.t], [Tst.sub[h]])
            if not need_cb:
                continue
            nxt = Cb[(cbi[0] + ci + 1) % 3]
            for k, h in enumerate(grp):
                col = dr * 8 + h
                P.act(nxt.ap[:, h, :], Tst.ap[:, h, :], AF.Copy, [Tst.sub[h], egt[b].t], [nxt.sub[h]],
                      scale=egt[b].ap[:, own, col:col + 1])

    seq = [(ti, real) for (tiles, real) in plan for ti in tiles]
    nseq = len(seq)
    for k_ in range(min(NB - 1, nseq)):
        load(seq[k_][0], k_ % NB, seq[k_][1])
    for n_, (ti, real) in enumerate(seq):
        b = n_ % NB
        b2 = n_ % 2
        if n_ + NB - 1 < nseq:
            load(seq[n_ + NB - 1][0], (n_ + NB - 1) % NB, seq[n_ + NB - 1][1])
        if not real:
            nxt_real = (n_ + 1 < nseq) and seq[n_ + 1][1]
            for ci in range(2):
                chunk_updates(b, ci, need_cb=(nxt_real and ci == 1))
            cbi[0] = (cbi[0] + 2) % 3
            continue
        for (src, dst) in ((qtm[b], qT_[b2]), (ktm[b], kT_[b2])):
            for h in range(8):
                P.tr(tpm.ap[:, h * 128:(h + 1) * 128], src.ap[:, h * 128:(h + 1) * 128], ident.ap,
                     [src.t, ident.t], [tpm.t])
            P.copy("act", dst.ap.rearrange("p h t -> p (h t)"), tpm.ap, [tpm.t], [dst.t])
        chunk_updates(b, 0)
        chunk_updates(b, 1)
        for g in range(2):
            for hh in range(4):
                h = g * 4 + hh
                P.mm(sT[g].ap[:, hh * 128:(hh + 1) * 128], kT_[b2].ap[:, h, :], qT_[b2].ap[:, h, :], True, True,
                     [kT_[b2].t, qT_[b2].t], [sT[g].t])
            P.tt("dve", pT[g].ap.rearrange("p h t -> p (h t)"), sT[g].ap,
                 mask_bf.ap[:, dr].rearrange("p c t -> p (c t)"), ALU.mult, [sT[g].t, mask_bf.t], [pT[g].t])
        for h in range(8):
            nbk = num[h // 3]
            o_ = nbk.ap[:, h % 3, :]
            P.mm(o_, pT[h // 4].ap[:, h % 4, :], vau[b].ap[:, h, :], True, False, [pT[h // 4].t, vau[b].t], [nbk.t])
            for ci in range(2):
                lo, hi = rows[ci]
                cbuf = Cb[(cbi[0] + ci) % 3]
                P.mm(nbk.ap[lo:hi, h % 3, :], qT_[b2].ap[:, h, lo:hi], cbuf.ap[:, h, :], False, True,
                     [qT_[b2].t, cbuf.sub[h]], [nbk.t])
        cbi[0] = (cbi[0] + 2) % 3
        ho = hout[b2]
        for k in range(3):
            nh = 3 if k < 2 else 2
            P.ts("dve", nd.ap[:, 3 * k:3 * k + nh], num[k].ap[:, 0:nh, 128], -1.0, ALU.mult, [num[k].t], [nd.t])
            P.stt("dve", dn.ap[:, 3 * k:3 * k + nh], num[k].ap[:, 0:nh, 128], 1.0, nd.ap[:, 3 * k:3 * k + nh],
                  ALU.max, ALU.max, [num[k].t, nd.t], [dn.t])
            P.recip(rd.ap[:, 3 * k:3 * k + nh], dn.ap[:, 3 * k:3 * k + nh], [dn.t], [rd.t])
            P.tt("dve", ho.ap[:, 384 * k:384 * k + 128 * nh].rearrange("p (h d) -> p h d", h=nh),
                 num[k].ap[:, 0:nh, 0:128], rd.ap[:, 3 * k:3 * k + nh].unsqueeze(2).broadcast_to([128, nh, 128]),
                 ALU.mult, [num[k].t, rd.t], [ho.t])
        r0 = ti * 128
        if dr == 0:
            P.dma("pool", d["hfw"][r0:r0 + 128, :], ho.ap, reads=[ho.t], load=False)
        else:
            P.tt("dve", ho.ap, ho.ap, hfw[b].ap, ALU.add, [ho.t, hfw[b].t], [ho.t])
            P.tt("dve", hsq.ap, ho.ap, ho.ap, ALU.mult, [ho.t], [hsq.t])
            P.op("dve", lambda e: e.tensor_reduce(out=ssm.ap, in_=hsq.ap.rearrange("p (h d) -> p h d", h=8),
                                                  axis=AX.X, op=ALU.add), [hsq.t], [ssm.t])
            P.act(rsm.ap, ssm.ap, AF.Ln, [ssm.t], [rsm.t], scale=1.0 / 128, bias=EPS)
            P.act(rsm.ap, rsm.ap, AF.Exp, [rsm.t], [rsm.t], scale=-0.5)
            P.tt("dve", ho.ap.rearrange("p (h d) -> p h d", h=8), ho.ap.rearrange("p (h d) -> p h d", h=8),
                 rsm.ap.unsqueeze(2).broadcast_to([128, 8, 128]), ALU.mult, [ho.t, rsm.t], [ho.t])
            P.tt("dve", ho.ap, ho.ap, mhn.ap, ALU.mult, [ho.t, mhn.t], [ho.t])
            P.tt("dve", ybt.ap, ho.ap, ozm[b].ap, ALU.mult, [ho.t, ozm[b].t], [ybt.t])
            slot = ti % 4
            sb_ = ybst[(ti // 4) % 2]
            for h in range(8):
                P.tr(tpm.ap[:, h * 128:(h + 1) * 128], ybt.ap[:, h * 128:(h + 1) * 128], ident.ap,
                     [ybt.t, ident.t], [tpm.t])
            P.copy("act", sb_.ap[:, :, slot * 128:(slot + 1) * 128], tpm.ap.rearrange("p (h t) -> p h t", h=8),
                   [tpm.t], [sb_.t])
            if slot == 0:
                t0 = ti * 128
                P.dma("pool", d["ybT"].rearrange("(c p) t -> p c t", p=128)[:, :, t0:t0 + 512], sb_.ap,
                      reads=[sb_.t], load=False)
    st.close()
    P.end_phase()


def phase_out(P, L, s, xin, xout, w_oa, w_ob, w_out, normf_b, own):
    st = ExitStack()
    S = s.S
    d = s.d
    TT = 512
    wa = Buf(P, st, "sb", "wa", [128, 8, D_MODEL], BF16, nsub=4)
    wbb = Buf(P, st, "sb", "wbb", [128, 8, D_MODEL], BF16, nsub=4)
    wo = Buf(P, st, "sb", "wo", [128, 16, D_MODEL], BF16, nsub=4)
    ya = Buf(P, st, "sb", "ya", [128, 8, TT], BF16)
    yb = Buf(P, st, "sb", "ybb", [128, 8, TT], BF16)
    mT = Buf(P, st, "sb", "mT", [128, 16, TT], BF16, nsub=16)
    sga = [Buf(P, st, "sb", f"sga{i}", [128, TT], BF16) for i in range(2)]
    sgb = [Buf(P, st, "sb", f"sgb{i}", [128, TT], BF16) for i in range(2)]
    t1_ = Buf(P, st, "sb", "t1_", [128, TT], F32)
    t2_ = Buf(P, st, "sb", "t2_", [128, TT], F32)
    t1 = [t1_, t1_]
    t2 = [t2_, t2_]
    xb = [Buf(P, st, "sb", f"xb{i}", [128, 512], F32) for i in range(2)]
    xo = [Buf(P, st, "sb", f"xo{i}", [128, D_MODEL], F32) for i in range(2)]
    pa = [Buf(P, st, "ps", f"pa{i}", [128, 512], F32) for i in range(2)]
    pb = [Buf(P, st, "ps", f"pb{i}", [128, 512], F32) for i in range(2)]
    po = [Buf(P, st, "ps", f"po{i}", [128, 512], F32) for i in range(3)]
    if normf_b is not None:
        nf = Buf(P, st, "sb", "nf", [128, D_MODEL], F32)
        junk = Buf(P, st, "sb", "junk2", [128, D_MODEL], BF16)
        ss = Buf(P, st, "sb", "ss2", [128, 4], F32)
        rs = Buf(P, st, "sb", "rs2", [128, 4], F32)
        P.dma("sp", nf.ap, normf_b[:, :], writes=[nf.t])
    cnt = {"sg": 0, "x": 0, "po": 0, "xo": 0}
    woaL = w_oa[L].rearrange("(k p) n -> p k n", p=128)
    wobL = w_ob[L].rearrange("(k p) n -> p k n", p=128)
    woutL = w_out[L].rearrange("(k p) n -> p k n", p=128)
    for c4 in range(4):
        P.dma("sp", wa.ap[:, :, c4 * 512:(c4 + 1) * 512], woaL[:, :, c4 * 512:(c4 + 1) * 512], writes=[wa.sub[c4]])
        P.dma("sp", wbb.ap[:, :, c4 * 512:(c4 + 1) * 512], wobL[:, :, c4 * 512:(c4 + 1) * 512], writes=[wbb.sub[c4]])
    for c4 in range(4):
        P.dma("sp", wo.ap[:, :, c4 * 512:(c4 + 1) * 512], woutL[:, :, c4 * 512:(c4 + 1) * 512], writes=[wo.sub[c4]])
    for tt in range(own // TT):
        t0 = tt * TT
        P.dma("sp", ya.ap, d["yaT"].rearrange("(c p) t -> p c t", p=128)[:, :, t0:t0 + TT], writes=[ya.t])
        P.dma("sp", yb.ap, d["ybT"].rearrange("(c p) t -> p c t", p=128)[:, :, t0:t0 + TT], writes=[yb.t])
        for fc in range(16):
            f4 = fc // 4
            si = cnt["sg"] % 2
            cnt["sg"] += 1
            P.dma("sp", sga[si].ap, d["sgaT"][fc * 128:(fc + 1) * 128, t0:t0 + TT], writes=[sga[si].t])
            P.dma("sp", sgb[si].ap, d["sgbT"][fc * 128:(fc + 1) * 128, t0:t0 + TT], writes=[sgb[si].t])
            a = pa[si]
            b = pb[si]
            for kc in range(8):
                P.mm(a.ap, wa.ap[:, kc, fc * 128:(fc + 1) * 128], ya.ap[:, kc, :], kc == 0, kc == 7,
                     [wa.sub[f4], ya.t], [a.t])
            for kc in range(8):
                P.mm(b.ap, wbb.ap[:, kc, fc * 128:(fc + 1) * 128], yb.ap[:, kc, :], kc == 0, kc == 7,
                     [wbb.sub[f4], yb.t], [b.t])
            P.tt("dve", t1[si].ap, a.ap, sga[si].ap, ALU.mult, [a.t, sga[si].t], [t1[si].t])
            P.tt("dve", t2[si].ap, b.ap, sgb[si].ap, ALU.mult, [b.t, sgb[si].t], [t2[si].t])
            P.tt("dve", mT.ap[:, fc, :], t1[si].ap, t2[si].ap, ALU.add, [t1[si].t, t2[si].t], [mT.sub[fc]])
        for sub in range(4):
            r0 = t0 + sub * 128
            xo_ = xo[cnt["xo"] % 2]
            cnt["xo"] += 1
            for cb in range(4):
                xi = cnt["x"] % 2
                cnt["x"] += 1
                P.dma("sp", xb[xi].ap, xin[r0:r0 + 128, cb * 512:(cb + 1) * 512], writes=[xb[xi].t])
                o = po[cnt["po"] % 3]
                cnt["po"] += 1
                for kc in range(16):
                    P.mm(o.ap, mT.ap[:, kc, sub * 128:(sub + 1) * 128], wo.ap[:, kc, cb * 512:(cb + 1) * 512],
                         kc == 0, kc == 15, [mT.sub[kc], wo.sub[cb]], [o.t])
                P.tt("dve", xo_.ap[:, cb * 512:(cb + 1) * 512], o.ap, xb[xi].ap, ALU.add, [o.t, xb[xi].t], [xo_.t])
            if normf_b is None:
                P.dma("pool", xout[r0:r0 + 128, :], xo_.ap, reads=[xo_.t], load=False)
            else:
                P.memset("dve", ss.ap[:, sub:sub + 1], 0.0, [ss.t])
                P.act(junk.ap, xo_.ap, AF.Square, [xo_.t, ss.t], [junk.t, ss.t], accum_out=ss.ap[:, sub:sub + 1])
                P.act(rs.ap[:, sub:sub + 1], ss.ap[:, sub:sub + 1], AF.Ln, [ss.t], [rs.t], scale=1.0 / D_MODEL,
                      bias=EPS)
                P.act(rs.ap[:, sub:sub + 1], rs.ap[:, sub:sub + 1], AF.Exp, [rs.t], [rs.t], scale=-0.5)
                P.stt("dve", xo_.ap, xo_.ap, rs.ap[:, sub:sub + 1], nf.ap, ALU.mult, ALU.mult,
                      [xo_.t, rs.t, nf.t], [xo_.t])
                P.dma("pool", xout[r0:r0 + 128, :], xo_.ap, reads=[xo_.t], load=False)
    st.close()
    P.end_phase()


def make_consts(smax):
    idx = np.arange(128)
    same = (idx[:, None] // CHUNK) == (idx[None, :] // CHUNK)
    ident = np.eye(128, dtype=np.float32)
    u_fw = (same & (idx[:, None] <= idx[None, :])).astype(np.float32)
    u_bw = (same & (idx[:, None] >= idx[None, :])).astype(np.float32)
    ones = np.ones((128, 128), np.float32)
    b0 = np.zeros((128, 128), np.float32)
    b0[0:64, :] = 1.0
    b1 = np.zeros((128, 128), np.float32)
    b1[64:128, :] = 1.0
    consts = np.concatenate([ident, u_fw, u_bw, ones, b0, b1], axis=1)
    inv = (np.float32(ROPE_THETA) ** (-np.arange(0, QK_ROPE, 2, dtype=np.float32) / np.float32(QK_ROPE))).astype(np.float32)
    ang = (np.arange(smax, dtype=np.float32)[:, None] * inv[None, :]).astype(np.float32)
    cos = np.cos(ang).astype(np.float32).T
    sin = np.sin(ang).astype(np.float32).T
    cos64 = np.concatenate([cos, cos], axis=0)
    sin64 = np.concatenate([-sin, sin], axis=0)
    rope = np.stack([np.concatenate([cos64, cos64], 0), np.concatenate([sin64, sin64], 0)], axis=1)
    return np.ascontiguousarray(consts), np.ascontiguousarray(rope.astype(np.float32))


def prep_weights(w_in, w_uq, w_ukv, norm_in, q_a_norm, kv_a_norm, b_gates, m_head_norm, norm_f):
    depth = w_in.shape[0]
    kr = w_in[:, :, O_KR:O_KR + 64]
    kr_sw = np.concatenate([kr[:, :, 32:64], kr[:, :, 0:32]], axis=2)
    w_in_ext = np.ascontiguousarray(np.concatenate([w_in, kr_sw], axis=2))
    uq = w_uq.reshape(depth, Q_LORA, A_HEADS, 192)
    nope = uq[..., :128].reshape(depth, Q_LORA, 1024)
    rope = uq[..., 128:]
    rope_sw = np.concatenate([rope[..., 32:], rope[..., :32]], axis=-1)
    w_uq_p = np.ascontiguousarray(np.concatenate([nope, rope.reshape(depth, Q_LORA, 512),
                                                  rope_sw.reshape(depth, Q_LORA, 512)], axis=2))
    ukv = w_ukv.reshape(depth, KV_LORA, A_HEADS, 256)
    w_ukv_p = np.ascontiguousarray(np.concatenate([ukv[..., :128].reshape(depth, KV_LORA, 1024),
                                                   ukv[..., 128:].reshape(depth, KV_LORA, 1024)], axis=2))
    out = {
        "w_in_ext": w_in_ext, "w_uq_p": w_uq_p, "w_ukv_p": w_ukv_p,
        "norm_in_t": np.ascontiguousarray(norm_in.reshape(depth, 16, 128).transpose(0, 2, 1)),
        "q_a_norm_t": np.ascontiguousarray(q_a_norm.reshape(depth, 4, 128).transpose(0, 2, 1)),
        "kv_a_norm_t": np.ascontiguousarray(kv_a_norm.reshape(depth, 4, 128).transpose(0, 2, 1)),
        "b_gates_b": np.ascontiguousarray(np.broadcast_to(b_gates[:, None, :], (depth, 128, 32))),
        "m_head_norm_b": np.ascontiguousarray(np.broadcast_to(m_head_norm[:, None, :], (depth, 128, 1024))),
        "norm_f_b": np.ascontiguousarray(np.broadcast_to(norm_f[None, :], (128, D_MODEL))),
    }
    return out


_CACHE = {}


def make_keep(S, reset_fw_chunk, reset_bw_chunk):
    nck = S // CHUNK
    keep = np.ones((nck, 16), np.float32)
    keep[(reset_fw_chunk - 1) % nck, 0:8] = 0.0
    keep[(reset_bw_chunk + 1) % nck, 8:16] = 0.0
    return np.ascontiguousarray(np.broadcast_to(keep[None], (128, nck, 16)))


def make_rope(positions):
    inv = (np.float32(ROPE_THETA) ** (-np.arange(0, QK_ROPE, 2, dtype=np.float32) / np.float32(QK_ROPE))).astype(np.float32)
    ang = (positions.astype(np.float32)[:, None] * inv[None, :]).astype(np.float32)
    cos = np.cos(ang).astype(np.float32).T
    sin = np.sin(ang).astype(np.float32).T
    cos64 = np.concatenate([cos, cos], axis=0)
    sin64 = np.concatenate([-sin, sin], axis=0)
    rope = np.stack([np.concatenate([cos64, cos64], 0), np.concatenate([sin64, sin64], 0)], axis=1)
    return np.ascontiguousarray(rope.astype(np.float32))


def core_inputs(c, x_prompt, x_sample, nq=4):
    B, S, _ = x_prompt.shape
    DB, DS, _ = x_sample.shape
    p, r = c % B, (c // B) % nq
    Q = S // nq
    m = {}
    m["x_s"] = x_sample[c % DB]
    m["x_p"] = np.ascontiguousarray(np.roll(x_prompt[p], -r * Q, axis=0))
    m["rope_s"] = make_rope(np.arange(DS))
    m["rope_p"] = make_rope((np.arange(S) + r * Q) % S)
    m["keep_s"] = make_keep(DS, 0, DS // CHUNK - 1)
    j0 = (nq - r) % nq
    je = (nq - 1 - r) % nq
    m["keep_p"] = make_keep(S, j0 * (Q // CHUNK), (je + 1) * (Q // CHUNK) - 1)
    return m


def kernel(x_prompt, x_sample, norm_in, w_in, b_gates, q_a_norm, w_uq, kv_a_norm, w_ukv,
           w_oa, m_head_norm, w_ob, w_out, norm_f):
    f = lambda a: np.ascontiguousarray(np.asarray(a, dtype=np.float32))
    x_prompt, x_sample = f(x_prompt), f(x_sample)
    B, S, _ = x_prompt.shape
    DB, DS, _ = x_sample.shape
    n = 8
    nq = n // B
    Q = S // nq
    key = (S, DS)
    if key not in _CACHE:
        _CACHE[key] = build([("s", DS), ("p", S)], prune={"p": Q})
    nc, _ = _CACHE[key]
    common = prep_weights(f(w_in), f(w_uq), f(w_ukv), f(norm_in), f(q_a_norm), f(kv_a_norm), f(b_gates),
                          f(m_head_norm), f(norm_f))
    common["w_oa"] = f(w_oa)
    common["w_ob"] = f(w_ob)
    common["w_out"] = f(w_out)
    common["consts"] = make_consts(128)[0]
    in_maps = []
    for c in range(n):
        m = dict(common)
        m.update(core_inputs(c, x_prompt, x_sample, nq))
        in_maps.append(m)
    res = run_bass_kernel_spmd(nc, in_maps, core_ids=list(range(n)))
    y_prompt = np.empty((B, S, D_MODEL), np.float32)
    for c in range(n):
        p, r = c % B, (c // B) % nq
        y_prompt[p, r * Q:(r + 1) * Q] = res.results[c]["y_p"]
    y_sample = np.stack([res.results[c]["y_s"] for c in range(DB)], axis=0)
    return (y_prompt, y_sample.astype(np.float32))
```
